# Optimizing a Trainium2 kernel written in Bass

```python
import jax, jax.numpy as jnp
from jax import lax
import numpy as np

D_MODEL = 1024
BATCH = 4
SEQ = 4096
DEPTH = 4

HEAD_DIM = 128
MOBA_HEADS = 4
RET_HEADS = 4
MOBA_BLOCK = 256
MOBA_TOPK = 3
MOBA_QCHUNK = 32
RET_CHUNK = 128
CONV_WIDTH = 3
PEER_HEADS = 8
PEER_NKEYS = 128
PEER_NEXPERTS = PEER_NKEYS * PEER_NKEYS
PEER_TOPK = 16
PEER_DKEY = 256
PEER_TOKCHUNK = 128
ROPE_THETA = 10000.0
EPS = 1e-6
N_EVEN = (DEPTH + 1) // 2
N_ODD = DEPTH // 2
MOBA_W = MOBA_HEADS * HEAD_DIM
RET_W = RET_HEADS * HEAD_DIM
EVEN_SPLITS = (MOBA_W, MOBA_W, MOBA_W, RET_W, RET_W, RET_W, RET_W)
EVEN_IN = sum(EVEN_SPLITS)
EVEN_OUT_IN = MOBA_W + RET_W

kernel_name = "moba_retention_shortconv_peer_hybrid"


def rmsnorm(x, g):
    xf = x.astype(jnp.float32)
    y = xf * lax.rsqrt(jnp.mean(xf * xf, axis=-1, keepdims=True) + EPS)
    return (y * g.astype(jnp.float32)).astype(x.dtype)


def rope(t):
    s, dh = t.shape[2], t.shape[3]
    half = dh // 2
    inv = ROPE_THETA ** (-jnp.arange(half, dtype=jnp.float32) / half)
    ang = jnp.arange(s, dtype=jnp.float32)[:, None] * inv[None, :]
    cos = jnp.cos(ang).astype(t.dtype)
    sin = jnp.sin(ang).astype(t.dtype)
    t1, t2 = t[..., :half], t[..., half:]
    return jnp.concatenate([t1 * cos - t2 * sin, t2 * cos + t1 * sin], axis=-1)


def split_heads(t, n):
    b, s, _ = t.shape
    return t.reshape(b, s, n, HEAD_DIM).transpose(0, 2, 1, 3)


def merge_heads(t):
    b, h, s, dh = t.shape
    return t.transpose(0, 2, 1, 3).reshape(b, s, h * dh)


def moba_attention(q, k, v):
    b, h, s, dh = q.shape
    nb = -(-s // MOBA_BLOCK)
    pad = nb * MOBA_BLOCK - s
    kb = jnp.pad(k, ((0, 0), (0, 0), (0, pad), (0, 0))).reshape(b, h, nb, MOBA_BLOCK, dh)
    vb = jnp.pad(v, ((0, 0), (0, 0), (0, pad), (0, 0))).reshape(b, h, nb, MOBA_BLOCK, dh)
    counts = jnp.minimum(MOBA_BLOCK, s - jnp.arange(nb) * MOBA_BLOCK).astype(jnp.float32)
    kmean = kb.astype(jnp.float32).sum(axis=3) / counts[None, None, :, None]
    gate = jnp.einsum('bhsd,bhnd->bhsn', q.astype(jnp.float32), kmean)
    qblk = jnp.arange(s) // MOBA_BLOCK
    past = jnp.arange(nb)[None, :] < qblk[:, None]
    gate = jnp.where(past, gate, -jnp.inf)
    kk = min(MOBA_TOPK, nb)
    _, sel = lax.top_k(gate, kk)
    valid = sel < qblk[:, None]
    nq = s // MOBA_QCHUNK
    def chunked(t):
        return t.reshape(b, h, nq, MOBA_QCHUNK, *t.shape[3:]).transpose(2, 0, 1, 3, *range(4, t.ndim + 1))
    qs, sels, valids = chunked(q), chunked(sel), chunked(valid)
    scale = dh ** -0.5
    bi = jnp.arange(b)[:, None, None, None]
    hi = jnp.arange(h)[None, :, None, None]

    def step(args):
        ci, qc, selc, validc = args
        tpos = ci * MOBA_QCHUNK + jnp.arange(MOBA_QCHUNK)
        ob = (ci * MOBA_QCHUNK) // MOBA_BLOCK
        k_own = lax.dynamic_index_in_dim(kb, ob, axis=2, keepdims=False)
        v_own = lax.dynamic_index_in_dim(vb, ob, axis=2, keepdims=False)
        kpos = ob * MOBA_BLOCK + jnp.arange(MOBA_BLOCK)
        s_own = jnp.einsum('bhqd,bhkd->bhqk', qc, k_own).astype(jnp.float32) * scale
        s_own = jnp.where(kpos[None, :] <= tpos[:, None], s_own, -jnp.inf)
        kg = kb[bi, hi, selc]
        vg = vb[bi, hi, selc]
        s_sel = jnp.einsum('bhqd,bhqjkd->bhqjk', qc, kg).astype(jnp.float32) * scale
        s_sel = jnp.where(validc[..., None], s_sel, -jnp.inf)
        scores = jnp.concatenate([s_sel.reshape(b, h, MOBA_QCHUNK, kk * MOBA_BLOCK), s_own], axis=-1)
        p = jax.nn.softmax(scores, axis=-1).astype(v.dtype)
        p_sel = p[..., :kk * MOBA_BLOCK].reshape(b, h, MOBA_QCHUNK, kk, MOBA_BLOCK)
        p_own = p[..., kk * MOBA_BLOCK:]
        return (jnp.einsum('bhqjk,bhqjkd->bhqd', p_sel, vg)
                + jnp.einsum('bhqk,bhkd->bhqd', p_own, v_own))

    out = lax.map(step, (jnp.arange(nq), qs, sels, valids))
    return out.transpose(1, 2, 0, 3, 4).reshape(b, h, s, dh)


def retention(q, k, v):
    b, h, s, dh = q.shape
    c = RET_CHUNK
    nc = s // c
    log_g = jnp.log(1.0 - 2.0 ** (-5.0 - jnp.arange(h, dtype=jnp.float32)))
    idx = jnp.arange(c, dtype=jnp.float32)
    diff = idx[:, None] - idx[None, :]
    dmask = jnp.where(diff >= 0, jnp.exp(log_g[:, None, None] * jnp.maximum(diff, 0.0)), 0.0)
    xi = jnp.exp(log_g[:, None] * (idx + 1.0))[..., None]
    zeta = jnp.exp(log_g[:, None] * (c - 1.0 - idx))[..., None]
    gc = jnp.exp(log_g * c)[:, None, None]
    def chunked(t):
        return t.astype(jnp.float32).reshape(b, h, nc, c, dh).transpose(2, 0, 1, 3, 4)
    qs, ks, vs = chunked(q), chunked(k * (dh ** -0.5)), chunked(v)

    def step(state, inp):
        qc, kc, vc = inp
        inner = jnp.einsum('bhnd,bhmd->bhnm', qc, kc) * dmask
        o = jnp.einsum('bhnm,bhmd->bhnd', inner, vc) + jnp.einsum('bhnd,bhde->bhne', qc, state) * xi
        state = state * gc + jnp.einsum('bhmd,bhme->bhde', kc * zeta, vc)
        return state, o

    state0 = jnp.zeros((b, h, dh, dh), jnp.float32)
    _, outs = lax.scan(step, state0, (qs, ks, vs))
    return outs.transpose(1, 2, 0, 3, 4).reshape(b, h, s, dh)


def attn_retention_mixer(xn, w_in, w_out):
    proj = xn @ w_in
    cuts = list(np.cumsum(EVEN_SPLITS)[:-1])
    mq, mk, mv, rq, rk, rv, rg = jnp.split(proj, cuts, axis=-1)
    mq, mk, mv = split_heads(mq, MOBA_HEADS), split_heads(mk, MOBA_HEADS), split_heads(mv, MOBA_HEADS)
    rq, rk, rv = split_heads(rq, RET_HEADS), split_heads(rk, RET_HEADS), split_heads(rv, RET_HEADS)
    mo = merge_heads(moba_attention(rope(mq), rope(mk), mv))
    ro = retention(rope(rq), rope(rk), rv)
    ro = ro * lax.rsqrt(jnp.mean(ro * ro, axis=-1, keepdims=True) + EPS)
    ro = merge_heads(ro).astype(xn.dtype) * jax.nn.silu(rg)
    return jnp.concatenate([mo, ro], axis=-1) @ w_out


def short_conv_mixer(xn, w_in, conv_w, w_out):
    d = xn.shape[-1]
    bg, cg, hx = jnp.split(xn @ w_in, 3, axis=-1)
    u = cg * hx
    y = lax.conv_general_dilated(u, conv_w[:, None, :], window_strides=(1,),
                                 padding=[(CONV_WIDTH - 1, 0)],
                                 dimension_numbers=('NWC', 'WIO', 'NWC'),
                                 feature_group_count=d)
    return (bg * y) @ w_out


def peer(xn, w_q, sub_keys, u, v):
    b, s, d = xn.shape
    t = xn.reshape(b * s, d)
    n_tok = b * s
    q = (t @ w_q).reshape(n_tok, PEER_HEADS, 2, PEER_DKEY // 2)
    sc = jnp.einsum('thpd,hpnd->thpn', q, sub_keys).astype(jnp.float32)
    s1, i1 = lax.top_k(sc[:, :, 0], PEER_TOPK)
    s2, i2 = lax.top_k(sc[:, :, 1], PEER_TOPK)
    cand = (s1[..., :, None] + s2[..., None, :]).reshape(n_tok, PEER_HEADS, PEER_TOPK * PEER_TOPK)
    cidx = (i1[..., :, None] * PEER_NKEYS + i2[..., None, :]).reshape(n_tok, PEER_HEADS, PEER_TOPK * PEER_TOPK)
    top_s, pos = lax.top_k(cand, PEER_TOPK)
    eidx = jnp.take_along_axis(cidx, pos, axis=-1)
    gate = jax.nn.softmax(top_s, axis=-1).astype(xn.dtype)
    nt = n_tok // PEER_TOKCHUNK

    def step(args):
        tc, ec, gc = args
        hid = jax.nn.gelu(jnp.einsum('td,thkd->thk', tc, u[ec]), approximate=False)
        return jnp.einsum('thk,thkd->td', gc * hid, v[ec])

    out = lax.map(step, (t.reshape(nt, PEER_TOKCHUNK, d),
                         eidx.reshape(nt, PEER_TOKCHUNK, PEER_HEADS, PEER_TOPK),
                         gate.reshape(nt, PEER_TOKCHUNK, PEER_HEADS, PEER_TOPK)))
    return out.reshape(b, s, d)


def setup_inputs(seed: int = 0) -> dict:
    key = jax.random.key(seed)
    ks = jax.random.split(key, 16)
    f32 = jnp.float32
    nrm = lambda k, shape, sc: jax.random.normal(k, shape, f32) * sc
    return {
        "x": nrm(ks[0], (BATCH, SEQ, D_MODEL), 1.0),
        "norm_mix": 1.0 + nrm(ks[1], (DEPTH, D_MODEL), 0.02),
        "norm_ffn": 1.0 + nrm(ks[2], (DEPTH, D_MODEL), 0.02),
        "even_w_in": nrm(ks[3], (N_EVEN, D_MODEL, EVEN_IN), D_MODEL ** -0.5),
        "even_w_out": nrm(ks[4], (N_EVEN, EVEN_OUT_IN, D_MODEL), EVEN_OUT_IN ** -0.5),
        "odd_w_in": nrm(ks[5], (N_ODD, D_MODEL, 3 * D_MODEL), D_MODEL ** -0.5),
        "odd_conv": nrm(ks[6], (N_ODD, CONV_WIDTH, D_MODEL), CONV_WIDTH ** -0.5),
        "odd_w_out": nrm(ks[7], (N_ODD, D_MODEL, D_MODEL), D_MODEL ** -0.5),
        "peer_w_q": nrm(ks[8], (DEPTH, D_MODEL, PEER_HEADS * PEER_DKEY), D_MODEL ** -0.5),
        "peer_sub_keys": nrm(ks[9], (DEPTH, PEER_HEADS, 2, PEER_NKEYS, PEER_DKEY // 2), (PEER_DKEY // 2) ** -0.5),
        "peer_u": nrm(ks[10], (DEPTH, PEER_NEXPERTS, D_MODEL), D_MODEL ** -0.5),
        "peer_v": nrm(ks[11], (DEPTH, PEER_NEXPERTS, D_MODEL), 0.1),
        "final_norm": 1.0 + nrm(ks[12], (D_MODEL,), 0.02),
    }


def reference(x, norm_mix, norm_ffn, even_w_in, even_w_out, odd_w_in, odd_conv, odd_w_out,
              peer_w_q, peer_sub_keys, peer_u, peer_v, final_norm):
    h = x
    for layer in range(DEPTH):
        xn = rmsnorm(h, norm_mix[layer])
        i = layer // 2
        if layer % 2 == 0:
            h = h + attn_retention_mixer(xn, even_w_in[i], even_w_out[i])
        else:
            h = h + short_conv_mixer(xn, odd_w_in[i], odd_conv[i], odd_w_out[i])
        h = h + peer(rmsnorm(h, norm_ffn[layer]), peer_w_q[layer], peer_sub_keys[layer],
                     peer_u[layer], peer_v[layer])
    return rmsnorm(h, final_norm)
```

```python
from contextlib import ExitStack

import numpy as np
import concourse.bass as bass
import concourse.mybir as mybir

F32 = mybir.dt.float32
BF16 = mybir.dt.bfloat16
I32 = mybir.dt.int32
U32 = mybir.dt.uint32
AF = mybir.ActivationFunctionType
ALU = mybir.AluOpType
AX = mybir.AxisListType

ENGS = ("pe", "act", "dve", "pool", "sp")
STRICT = False


class Prog:
    def __init__(self, nc, stack, n_dma_sems=64, n_shared=12):
        self.n_shared = n_shared
        self.nc = nc
        self.stack = stack
        self.eng = {"pe": nc.tensor, "act": nc.scalar, "dve": nc.vector, "pool": nc.gpsimd, "sp": nc.sync}
        self.sem = {e: stack.enter_context(nc.semaphore("s_" + e)) for e in ENGS}
        self.cnt = {e: 0 for e in ENGS}
        self.lists = {e: [] for e in ENGS}
        self.waited = {e: {} for e in ENGS}
        self.last_w = {}
        self.readers = {}
        self.dma_sems = [stack.enter_context(nc.semaphore("s_dma%d" % i)) for i in range(n_dma_sems)]
        self.dma_cnt = [0] * n_dma_sems
        self.dma_rr = 0
        self.dma_rr_pool = 0
        self.uid = 0
        self.lazy = set()
        self._rec = None
        self.semname = {}
        for e in ENGS:
            self.semname[id(self.sem[e])] = e

    def sb(self, name, shape, dt, st=None):
        self.uid += 1
        return (st or self.stack).enter_context(self.nc.sbuf_tensor("sb_%s_%d" % (name, self.uid), list(shape), dt))

    def ps(self, name, shape, dt=F32, st=None):
        self.uid += 1
        return (st or self.stack).enter_context(self.nc.psum_tensor("ps_%s_%d" % (name, self.uid), list(shape), dt))

    def cc(self, fn, semi):
        waits = self._deps("pool", [], [])
        semi = self.n_shared + semi
        self.dma_cnt[semi] += 1
        self.lists["pool"].append((waits, fn, self.dma_sems[semi], None))

    def _deps(self, e, reads, writes):
        toks = []
        assert all(not r.startswith("ps:") for r in reads)
        for r in reads:
            t = self.last_w.get(r)
            if t is not None:
                toks.append((t, "raw"))
        for w in writes:
            t = self.last_w.get(w)
            if t is not None:
                toks.append((t, "waw"))
            for t in self.readers.get(w, ()):
                toks.append((t, "war"))
        waits = {}
        for (semkey, sem, val), kind in toks:
            if isinstance(semkey, int) and semkey < self.n_shared:
                val = max(val, self.dma_cnt[semkey])
            if semkey == e and kind != "raw" and not STRICT:
                continue
            if self.waited[e].get(semkey, 0) >= val:
                continue
            if waits.get(semkey, (None, 0))[1] < val:
                waits[semkey] = (sem, val)
        for semkey, (sem, val) in waits.items():
            self.waited[e][semkey] = val
        return list(waits.values())

    def _commit(self, tok, reads, writes):
        for w in writes:
            self.last_w[w] = tok
            self.readers[w] = []
        for r in reads:
            self.readers.setdefault(r, []).append(tok)

    def start_record(self):
        self._rec = []

    def stop_record(self):
        r, self._rec = self._rec, None
        return r

    def gap(self):
        if self._rec is not None:
            self._rec.append(("gap",))

    def op(self, e, fn, reads=(), writes=()):
        if self._rec is not None:
            self._rec.append(("op", e, fn, list(reads), list(writes)))
            return None
        writes = list(writes) + [r for r in reads if r.startswith("ps:")]
        reads = [r for r in reads if not r.startswith("ps:")]
        waits = self._deps(e, reads, writes)
        self.cnt[e] += 1
        tok = (e, self.sem[e], self.cnt[e])
        self.lists[e].append((waits, fn, self.sem[e], 1))
        self._commit(tok, reads, writes)
        return tok

    def dma(self, e, fn, reads=(), writes=(), semi=None):
        waits = self._deps(e, reads, writes)
        if semi is None:
            if e == "pool":
                semi = 8 + self.dma_rr_pool
                self.dma_rr_pool = (self.dma_rr_pool + 1) % (self.n_shared - 8)
            else:
                semi = self.dma_rr
                self.dma_rr = (self.dma_rr + 1) % 8
        else:
            semi = self.n_shared + semi
        self.dma_cnt[semi] += 16
        tok = (semi, self.dma_sems[semi], self.dma_cnt[semi])
        self.lists[e].append((waits, fn, self.dma_sems[semi], 16))
        self._commit(tok, reads, writes)
        return tok

    def join(self, engs=ENGS):
        for e in engs:
            waits = []
            for ee in ENGS:
                if ee != e and self.cnt[ee] > 0 and self.waited[e].get(ee, 0) < self.cnt[ee]:
                    waits.append((self.sem[ee], self.cnt[ee]))
                    self.waited[e][ee] = self.cnt[ee]
            for i, s in enumerate(self.dma_sems):
                if i in self.lazy:
                    continue
                if self.dma_cnt[i] > 0 and self.waited[e].get(i, 0) < self.dma_cnt[i]:
                    waits.append((s, self.dma_cnt[i]))
                    self.waited[e][i] = self.dma_cnt[i]
            if waits:
                self.lists[e].append((waits, None, None, 0))

    def wait_all(self, e):
        waits = []
        for ee in ENGS:
            if self.cnt[ee] > 0 and self.waited[e].get(ee, 0) < self.cnt[ee]:
                waits.append((self.sem[ee], self.cnt[ee]))
        for i, s in enumerate(self.dma_sems):
            if self.dma_cnt[i] > 0:
                waits.append((s, self.dma_cnt[i]))
        self.lists[e].append((waits, None, None, 0))

    def emit(self):
        nc = self.nc
        with nc.Block() as block:
            def run(e):
                def body(engine):
                    for waits, fn, sem, inc in self.lists[e]:
                        for s, v in waits:
                            engine.wait_ge(s, v)
                        if fn is not None:
                            ins = fn(engine)
                            if inc is None:
                                ins.then_inc(sem)
                            else:
                                ins.then_inc(sem, inc)
                return body
            block.sync(run("sp"))
            block.scalar(run("act"))
            block.vector(run("dve"))
            block.gpsimd(run("pool"))
            block.tensor(run("pe"))


EPS = 1e-6
DBG_STAGE = 99
DBG_SUB = 99
NT = 16
TOK = NT * 128


def cust(ap, dims):
    return bass.AP(ap.tensor, ap.offset, [list(ap.ap[0])] + [list(d) for d in dims])


def bcast_rows(dram_ap, n):
    return bass.AP(dram_ap.tensor, dram_ap.offset, [[0, 128], [1, n]])


def emit_rstd(P, st, pref, h, hkey, ntiles, junk):
    ss = P.sb(pref + "ss", [128, ntiles], F32, st)
    sq = P.sb(pref + "sq", [128, ntiles], F32, st)
    rstd = P.sb(pref + "rstd", [128, ntiles], F32, st)
    for i in range(ntiles):
        P.op("act", lambda e, i=i: e.activation(out=junk[:], in_=h[:, i, :], func=AF.Square, accum_out=ss[:, i:i + 1]),
             reads=[hkey(i)], writes=[pref + "junk", pref + "ss%d" % i])
    P.op("act", lambda e: e.activation(out=sq[:], in_=ss[:], func=AF.Sqrt, scale=1.0 / 1024, bias=EPS),
         reads=[pref + "ss%d" % i for i in range(ntiles)], writes=[pref + "sq"])
    P.op("dve", lambda e: e.reciprocal(out=rstd[:], in_=sq[:]), reads=[pref + "sq"], writes=[pref + "rstd"])
    return rstd


def emit_peer(P, h, hkey, D, C):
    idf, idb, iota16 = C["idf"], C["idb"], C["iota16"]
    with ExitStack() as stC:
        xn = P.sb("pe_xn", [128, NT, 1024], BF16, stC)
        junk = P.sb("pe_junk", [128, 1024], BF16, stC)
        idxT = P.sb("pe_idxT", [128, 256], I32, stC)
        gateT = P.sb("pe_gateT", [128, 256], F32, stC)
        wqb = P.sb("pe_wqb", [128, 8, 2048], BF16, stC)
        KT = P.sb("pe_KT", [128, 16, 128], BF16, stC)
        pA0 = P.ps("pe_pA0", [128, 512], F32, stC)
        pA1 = P.ps("pe_pA1", [128, 512], F32, stC)
        pT = pA0[:].bitcast(BF16).rearrange("p (c t) -> p c t", c=8)
        for c in range(8):
            P.dma("pool", lambda e, c=c: e.dma_start(out=wqb[:, c, :], in_=D["wq"][c * 128:(c + 1) * 128, :]), writes=["pe_wqb"])
        scs = P.sb("pe_scs", [128, 2048], F32, stC)
        skf = scs[:].rearrange("p (a d) -> p a d", a=16)
        P.dma("sp", lambda e: e.dma_start(out=skf, in_=D["sk"].rearrange("a n d -> n a d")), writes=["pe_scs%d" % q for q in range(4)])
        for a in range(16):
            P.op("pe", lambda e, a=a: e.transpose(out=pA1[:, (a % 4) * 128:(a % 4 + 1) * 128], in_=skf[:, a, :], identity=idf[:]),
                 reads=["pe_scs%d" % q for q in range(4)] + ["idf"], writes=["ps:A1"])
            if a % 4 == 3:
                g4 = a // 4
                P.op("act", lambda e, g4=g4: e.activation(out=KT[:, g4 * 4:(g4 + 1) * 4, :], in_=pA1[:], func=AF.Copy),
                     reads=["ps:A1"], writes=["pe_KT"])
        with ExitStack() as stN:
            gB = P.sb("pe_gB", [128, 1024], F32, stN)
            P.dma("sp", lambda e: e.dma_start(out=gB[:], in_=bcast_rows(D["g"], 1024)), writes=["pe_gB"])
            rstd = emit_rstd(P, stN, "pe_", h, hkey, NT, junk)
            for i in range(NT):
                P.op("dve", lambda e, i=i: e.scalar_tensor_tensor(out=xn[:, i, :], in0=h[:, i, :], scalar=rstd[:, i:i + 1], in1=gB[:],
                                                                 op0=ALU.mult, op1=ALU.mult),
                     reads=[hkey(i), "pe_rstd", "pe_gB"], writes=["pe_xn%d" % i])
            P.join()
        xnT = P.sb("pe_xnT", [128, 8, 128], BF16, stC)
        qT = P.sb("pe_qT", [128, 16, 128], BF16, stC)
        scr = P.sb("pe_scr", [128, 2048], F32, stC)
        eq = scr[:].bitcast(BF16)[:, 0:2048]
        prod = scr[:].bitcast(BF16)[:, 2048:4096]
        m = P.sb("pe_m", [128, 16, 16], F32, stC)
        ix = P.sb("pe_ix", [128, 16, 16], U32, stC)
        ixf = P.sb("pe_ixf", [128, 16, 16], F32, stC)
        cand = scs[:].rearrange("p (h c) -> p h c", h=8)
        ts = P.sb("pe_ts", [128, 8, 16], F32, stC)
        pos = P.sb("pe_pos", [128, 8, 16], U32, stC)
        au = P.sb("pe_au", [128, 128], U32, stC)
        bu = P.sb("pe_bu", [128, 128], U32, stC)
        af = P.sb("pe_af", [128, 128], F32, stC)
        bf = P.sb("pe_bf", [128, 128], F32, stC)
        e1 = P.sb("pe_e1", [128, 128], F32, stC)
        e2 = P.sb("pe_e2", [128, 128], F32, stC)
        eidx = P.sb("pe_eidx", [128, 128], F32, stC)
        negm = P.sb("pe_negm", [128, 8], F32, stC)
        ex = P.sb("pe_ex", [128, 8, 16], F32, stC)
        Z = P.sb("pe_Z", [128, 8], F32, stC)
        rZ = P.sb("pe_rZ", [128, 8], F32, stC)
        gate = P.sb("pe_gate", [128, 128], F32, stC)
        SCRK = ["pe_scr%d" % k for k in range(16)]
        SCSK = ["pe_scs%d" % q for q in range(4)]

        def phase_a(i):
            sl = (i % 2) * 128
            banks = [(pA1, "ps:A1"), (pA0, "ps:A0")]
            for c in range(8):
                P.op("pe", lambda e, c=c: e.transpose(out=pT[:, c, :], in_=xn[:, i, c * 128:(c + 1) * 128], identity=idb[:]),
                     reads=["pe_xn%d" % i, "idb"], writes=["ps:A0"])
            P.gap()
            P.op("act", lambda e: e.activation(out=xnT[:], in_=pT, func=AF.Copy), reads=["ps:A0"], writes=["pe_xnT"])
            P.gap()

            def qproj(g):
                bk, bn = banks[g % 2]
                for k in range(4):
                    hp = 4 * g + k
                    for c in range(8):
                        P.op("pe", lambda e, hp=hp, c=c, k=k: e.matmul(bk[:, k * 128:(k + 1) * 128], lhsT=wqb[:, c, hp * 128:(hp + 1) * 128], rhs=xnT[:, c, :],
                                                                       start=(c == 0), stop=(c == 7)),
                             reads=["pe_wqb", "pe_xnT"], writes=[bn])

            def qcopy(g):
                bk, bn = banks[g % 2]
                P.op("act", lambda e: e.activation(out=qT[:, 4 * g:4 * g + 4, :], in_=bk[:], func=AF.Copy), reads=[bn], writes=["pe_qT%d" % g])

            def score(q4):
                bk, bn = banks[q4 % 2]
                for k in range(4):
                    hp = q4 * 4 + k
                    P.op("pe", lambda e, k=k, hp=hp: e.matmul(bk[:, k * 128:(k + 1) * 128], lhsT=qT[:, hp, :], rhs=KT[:, hp, :], start=True, stop=True),
                         reads=["pe_qT%d" % q4, "pe_KT"], writes=[bn])

            def scopy(q4):
                bk, bn = banks[q4 % 2]
                P.op("act", lambda e: e.activation(out=scs[:, q4 * 512:(q4 + 1) * 512], in_=bk[:], func=AF.Copy), reads=[bn], writes=["pe_scs%d" % q4])

            qproj(0); qproj(1); P.gap(); qcopy(0); qcopy(1); P.gap()
            qproj(2); qproj(3); P.gap(); qcopy(2); qcopy(3); P.gap()
            score(0); score(1); P.gap(); scopy(0); scopy(1); P.gap()
            score(2); score(3); P.gap(); scopy(2); scopy(3); P.gap()
            svs = [scs[:, hp * 128:(hp + 1) * 128] for hp in range(16)]
            sks = ["pe_scs%d" % (hp // 4) for hp in range(16)]
            for hp in range(16):
                P.op("dve", lambda e, hp=hp: e.max(out=m[:, hp, 0:8], in_=svs[hp]), reads=[sks[hp]], writes=["pe_m%da" % hp])
            for hp in range(16):
                P.op("dve", lambda e, hp=hp: e.max_index(out=ix[:, hp, 0:8], in_max=m[:, hp, 0:8], in_values=svs[hp]),
                     reads=[sks[hp], "pe_m%da" % hp], writes=["pe_ix%da" % hp])
            for hp in range(16):
                P.op("dve", lambda e, hp=hp: e.match_replace(out=scr[:, hp * 128:(hp + 1) * 128], in_to_replace=m[:, hp, 0:8], in_values=svs[hp], imm_value=-1e30),
                     reads=[sks[hp], "pe_m%da" % hp], writes=[SCRK[hp]])
            for hp in range(16):
                P.op("dve", lambda e, hp=hp: e.max(out=m[:, hp, 8:16], in_=scr[:, hp * 128:(hp + 1) * 128]), reads=[SCRK[hp]], writes=["pe_m%db" % hp])
            for hp in range(16):
                P.op("dve", lambda e, hp=hp: e.max_index(out=ix[:, hp, 8:16], in_max=m[:, hp, 8:16], in_values=svs[hp]),
                     reads=[sks[hp], "pe_m%db" % hp], writes=["pe_ix%db" % hp])
            mkeys = ["pe_m%d%s" % (hp, x) for hp in range(16) for x in "ab"]
            ixkeys = ["pe_ix%d%s" % (hp, x) for hp in range(16) for x in "ab"]
            P.op("dve", lambda e: e.tensor_tensor(
                out=cand[:].rearrange("p h (a b) -> p h a b", a=16),
                in0=cust(m[:, 0, 0:1], [[32, 8], [1, 16], [0, 16]]),
                in1=cust(m[:, 1, 0:1], [[32, 8], [0, 16], [1, 16]]), op=ALU.add),
                reads=mkeys + ixkeys, writes=["pe_cand"] + SCSK)
            P.op("dve", lambda e: e.tensor_copy(out=ixf[:], in_=ix[:]), reads=ixkeys, writes=["pe_ixf"])
            cvs = [cand[:, hh, :] for hh in range(8)]
            for hh in range(8):
                P.op("dve", lambda e, hh=hh: e.max(out=ts[:, hh, 0:8], in_=cvs[hh]), reads=["pe_cand"] + SCSK, writes=["pe_ts%da" % hh])
            for hh in range(8):
                P.op("dve", lambda e, hh=hh: e.max_index(out=pos[:, hh, 0:8], in_max=ts[:, hh, 0:8], in_values=cvs[hh]),
                     reads=["pe_cand", "pe_ts%da" % hh] + SCSK, writes=["pe_pos%da" % hh])
            for hh in range(8):
                P.op("dve", lambda e, hh=hh: e.match_replace(out=scr[:, hh * 256:(hh + 1) * 256], in_to_replace=ts[:, hh, 0:8], in_values=cvs[hh], imm_value=-1e30),
                     reads=["pe_cand", "pe_ts%da" % hh] + SCSK, writes=[SCRK[2 * hh], SCRK[2 * hh + 1]])
            for hh in range(8):
                P.op("dve", lambda e, hh=hh: e.max(out=ts[:, hh, 8:16], in_=scr[:, hh * 256:(hh + 1) * 256]),
                     reads=[SCRK[2 * hh], SCRK[2 * hh + 1]], writes=["pe_ts%db" % hh])
            for hh in range(8):
                P.op("dve", lambda e, hh=hh: e.max_index(out=pos[:, hh, 8:16], in_max=ts[:, hh, 8:16], in_values=cvs[hh]),
                     reads=["pe_cand", "pe_ts%db" % hh] + SCSK, writes=["pe_pos%db" % hh])
            tskeys = ["pe_ts%d%s" % (hh, x) for hh in range(8) for x in "ab"]
            poskeys = ["pe_pos%d%s" % (hh, x) for hh in range(8) for x in "ab"]
            posf = pos[:].rearrange("p h k -> p (h k)")
            P.op("dve", lambda e: e.tensor_single_scalar(out=au[:], in_=posf, scalar=4, op=ALU.logical_shift_right), reads=poskeys, writes=["pe_au"])
            P.op("dve", lambda e: e.tensor_single_scalar(out=bu[:], in_=posf, scalar=15, op=ALU.bitwise_and), reads=poskeys, writes=["pe_bu"])
            P.op("dve", lambda e: e.tensor_copy(out=af[:], in_=au[:]), reads=["pe_au"], writes=["pe_af"])
            P.op("dve", lambda e: e.tensor_copy(out=bf[:], in_=bu[:]), reads=["pe_bu"], writes=["pe_bf"])
            for (sel, side, eo, en) in ((af, 0, e1, "pe_e1"), (bf, 1, e2, "pe_e2")):
                sn = "pe_af" if side == 0 else "pe_bf"
                P.op("dve", lambda e, sel=sel: e.tensor_tensor(
                    out=eq.rearrange("p (s a) -> p s a", a=16),
                    in0=cust(sel[:, 0:1], [[1, 128], [0, 16]]),
                    in1=cust(iota16[:, 0:1], [[0, 128], [1, 16]]), op=ALU.is_equal),
                    reads=[sn, "iota16"] + SCRK, writes=["pe_eq"])
                P.op("dve", lambda e, side=side: e.tensor_tensor(
                    out=prod.rearrange("p (h k a) -> p h k a", h=8, k=16),
                    in0=eq.rearrange("p (h k a) -> p h k a", h=8, k=16),
                    in1=cust(ixf[:, side, 0:1], [[32, 8], [0, 16], [1, 16]]), op=ALU.mult),
                    reads=["pe_eq", "pe_ixf"], writes=["pe_prod"])
                P.op("dve", lambda e, eo=eo: e.tensor_reduce(out=eo[:], in_=prod.rearrange("p (s a) -> p s a", a=16), axis=AX.X, op=ALU.add),
                     reads=["pe_prod"], writes=[en])
            P.op("dve", lambda e: e.scalar_tensor_tensor(out=eidx[:], in0=e1[:], scalar=C["c128"][:, 0:1], in1=e2[:], op0=ALU.mult, op1=ALU.add),
                 reads=["pe_e1", "pe_e2", "c128"], writes=["pe_eidx"] + SCRK)
            P.op("dve", lambda e: e.tensor_scalar(out=negm[:], in0=cust(ts[:, 0, 0:1], [[16, 8]]), scalar1=-1.0, scalar2=None, op0=ALU.mult),
                 reads=tskeys, writes=["pe_negm"])
            P.gap()
            for hh in range(8):
                P.op("act", lambda e, hh=hh: e.activation(out=ex[:, hh, :], in_=ts[:, hh, :], func=AF.Exp, bias=negm[:, hh:hh + 1],
                                                          accum_out=Z[:, hh:hh + 1]),
                     reads=tskeys + ["pe_negm"], writes=["pe_ex%d" % hh, "pe_Z%d" % hh])
            P.gap()
            P.op("dve", lambda e: e.reciprocal(out=rZ[:], in_=Z[:]), reads=["pe_Z%d" % hh for hh in range(8)], writes=["pe_rZ"])
            P.op("dve", lambda e: e.tensor_tensor(out=gate[:].rearrange("p (h k) -> p h k", h=8), in0=ex[:],
                                                  in1=cust(rZ[:, 0:1], [[1, 8], [0, 16]]), op=ALU.mult),
                 reads=["pe_ex%d" % hh for hh in range(8)] + ["pe_rZ"], writes=["pe_gate"])
            P.gap()
            P.op("pe", lambda e: e.transpose(out=pA0[:, 0:128], in_=eidx[:], identity=idf[:]), reads=["pe_eidx", "idf"], writes=["ps:A0"])
            P.op("pe", lambda e: e.transpose(out=pA0[:, 128:256], in_=gate[:], identity=idf[:]), reads=["pe_gate", "idf"], writes=["ps:A0"])
            P.gap()
            P.op("dve", lambda e: e.tensor_copy(out=idxT[:, sl:sl + 128], in_=pA0[:, 0:128]), reads=["ps:A0"], writes=["pe_idxT%d" % (i % 2)])
            P.op("act", lambda e: e.activation(out=gateT[:, sl:sl + 128], in_=pA0[:, 128:256], func=AF.Copy),
                 reads=["ps:A0"], writes=["pe_gateT%d" % (i % 2)])

        NR = 8
        UV = [P.sb("pe_UV%d" % r, [128, 2048], BF16, stC) for r in range(NR)]
        NZ = 4
        Zb = [P.sb("pe_Zb%d" % r, [128, 256], BF16, stC) for r in range(NZ)]
        hidc = P.sb("pe_hidc", [128, 8], F32, stC)
        gelc = P.sb("pe_gelc", [128, 8], F32, stC)
        pxb = [P.ps("pe_pxb%d" % r, [128, 1024], F32, stC) for r in range(2)]
        pout = P.ps("pe_pout", [128, 1024], F32, stC)
        for r in range(NZ):
            P.op("pool", lambda e, r=r: e.memset(Zb[r][:], 0.0), writes=["pe_Zb%d" % r])

        def u_stage(i, t):
            tt = i * 128 + t
            col = (i % 2) * 128 + t
            r, k8 = tt % NR, tt % 8
            P.dma("pool", lambda e: e.indirect_dma_start(out=UV[r][:], out_offset=None, in_=D["uv"],
                                                         in_offset=bass.IndirectOffsetOnAxis(ap=idxT[:, col:col + 1], axis=0)),
                  reads=["pe_idxT%d" % (i % 2), D["uvkey"]], writes=["pe_UV%d" % r], semi=r)
            pb = pxb[tt % 2]
            lhs = cust(idb[:, t:t + 1], [[0, 128]])
            for hf in range(2):
                P.op("pe", lambda e, hf=hf: e.matmul(pb[:, hf * 512:(hf + 1) * 512], lhsT=lhs, rhs=xn[:, i, hf * 512:(hf + 1) * 512], start=True, stop=True),
                     reads=["idb", "pe_xn%d" % i], writes=["ps:pxb%d_%d" % (tt % 2, hf)])
            P.op("dve", lambda e: e.scalar_tensor_tensor(out=junk[:], in0=UV[r][:, 0:1024], scalar=C["c1"][:, 0:1], in1=pb[:], op0=ALU.mult, op1=ALU.mult,
                                                         accum_out=hidc[:, k8:k8 + 1]),
                 reads=["pe_UV%d" % r, "ps:pxb%d_0" % (tt % 2), "ps:pxb%d_1" % (tt % 2), "c1"], writes=["pe_junk3", "pe_hidc%d" % k8])
            z = tt % NZ
            P.op("act", lambda e: e.activation(out=gelc[:, k8:k8 + 1], in_=hidc[:, k8:k8 + 1], func=AF.Gelu), reads=["pe_hidc%d" % k8], writes=["pe_gelc%d" % k8])
            P.op("act", lambda e: e.activation(out=Zb[z][:, 127:128], in_=gelc[:, k8:k8 + 1], func=AF.Copy, scale=gateT[:, col:col + 1]),
                 reads=["pe_gelc%d" % k8, "pe_gateT%d" % (i % 2)], writes=["pe_Zb%d" % z])

        def v_mm(i, t):
            tt = i * 128 + t
            r, z = tt % NR, tt % NZ
            for hf in range(2):
                P.op("pe", lambda e, hf=hf: e.matmul(pout[:, hf * 512:(hf + 1) * 512], lhsT=Zb[z][:, 127 - t:255 - t],
                                                     rhs=UV[r][:, 1024 + hf * 512:1024 + (hf + 1) * 512], start=(t == 0), stop=(t == 127)),
                     reads=["pe_Zb%d" % z, "pe_UV%d" % r], writes=["ps:pout_%d" % hf])
            if t == 127:
                P.op("dve", lambda e: e.tensor_tensor(out=h[:, i, :], in0=h[:, i, :], in1=pout[:], op=ALU.add),
                     reads=[hkey(i), "ps:pout_0", "ps:pout_1"], writes=[hkey(i)])

        def record_a(i):
            P.start_record()
            phase_a(i)
            return P.stop_record()

        for it in record_a(0):
            if it[0] == "op":
                P.op(*it[1:])
        LAG = 2
        pend = []
        for i in range(NT):
            ops = record_a(i + 1) if i + 1 < NT else []
            skip = 0
            for t in range(128):
                u_stage(i, t)
                pend.append((i, t))
                if len(pend) > LAG:
                    v_mm(*pend.pop(0))
                if skip > 0:
                    skip -= 1
                    continue
                nd = no = 0
                while ops:
                    it = ops[0]
                    if it[0] == "gap":
                        ops.pop(0); skip = 2
                        break
                    if it[1] == "dve":
                        if nd >= 2:
                            break
                        nd += 1
                    else:
                        if no >= 16:
                            break
                        no += 1
                    ops.pop(0)
                    P.op(*it[1:])
            for it in ops:
                if it[0] == "op":
                    P.op(*it[1:])
        while pend:
            v_mm(*pend.pop(0))
        P.join()


def uv_convert_steps(P, u_dram, v_dram, uv_dram, uvkey, semi):
    steps = []
    for k in range(16):
        rs = slice(k * 1024, (k + 1) * 1024)
        steps.append(lambda rs=rs: P.dma("pool", lambda e: e.dma_start(out=uv_dram[rs, 0:1024], in_=u_dram[rs, :]), writes=[uvkey], semi=semi))
        steps.append(lambda rs=rs: P.dma("pool", lambda e: e.dma_start(out=uv_dram[rs, 1024:2048], in_=v_dram[rs, :]), writes=[uvkey], semi=semi))
    return steps


def emit_uv_convert(P, u_dram, v_dram, uv_dram, uvkey, semi):
    for f in uv_convert_steps(P, u_dram, v_dram, uv_dram, uvkey, semi):
        f()


def load_consts(P, nc, cd):
    idf = P.sb("idf", [128, 128], F32)
    idb = P.sb("idb", [128, 128], BF16)
    iota16 = P.sb("iota16", [128, 16], F32)
    P.dma("sp", lambda e: e.dma_start(out=idf[:], in_=cd["ident"]), writes=["idf"])
    P.dma("sp", lambda e: e.dma_start(out=iota16[:], in_=cd["iota16"]), writes=["iota16"])
    P.op("dve", lambda e: e.tensor_copy(out=idb[:], in_=idf[:]), reads=["idf"], writes=["idb"])
    c1 = P.sb("c1", [128, 1], F32)
    c128 = P.sb("c128", [128, 1], F32)
    P.op("pool", lambda e: e.memset(c1[:], 1.0), writes=["c1"])
    P.op("pool", lambda e: e.memset(c128[:], 128.0), writes=["c128"])
    return {"idf": idf, "idb": idb, "iota16": iota16, "c1": c1, "c128": c128}


def host_consts():
    return {"ident": np.eye(128, dtype=np.float32),
            "iota16": np.tile(np.arange(16, dtype=np.float32)[None, :], (128, 1))}


def emit_peer_block(P, C, D, fin=None):
    with ExitStack() as st:
        h = P.sb("h", [128, NT, 1024], F32, st)
        hkey = lambda i: "h%d" % i
        hv = D["h_in"].rearrange("(i p) d -> p i d", p=128)
        for i in range(NT):
            P.dma("sp", lambda e, i=i: e.dma_start(out=h[:, i, :], in_=hv[:, i, :]), writes=[hkey(i)])
        emit_peer(P, h, hkey, D, C)
        ho = D["h_out"].rearrange("(i p) d -> p i d", p=128)
        if fin is None:
            for i in range(NT):
                P.dma("sp", lambda e, i=i: e.dma_start(out=ho[:, i, :], in_=h[:, i, :]), reads=[hkey(i)])
        else:
            fnorm, y_out = fin
            gF = P.sb("fn_gF", [128, 1024], F32, st)
            jF = P.sb("fn_jF", [128, 1024], BF16, st)
            yt = [P.sb("fn_yt%d" % k, [128, 1024], F32, st) for k in range(2)]
            P.dma("sp", lambda e: e.dma_start(out=gF[:], in_=bcast_rows(fnorm, 1024)), writes=["fn_gF"])
            rstd2 = emit_rstd(P, st, "fn_", h, hkey, NT, jF)
            yo = y_out.rearrange("(i p) d -> p i d", p=128)
            for i in range(NT):
                k = i % 2
                P.op("dve", lambda e, i=i, k=k: e.scalar_tensor_tensor(out=yt[k][:], in0=h[:, i, :], scalar=rstd2[:, i:i + 1], in1=gF[:],
                                                                      op0=ALU.mult, op1=ALU.mult),
                     reads=[hkey(i), "fn_rstd", "fn_gF"], writes=["fn_yt%d" % k])
                P.dma("sp", lambda e, i=i, k=k: e.dma_start(out=yo[:, i, :], in_=yt[k][:]), reads=["fn_yt%d" % k])
        P.join()


def build_peer_nc(final=False):
    nc = bass.Bass("TRN2", target_bir_lowering=False)
    dr = lambda n, s, k="ExternalInput", dt=F32: nc.dram_tensor(n, list(s), dt, kind=k).ap()
    D = {"h_in": dr("h_in", [TOK, 1024]), "g": dr("g", [1024]), "wq": dr("wq", [1024, 2048]), "sk": dr("sk", [16, 128, 128]),
         "u": dr("u", [16384, 1024]), "v": dr("v", [16384, 1024])}
    cd = {"ident": dr("ident", [128, 128]), "iota16": dr("iota16", [128, 16])}
    D["h_out"] = dr("h_out", [TOK, 1024], "ExternalOutput")
    fin = (dr("fnorm", [1024]), dr("y_out", [TOK, 1024], "ExternalOutput")) if final else None
    D["uv"] = nc.dram_tensor("uv_i", [16384, 2048], BF16).ap()
    D["uvkey"] = "uvtab"
    with ExitStack() as st:
        P = Prog(nc, st)
        C = load_consts(P, nc, cd)
        emit_uv_convert(P, D["u"], D["v"], D["uv"], "uvtab", 50)
        emit_peer_block(P, C, D, fin)
        P.wait_all("sp")
        P.emit()
    return nc


def emit_xnT(P, st, pref, src, g_dram, xnT, col0, ntiles, C, pT):
    idb = C["idb"]
    gB = P.sb(pref + "gB", [128, 1024], F32, st)
    P.dma("sp", lambda e: e.dma_start(out=gB[:], in_=bcast_rows(g_dram, 1024)), writes=[pref + "gB"])
    NH = 4
    ht = [P.sb(pref + "ht%d" % k, [128, 1024], F32, st) for k in range(NH)]
    xt = [P.sb(pref + "xt%d" % k, [128, 1024], BF16, st) for k in range(2)]
    jk = P.sb(pref + "jk", [128, 1024], BF16, st)
    ss = P.sb(pref + "ss", [128, ntiles], F32, st)
    sq = P.sb(pref + "sq", [128, ntiles], F32, st)
    rs = P.sb(pref + "rs", [128, ntiles], F32, st)
    if callable(src):
        tile_ap = src
    else:
        sv = src.rearrange("(i p) d -> p i d", p=128)
        tile_ap = lambda i: sv[:, i, :]

    def load(i):
        P.dma("sp", lambda e: e.dma_start(out=ht[i % NH][:], in_=tile_ap(i)), writes=[pref + "ht%d" % (i % NH)])

    def stats(i):
        k = i % NH
        P.op("dve", lambda e: e.scalar_tensor_tensor(out=jk[:], in0=ht[k][:], scalar=C["c1"][:, 0:1], in1=ht[k][:], op0=ALU.mult, op1=ALU.mult,
                                                     accum_out=ss[:, i:i + 1]),
             reads=[pref + "ht%d" % k, "c1"], writes=[pref + "jk", pref + "ss%d" % i])
        P.op("act", lambda e: e.activation(out=sq[:, i:i + 1], in_=ss[:, i:i + 1], func=AF.Sqrt, scale=1.0 / 1024, bias=EPS),
             reads=[pref + "ss%d" % i], writes=[pref + "sq%d" % i])

    def evac(i):
        P.op("dve", lambda e: e.tensor_copy(out=xnT[:, :, col0 + i * 128:col0 + (i + 1) * 128], in_=pT[:]),
             reads=["ps:pT"], writes=[pref + "xnT"])

    for i in range(min(NH - 1, ntiles)):
        load(i)
    stats(0)
    for i in range(ntiles):
        k, kx = i % NH, i % 2
        if i + 1 < ntiles:
            stats(i + 1)
        P.op("dve", lambda e, i=i: e.reciprocal(out=rs[:, i:i + 1], in_=sq[:, i:i + 1]), reads=[pref + "sq%d" % i], writes=[pref + "rs%d" % i])
        P.op("dve", lambda e, i=i, k=k, kx=kx: e.scalar_tensor_tensor(out=xt[kx][:], in0=ht[k][:], scalar=rs[:, i:i + 1], in1=gB[:], op0=ALU.mult, op1=ALU.mult),
             reads=[pref + "ht%d" % k, pref + "rs%d" % i, pref + "gB"], writes=[pref + "xt%d" % kx])
        if i > 0:
            evac(i - 1)
        for c in range(8):
            P.op("pe", lambda e, c=c, kx=kx: e.transpose(out=pT[:, c, :], in_=xt[kx][:, c * 128:(c + 1) * 128], identity=idb[:]),
                 reads=[pref + "xt%d" % kx, "idb"], writes=["ps:pT"])
        if i + NH - 1 < ntiles:
            load(i + NH - 1)
    evac(ntiles - 1)


def emit_outproj(P, st, pref, catT, catkey, wout_dram, h_src, h_dst, po):
    wo = P.sb(pref + "wo", [128, 8, 1024], BF16, st)
    for c in range(8):
        P.dma("pool", lambda e, c=c: e.dma_start(out=wo[:, c, :], in_=wout_dram[c * 128:(c + 1) * 128, :]), writes=[pref + "wo"])
    NB = 4
    hb = [P.sb(pref + "hb%d" % k, [128, 1024], F32, st) for k in range(NB)]
    sv = h_src.rearrange("(i p) d -> p i d", p=128)
    dv = h_dst.rearrange("(i p) d -> p i d", p=128)

    def load(i):
        P.dma("sp", lambda e: e.dma_start(out=hb[i % NB][:], in_=sv[:, i, :]), writes=[pref + "hb%d" % (i % NB)])

    for i in range(NB - 1):
        load(i)
    for i in range(NT):
        k = i % NB
        for hf in range(2):
            for f in range(8):
                P.op("pe", lambda e, i=i, hf=hf, f=f: e.matmul(po[:, hf * 512:(hf + 1) * 512], lhsT=catT[:, f, i * 128:(i + 1) * 128],
                                                               rhs=wo[:, f, hf * 512:(hf + 1) * 512], start=(f == 0), stop=(f == 7)),
                     reads=[catkey, pref + "wo"], writes=["ps:po%d" % hf])
        P.op("dve", lambda e, k=k: e.tensor_tensor(out=hb[k][:], in0=hb[k][:], in1=po[:], op=ALU.add),
             reads=[pref + "hb%d" % k, "ps:po0", "ps:po1"], writes=[pref + "hb%d" % k])
        if i + NB - 1 < NT:
            load(i + NB - 1)
        P.dma("sp", lambda e, i=i, k=k: e.dma_start(out=dv[:, i, :], in_=hb[k][:]), reads=[pref + "hb%d" % k])


def emit_even(P, C, D, bg=None):
    h_own, h_oth, g, w_in, w_out = D["h_own"], D["h_oth"], D["g"], D["w_in"], D["w_out"]
    cos2, sin2, gmask, cm, dpat, cp, h_out = D["cos2"], D["sin2"], D["gmask"], D["cm"], D["dpat"], D["cp"], D["h_out"]
    SC = 128.0 ** -0.5
    with ExitStack() as st:
        idf, idb = C["idf"], C["idb"]
        xnT = P.sb("xnT", [128, 8, 4096], BF16, st)
        catT = P.sb("catT", [128, 8, TOK], BF16, st)
        cosb = P.sb("cosb", [128, 4096], BF16, st); sinb = P.sb("sinb", [128, 4096], BF16, st)
        cmb = P.sb("cmb", [128, 2048], BF16, st)
        gm3 = P.sb("gm3", [128, 768], F32, st)
        onesb = P.sb("onesb", [128, 128], BF16, st)
        P.dma("pool", lambda e: e.dma_start(out=cosb[:], in_=cos2), writes=["cosb"])
        P.dma("pool", lambda e: e.dma_start(out=sinb[:], in_=sin2), writes=["sinb"])
        P.dma("pool", lambda e: e.dma_start(out=cmb[:], in_=cm), writes=["cmb"])
        P.dma("sp", lambda e: e.dma_start(out=gm3[:], in_=bcast_rows(gmask, 768)), writes=["gm3"])
        P.op("pool", lambda e: e.memset(onesb[:], 1.0), writes=["onesb"])
        pT = P.ps("pT", [128, 8, 128], BF16, st)
        with ExitStack() as s1:
            emit_xnT(P, s1, "no_", h_oth, g, xnT, 0, NT, C, pT)
            P.join()
        with ExitStack() as s1:
            emit_xnT(P, s1, "nw_", h_own, g, xnT, TOK, NT, C, pT)
            P.join()
        with ExitStack() as s2:
            pp = [P.ps("pp%d" % k, [128, 512], F32, s2) for k in range(2)]
            pst = [P.ps("pst%d" % k, [128, 512], F32, s2) for k in range(2)]
            pout = P.ps("pout", [128, 512], F32, s2)
            psum = P.ps("psum", [128, 512], F32, s2)
            pmisc = P.ps("pmisc", [128, 512], F32, s2)
            wh = P.sb("wh", [128, 8, 4, 128], BF16, s2)
            qT = P.sb("qT", [128, TOK], BF16, s2)
            kT = P.sb("kT", [128, 4096], BF16, s2)
            V = P.sb("V", [128, 32, 128], BF16, s2)
            sg = P.sb("sg", [128, TOK], F32, s2)
            tA = P.sb("tA", [128, 512], F32, s2)
            tB = P.sb("tB", [128, 512], F32, s2)
            PTb = [P.sb("PTb%d" % k, [128, 512], BF16, s2) for k in range(2)]
            dp = P.sb("dp", [128, 2560], F32, s2)
            cpb = P.sb("cpb", [128, 128], F32, s2)
            km = P.sb("km", [128, 16], F32, s2); kmb = P.sb("kmb", [128, 16], BF16, s2)
            gmv = P.sb("gmv", [128, 256], F32, s2); sel = P.sb("sel", [128, 256], F32, s2)
            m8 = P.sb("m8", [128, 16, 8], F32, s2)
            biasT = P.sb("biasT", [128, TOK], BF16, s2)
            P.op("pool", lambda e: e.memset(biasT[:], 0.0), writes=["biasT"])
            rsb = P.sb("rsb", [128, 512], F32, s2)
            sqb = P.sb("sqb", [128, 512], BF16, s2)
            accS = P.sb("accS", [128, 512], F32, s2)
            onesf = P.sb("onesf", [128, 128], F32, s2)
            P.op("pool", lambda e: e.memset(onesf[:], 1.0), writes=["onesf"])
            ppi = [0]

            def proj_feat(which, dst, tok0, ntok, mode, dkey):
                for t0 in range(0, ntok, 512):
                    k = ppi[0] % 2; ppi[0] += 1
                    for c in range(8):
                        P.op("pe", lambda e, c=c, k=k, t0=t0: e.matmul(pp[k][:], lhsT=wh[:, c, which, :], rhs=xnT[:, c, tok0 + t0:tok0 + t0 + 512],
                                                                       start=(c == 0), stop=(c == 7)),
                             reads=["wh", "no_xnT", "nw_xnT"], writes=["ps:pp%d" % k])
                    if mode == "silu":
                        P.op("act", lambda e, k=k, t0=t0: e.activation(out=dst[:, t0:t0 + 512], in_=pp[k][:], func=AF.Silu), reads=["ps:pp%d" % k], writes=[dkey])
                        continue
                    l0 = tok0 + t0
                    P.op("dve", lambda e, k=k, l0=l0: e.tensor_tensor(out=tA[:], in0=pp[k][:], in1=cosb[:, l0:l0 + 512], op=ALU.mult),
                         reads=["ps:pp%d" % k, "cosb"], writes=["tA"])
                    P.op("dve", lambda e, k=k, l0=l0: e.tensor_tensor(out=tB[0:64, :], in0=pp[k][64:128, :], in1=sinb[64:128, l0:l0 + 512], op=ALU.mult),
                         reads=["ps:pp%d" % k, "sinb"], writes=["tB"])
                    P.op("dve", lambda e, k=k, l0=l0: e.tensor_tensor(out=tB[64:128, :], in0=pp[k][0:64, :], in1=sinb[0:64, l0:l0 + 512], op=ALU.mult),
                         reads=["ps:pp%d" % k, "sinb"], writes=["tB"])
                    P.op("dve", lambda e, t0=t0: e.tensor_tensor(out=dst[:, t0:t0 + 512], in0=tA[:], in1=tB[:], op=ALU.add), reads=["tA", "tB"], writes=[dkey])

            for head in range(8):
                moba = head < 4
                hh = head % 4
                base = 0 if moba else 1536
                cols = [base + hh * 128, base + 512 + hh * 128, base + 1024 + hh * 128] + ([] if moba else [3072 + hh * 128])
                for wi, c0 in enumerate(cols):
                    P.dma("pool", lambda e, wi=wi, c0=c0: e.dma_start(out=wh[:, :, wi, :], in_=w_in[:, c0:c0 + 128].rearrange("(c p) n -> p c n", p=128)),
                          writes=["wh"])
                if bg is not None:
                    bg()
                proj_feat(0, qT, TOK, TOK, "rope", "qT")
                proj_feat(1, kT, 0, 4096, "rope", "kT")
                if not moba:
                    proj_feat(3, sg, TOK, TOK, "silu", "sg")
                    P.dma("sp", lambda e, hh=hh: e.dma_start(out=dp[:], in_=dpat[hh]), writes=["dp"])
                    P.dma("sp", lambda e, hh=hh: e.dma_start(out=cpb[:], in_=bcast_rows(cp[hh], 128)), writes=["cpb"])
                for i4 in range(8):
                    k = ppi[0] % 2; ppi[0] += 1
                    for q4 in range(4):
                        i = i4 * 4 + q4
                        for c in range(8):
                            P.op("pe", lambda e, c=c, k=k, i=i, q4=q4: e.matmul(pp[k][:, q4 * 128:(q4 + 1) * 128], lhsT=xnT[:, c, i * 128:(i + 1) * 128],
                                                                               rhs=wh[:, c, 2, :], start=(c == 0), stop=(c == 7)),
                                 reads=["wh", "no_xnT", "nw_xnT"], writes=["ps:pp%d" % k])
                    P.op("act", lambda e, k=k, i4=i4: e.activation(out=V[:, i4 * 4:(i4 + 1) * 4, :], in_=pp[k][:], func=AF.Copy), reads=["ps:pp%d" % k], writes=["V"])
                if moba:
                    P.op("dve", lambda e: e.tensor_reduce(out=km[:], in_=kT[:].rearrange("p (n k) -> p n k", k=256), axis=AX.X, op=ALU.add), reads=["kT"], writes=["km"])
                    P.op("dve", lambda e: e.tensor_copy(out=kmb[:], in_=km[:]), reads=["km"], writes=["kmb"])
                    for j in range(16):
                        P.op("pe", lambda e, j=j: e.matmul(pmisc[:, j * 16:(j + 1) * 16], lhsT=qT[:, j * 128:(j + 1) * 128], rhs=kmb[:], start=True, stop=True),
                             reads=["qT", "kmb"], writes=["ps:pmisc"])
                    P.op("dve", lambda e: e.tensor_tensor(out=gmv[:], in0=pmisc[:, 0:256], in1=gm3[:, 0:256], op=ALU.add), reads=["ps:pmisc", "gm3"], writes=["gmv"])
                    for j in range(16):
                        P.op("dve", lambda e, j=j: e.max(out=m8[:, j, :], in_=gmv[:, j * 16:(j + 1) * 16]), reads=["gmv"], writes=["m8_%d" % j])
                        P.op("dve", lambda e, j=j: e.tensor_scalar(out=sel[:, j * 16:(j + 1) * 16], in0=gmv[:, j * 16:(j + 1) * 16], scalar1=m8[:, j, 2:3], scalar2=None,
                                                                   op0=ALU.is_ge), reads=["gmv", "m8_%d" % j], writes=["sel"])
                    P.op("dve", lambda e: e.tensor_tensor(out=sel[:], in0=sel[:], in1=gm3[:, 256:512], op=ALU.mult), reads=["sel", "gm3"], writes=["sel"])
                    P.op("dve", lambda e: e.tensor_tensor(out=sel[:], in0=sel[:], in1=gm3[:, 512:768], op=ALU.add), reads=["sel", "gm3"], writes=["sel"])
                    P.op("dve", lambda e: e.tensor_scalar(out=sel[:], in0=sel[:], scalar1=-1.0, scalar2=30000.0, op0=ALU.add, op1=ALU.mult), reads=["sel"], writes=["sel"])
                    for j4 in range(4):
                        for q4 in range(4):
                            j = j4 * 4 + q4
                            P.op("pe", lambda e, j=j, q4=q4: e.transpose(out=pmisc[0:16, q4 * 128:(q4 + 1) * 128], in_=sel[:, j * 16:(j + 1) * 16], identity=idf[:]),
                                 reads=["sel", "idf"], writes=["ps:pmisc"])
                        P.op("act", lambda e, j4=j4: e.activation(out=biasT[0:16, j4 * 512:(j4 + 1) * 512], in_=pmisc[0:16, :], func=AF.Copy), reads=["ps:pmisc"], writes=["biasT"])
                units = []
                for c in range(4):
                    ktiles = list(range(16)) + [16 + k for k in range(4 * c + 4)]
                    for idx, kt in enumerate(ktiles):
                        units.append((c, kt, idx == 0, idx == len(ktiles) - 1))

                def s1(u):
                    c, kt, first, last = units[u]
                    k = u % 2
                    qs = slice(c * 512, (c + 1) * 512)
                    diag = kt >= 16 + 4 * c
                    r4 = kt - 16 - 4 * c
                    P.op("pe", lambda e: e.matmul(pst[k][:], lhsT=kT[:, kt * 128:(kt + 1) * 128], rhs=qT[:, qs], start=True, stop=(not moba)),
                         reads=["kT", "qT"], writes=["ps:pst%d" % k])
                    if moba:
                        n = kt // 2
                        P.op("pe", lambda e: e.matmul(pst[k][:], lhsT=cust(idb[:, n:n + 1], [[0, 128]]), rhs=biasT[:, qs], start=False, stop=(not diag)),
                             reads=["idb", "biasT"], writes=["ps:pst%d" % k])
                        if diag:
                            P.op("pe", lambda e: e.matmul(pst[k][:], lhsT=idb[:], rhs=cmb[:, r4 * 512:(r4 + 1) * 512], start=False, stop=True),
                                 reads=["idb", "cmb"], writes=["ps:pst%d" % k])

                def s2(u):
                    c, kt, first, last = units[u]
                    k = u % 2
                    diag = kt >= 16 + 4 * c
                    r4 = kt - 16 - 4 * c
                    if moba:
                        P.op("act", lambda e: e.activation(out=PTb[k][:], in_=pst[k][:], func=AF.Exp, scale=SC), reads=["ps:pst%d" % k], writes=["PTb%d" % k])
                        if first:
                            P.op("dve", lambda e: e.tensor_copy(out=accS[:], in_=PTb[k][:]), reads=["PTb%d" % k], writes=["accS"])
                        else:
                            P.op("dve", lambda e: e.tensor_tensor(out=accS[:], in0=accS[:], in1=PTb[k][:], op=ALU.add), reads=["accS", "PTb%d" % k], writes=["accS"])
                    else:
                        var = (r4 + 1) if diag else 0
                        ci = c * 32 + kt
                        P.op("dve", lambda e: e.scalar_tensor_tensor(out=PTb[k][:], in0=pst[k][:], scalar=cpb[:, ci:ci + 1],
                                                                     in1=dp[:, var * 512:(var + 1) * 512], op0=ALU.mult, op1=ALU.mult),
                             reads=["ps:pst%d" % k, "cpb", "dp"], writes=["PTb%d" % k])

                def s3(u):
                    c, kt, first, last = units[u]
                    k = u % 2
                    qs = slice(c * 512, (c + 1) * 512)
                    P.op("pe", lambda e: e.matmul(pout[:], lhsT=V[:, kt, :], rhs=PTb[k][:], start=first, stop=last),
                         reads=["V", "PTb%d" % k], writes=["ps:pout"])
                    if not last:
                        return
                    if moba:
                        P.op("pe", lambda e: e.matmul(psum[:], lhsT=onesf[:], rhs=accS[:], start=True, stop=True),
                             reads=["onesf", "accS"], writes=["ps:psum"])
                    dst = catT[:, head, qs]
                    if moba:
                        P.op("dve", lambda e: e.reciprocal(out=rsb[:], in_=psum[:]), reads=["ps:psum"], writes=["rsb"])
                        P.op("dve", lambda e: e.tensor_tensor(out=dst, in0=pout[:], in1=rsb[:], op=ALU.mult), reads=["ps:pout", "rsb"], writes=["catT"])
                    else:
                        P.op("act", lambda e: e.activation(out=sqb[:], in_=pout[:], func=AF.Square), reads=["ps:pout"], writes=["sqb"])
                        P.op("pe", lambda e: e.matmul(psum[:], lhsT=onesb[:], rhs=sqb[:], start=True, stop=True), reads=["onesb", "sqb"], writes=["ps:psum"])
                        P.op("act", lambda e: e.activation(out=rsb[:], in_=psum[:], func=AF.Sqrt, scale=1.0 / 128, bias=EPS), reads=["ps:psum"], writes=["rsb"])
                        P.op("dve", lambda e: e.reciprocal(out=tA[:], in_=rsb[:]), reads=["rsb"], writes=["tA"])
                        P.op("dve", lambda e: e.tensor_tensor(out=tB[:], in0=pout[:], in1=tA[:], op=ALU.mult), reads=["ps:pout", "tA"], writes=["tB"])
                        P.op("dve", lambda e: e.tensor_tensor(out=dst, in0=tB[:], in1=sg[:, qs], op=ALU.mult), reads=["tB", "sg"], writes=["catT"])

                s1(0)
                for u in range(len(units)):
                    if u + 1 < len(units):
                        s1(u + 1)
                    s2(u)
                    s3(u)
            P.join()
        with ExitStack() as s3:
            po = P.ps("po", [128, 1024], F32, s3)
            emit_outproj(P, s3, "op_", catT, "catT", w_out, h_own, h_out, po)
            P.join()


def build_even_nc():
    nc = bass.Bass("TRN2", target_bir_lowering=False)
    dr = lambda n, s, k="ExternalInput", dt=F32: nc.dram_tensor(n, list(s), dt, kind=k).ap()
    D = {"h_own": dr("h_own", [TOK, 1024]), "h_oth": dr("h_oth", [TOK, 1024]), "g": dr("g", [1024]), "w_in": dr("w_in", [1024, 3584]),
         "w_out": dr("w_out", [1024, 1024]), "cos2": dr("cos2", [128, 4096]), "sin2": dr("sin2", [128, 4096]), "gmask": dr("gmask", [768]),
         "cm": dr("cm", [128, 2048]), "dpat": dr("dpat", [4, 128, 2560]), "cp": dr("cp", [4, 128])}
    cd = {"ident": dr("ident", [128, 128]), "iota16": dr("iota16", [128, 16])}
    D["h_out"] = dr("h_out", [TOK, 1024], "ExternalOutput")
    with ExitStack() as st:
        P = Prog(nc, st)
        C = load_consts(P, nc, cd)
        emit_even(P, C, D)
        P.wait_all("sp")
        P.emit()
    return nc


def even_consts(half):
    inv = (10000.0 ** (-np.arange(64, dtype=np.float32) / 64)).astype(np.float32)
    pos = np.concatenate([np.arange(2048) + (1 - half) * 2048, np.arange(2048) + half * 2048]).astype(np.float32)
    ang = (pos[None, :] * np.tile(inv, 2)[:, None]).astype(np.float32)
    cos2 = np.cos(ang).astype(np.float32)
    sin2 = np.sin(ang).astype(np.float32)
    sin2[64:] = -sin2[64:]
    past = np.zeros((16, 16), np.float32); own = np.zeros((16, 16), np.float32)
    for j in range(16):
        qb = j // 2
        for n in range(16):
            if n < 8:
                past[j, n] = 1.0 if half == 1 else 0.0
            else:
                past[j, n] = 1.0 if n - 8 < qb else 0.0
                own[j, n] = 1.0 if n - 8 == qb else 0.0
    gmask = np.concatenate([np.where(past > 0, 0.0, -1e30).ravel(), past.ravel(), own.ravel()]).astype(np.float32)
    p = np.arange(128)[:, None]; q = np.arange(512)[None, :]
    cm = np.zeros((128, 4, 512), np.float32)
    for r4 in range(4):
        same = (q // 256) == (r4 // 2)
        cm[:, r4, :] = np.where(same & (r4 * 128 + p > q), -30000.0, 0.0)
    SC = 128.0 ** -0.5
    dpat = np.zeros((4, 128, 5, 512), np.float64); cp = np.zeros((4, 128), np.float64)
    for hh in range(4):
        gam = 1.0 - 2.0 ** (-5.0 - hh)
        dpat[hh, :, 0, :] = gam ** (q - p).astype(np.float64)
        for r4 in range(4):
            e = (q - r4 * 128 - p).astype(np.float64)
            dpat[hh, :, r4 + 1, :] = np.where(e >= 0, gam ** np.maximum(e, 0), 0.0)
        for c in range(4):
            for kt in range(32):
                if kt < 16:
                    v = gam ** float(2048 + c * 512 - kt * 128) * SC if half == 1 else 0.0
                else:
                    ko = kt - 16
                    v = gam ** float(c * 512 - ko * 128) * SC if ko * 128 < c * 512 else (SC if ko < 4 * c + 4 else 0.0)
                cp[hh, c * 32 + kt] = v
    return {"cos2": cos2, "sin2": sin2, "gmask": gmask, "cm": cm.reshape(128, 2048),
            "dpat": dpat.reshape(4, 128, 2560).astype(np.float32), "cp": cp.astype(np.float32)}


def emit_odd(P, C, D, bg=None):
    h_own, h_prev, g, w_in, w_out, convw, flag, h_out = (D[k] for k in ("h_own", "h_prev", "g", "w_in", "w_out", "convw", "flag", "h_out"))
    with ExitStack() as st:
        idf = C["idf"]
        xnT = P.sb("xnT", [128, 8, 128 + TOK], BF16, st)
        zT = P.sb("zT", [128, 8, TOK], BF16, st)
        cw = P.sb("cw", [128, 24], F32, st)
        cws = P.sb("cws", [24, 128], F32, st)
        fl = P.sb("fl", [128, 1], F32, st)
        P.dma("sp", lambda e: e.dma_start(out=cws[:], in_=convw), writes=["cws"])
        P.dma("sp", lambda e: e.dma_start(out=fl[:], in_=flag), writes=["fl"])
        pT = P.ps("pT", [128, 8, 128], BF16, st)
        with ExitStack() as s1:
            emit_xnT(P, s1, "no_", h_prev, g, xnT, 0, 1, C, pT)
            P.join()
        with ExitStack() as s1:
            emit_xnT(P, s1, "nw_", h_own, g, xnT, 128, NT, C, pT)
            P.join()
        with ExitStack() as s2:
            pb = [P.ps("pb%d" % k, [128, 512], F32, s2) for k in range(2)]
            pc = [P.ps("pc%d" % k, [128, 512], F32, s2) for k in range(2)]
            ph = [P.ps("ph%d" % k, [128, 512], F32, s2) for k in range(2)]
            P.op("pe", lambda e: e.transpose(out=pb[0][:, 0:24], in_=cws[:], identity=idf[0:24, 0:24]), reads=["cws", "idf"], writes=["ps:pb0"])
            P.op("dve", lambda e: e.tensor_copy(out=cw[:], in_=pb[0][:, 0:24]), reads=["ps:pb0"], writes=["cw"])
            whs = [P.sb("wh%d" % k, [128, 8, 3, 128], BF16, s2) for k in range(2)]
            uT = P.sb("uT", [128, 2 + TOK], F32, s2)
            cgs = P.sb("cgs", [128, 512], F32, s2)
            yb = P.sb("yb", [128, 512], F32, s2)
            ui = 0
            def load_w(f):
                for gi in range(3):
                    c0 = gi * 1024 + f * 128
                    P.dma("pool", lambda e, gi=gi, c0=c0: e.dma_start(out=whs[f % 2][:, :, gi, :], in_=w_in[:, c0:c0 + 128].rearrange("(c p) n -> p c n", p=128)),
                          writes=["wh%d" % (f % 2)])

            load_w(0)
            for f in range(8):
                wh, whk = whs[f % 2], "wh%d" % (f % 2)
                if f + 1 < 8:
                    load_w(f + 1)
                if bg is not None:
                    bg()
                k = ui % 2; ui += 1
                for c in range(8):
                    P.op("pe", lambda e, c=c, k=k, wh=wh: e.matmul(pc[k][:, 0:128], lhsT=wh[:, c, 1, :], rhs=xnT[:, c, 0:128], start=(c == 0), stop=(c == 7)),
                         reads=[whk, "no_xnT"], writes=["ps:pc%d" % k])
                for c in range(8):
                    P.op("pe", lambda e, c=c, k=k, wh=wh: e.matmul(ph[k][:, 0:128], lhsT=wh[:, c, 2, :], rhs=xnT[:, c, 0:128], start=(c == 0), stop=(c == 7)),
                         reads=[whk, "no_xnT"], writes=["ps:ph%d" % k])
                P.op("act", lambda e, k=k: e.activation(out=cgs[:, 0:128], in_=pc[k][:, 0:128], func=AF.Copy), reads=["ps:pc%d" % k], writes=["cgs"])
                P.op("dve", lambda e, k=k: e.tensor_tensor(out=yb[:, 0:128], in0=ph[k][:, 0:128], in1=cgs[:, 0:128], op=ALU.mult), reads=["ps:ph%d" % k, "cgs"], writes=["yb"])
                P.op("dve", lambda e: e.tensor_scalar(out=uT[:, 0:2], in0=yb[:, 126:128], scalar1=fl[:, 0:1], scalar2=None, op0=ALU.mult), reads=["yb", "fl"], writes=["uT"])
                for t0 in range(0, TOK, 512):
                    k = ui % 2; ui += 1
                    for (gi, pz, nm) in ((0, pb, "pb"), (1, pc, "pc"), (2, ph, "ph")):
                        for c in range(8):
                            P.op("pe", lambda e, c=c, k=k, gi=gi, pz=pz, t0=t0, wh=wh: e.matmul(pz[k][:], lhsT=wh[:, c, gi, :], rhs=xnT[:, c, 128 + t0:128 + t0 + 512],
                                                                                        start=(c == 0), stop=(c == 7)),
                                 reads=[whk, "nw_xnT"], writes=["ps:%s%d" % (nm, k)])
                    P.op("act", lambda e, k=k: e.activation(out=cgs[:], in_=pc[k][:], func=AF.Copy), reads=["ps:pc%d" % k], writes=["cgs"])
                    P.op("dve", lambda e, k=k, t0=t0: e.tensor_tensor(out=uT[:, 2 + t0:2 + t0 + 512], in0=ph[k][:], in1=cgs[:], op=ALU.mult),
                         reads=["ps:ph%d" % k, "cgs"], writes=["uT"])
                    P.op("dve", lambda e, f=f, t0=t0: e.tensor_scalar(out=yb[:], in0=uT[:, t0:t0 + 512], scalar1=cw[:, f:f + 1], scalar2=None, op0=ALU.mult),
                         reads=["uT", "cw"], writes=["yb"])
                    for kk in (1, 2):
                        P.op("dve", lambda e, f=f, t0=t0, kk=kk: e.scalar_tensor_tensor(out=yb[:], in0=uT[:, t0 + kk:t0 + kk + 512], scalar=cw[:, kk * 8 + f:kk * 8 + f + 1],
                                                                                        in1=yb[:], op0=ALU.mult, op1=ALU.add),
                             reads=["uT", "cw", "yb"], writes=["yb"])
                    P.op("dve", lambda e, f=f, k=k, t0=t0: e.tensor_tensor(out=zT[:, f, t0:t0 + 512], in0=yb[:], in1=pb[k][:], op=ALU.mult),
                         reads=["yb", "ps:pb%d" % k], writes=["zT"])
            P.join()
        with ExitStack() as s3:
            po = P.ps("po", [128, 1024], F32, s3)
            emit_outproj(P, s3, "op_", zT, "zT", w_out, h_own, h_out, po)
            P.join()


def build_odd_nc():
    nc = bass.Bass("TRN2", target_bir_lowering=False)
    dr = lambda n, s, k="ExternalInput", dt=F32: nc.dram_tensor(n, list(s), dt, kind=k).ap()
    D = {"h_own": dr("h_own", [TOK, 1024]), "h_prev": dr("h_prev", [128, 1024]), "g": dr("g", [1024]), "w_in": dr("w_in", [1024, 3072]),
         "w_out": dr("w_out", [1024, 1024]), "convw": dr("convw", [24, 128]), "flag": dr("flag", [128, 1])}
    cd = {"ident": dr("ident", [128, 128]), "iota16": dr("iota16", [128, 16])}
    D["h_out"] = dr("h_out", [TOK, 1024], "ExternalOutput")
    with ExitStack() as st:
        P = Prog(nc, st)
        C = load_consts(P, nc, cd)
        emit_odd(P, C, D)
        P.wait_all("sp")
        P.emit()
    return nc


PAIRS = [[0, 1], [2, 3], [4, 5], [6, 7]]


def build_fused_nc(nlayers=4, do_cc=True, dbg=False):
    nc = bass.Bass("TRN2", target_bir_lowering=False)
    dr = lambda n, s, k="ExternalInput", dt=F32: nc.dram_tensor(n, list(s), dt, kind=k).ap()
    x_own = dr("x_own", [TOK, 1024]); x_first = dr("x_first", [TOK, 1024])
    gmix = [dr("gmix%d" % l, [1024]) for l in range(4)]
    gffn = [dr("gffn%d" % l, [1024]) for l in range(4)]
    ewin = [dr("ewin%d" % i, [1024, 3584]) for i in range(2)]
    ewout = [dr("ewout%d" % i, [1024, 1024]) for i in range(2)]
    owin = [dr("owin%d" % i, [1024, 3072]) for i in range(2)]
    oconv = [dr("oconv%d" % i, [24, 128]) for i in range(2)]
    owout = [dr("owout%d" % i, [1024, 1024]) for i in range(2)]
    wq = [dr("wq%d" % l, [1024, 2048]) for l in range(nlayers)]
    sk = [dr("sk%d" % l, [16, 128, 128]) for l in range(nlayers)]
    pu = [dr("pu%d" % l, [16384, 1024]) for l in range(nlayers)]
    pv = [dr("pv%d" % l, [16384, 1024]) for l in range(nlayers)]
    fnorm = dr("fnorm", [1024])
    ec = {"cos2": dr("cos2", [128, 4096]), "sin2": dr("sin2", [128, 4096]), "gmask": dr("gmask", [768]),
          "cm": dr("cm", [128, 2048]), "dpat": dr("dpat", [4, 128, 2560]), "cp": dr("cp", [4, 128])}
    flag = dr("flag", [128, 1])
    cd = {"ident": dr("ident", [128, 128]), "iota16": dr("iota16", [128, 16])}
    y_out = dr("y_out", [TOK, 1024], "ExternalOutput")
    hA = nc.dram_tensor("hA_i", [TOK, 1024], F32)
    hB = nc.dram_tensor("hB_i", [TOK, 1024], F32)
    hG = nc.dram_tensor("hG_i", [8, 512, 1024], F32)
    uvt = [nc.dram_tensor("uv%d_i" % k, [16384, 2048], BF16).ap() for k in range(2)]
    with ExitStack() as st:
        P = Prog(nc, st)
        C = load_consts(P, nc, cd)
        P.lazy = {P.n_shared + 50, P.n_shared + 51}
        conv = lambda l: uv_convert_steps(P, pu[l], pv[l], uvt[l % 2], "uvtab%d" % (l % 2), 50 + (l % 2))
        for layer in range(nlayers):
            i = layer // 2
            todo = []
            if layer % 2 == 0:
                todo = conv(layer) + (conv(layer + 1) if layer + 1 < nlayers else [])
            per = -(-len(todo) // 8)

            def bg(todo=todo, per=per):
                for _ in range(per):
                    if todo:
                        todo.pop(0)()
            h_src = x_own if layer == 0 else hB.ap()
            if layer % 2 == 0:
                D = dict(ec)
                D.update(h_own=h_src, h_oth=(x_first if layer == 0 else (lambda t: hG.ap()[t // 2, (t % 2) * 128:(t % 2 + 1) * 128, :])), g=gmix[layer], w_in=ewin[i], w_out=ewout[i], h_out=hA.ap())
                emit_even(P, C, D, bg)
            else:
                D = dict(h_own=h_src, h_prev=hG.ap()[7, 128:256, :], g=gmix[layer], w_in=owin[i], w_out=owout[i], convw=oconv[i], flag=flag, h_out=hA.ap())
                emit_odd(P, C, D, bg)
            while todo:
                todo.pop(0)()
            Dp = dict(h_in=hA.ap(), h_out=hB.ap(), g=gffn[layer], wq=wq[layer], sk=sk[layer], uv=uvt[layer % 2], uvkey="uvtab%d" % (layer % 2))
            emit_peer_block(P, C, Dp, fin=((fnorm, y_out) if layer == nlayers - 1 else None))
            if layer < 3 and do_cc and (layer < nlayers - 1 or dbg):
                for k in (range(8) if layer == 1 else [7]):
                    P.cc(lambda e, k=k: e.collective_compute("AllGather", ALU.bypass, replica_groups=PAIRS,
                                                             ins=[hB.ap()[k * 256:(k + 1) * 256, :].opt()], outs=[hG.ap()[k].opt()]), semi=48)
                P.join()
        P.wait_all("sp")
        P.emit()
    return nc

_NC_CACHE = {}


def _get_nc(kind):
    if kind not in _NC_CACHE:
        _NC_CACHE[kind] = {"even": build_even_nc, "odd": build_odd_nc, "peer": build_peer_nc,
                           "peer_final": lambda: build_peer_nc(final=True), "fused": build_fused_nc}[kind]()
    return _NC_CACHE[kind]


def kernel(x, norm_mix, norm_ffn, even_w_in, even_w_out, odd_w_in, odd_conv, odd_w_out,
           peer_w_q, peer_sub_keys, peer_u, peer_v, final_norm):
    from concourse.bass_utils import run_bass_kernel_spmd
    f32 = lambda a: np.ascontiguousarray(np.asarray(a, dtype=np.float32))
    x = f32(x)
    cst = host_consts()
    econ = [even_consts(0), even_consts(1)]
    shared = {"fnorm": f32(final_norm)}
    for l in range(4):
        shared["gmix%d" % l] = f32(norm_mix[l]); shared["gffn%d" % l] = f32(norm_ffn[l])
        shared["wq%d" % l] = f32(peer_w_q[l]); shared["sk%d" % l] = f32(np.asarray(peer_sub_keys[l]).reshape(16, 128, 128))
        shared["pu%d" % l] = f32(peer_u[l]); shared["pv%d" % l] = f32(peer_v[l])
    for i in range(2):
        shared["ewin%d" % i] = f32(even_w_in[i]); shared["ewout%d" % i] = f32(even_w_out[i])
        shared["owin%d" % i] = f32(odd_w_in[i]); shared["owout%d" % i] = f32(odd_w_out[i])
        shared["oconv%d" % i] = f32(np.asarray(odd_conv[i]).reshape(24, 128))
    maps = []
    for c in range(8):
        b, half = c // 2, c % 2
        m = dict(shared)
        m.update(cst); m.update(econ[half])
        m["x_own"] = f32(x[b, half * 2048:(half + 1) * 2048]); m["x_first"] = f32(x[b, 0:2048])
        m["flag"] = np.full((128, 1), float(half), np.float32)
        maps.append(m)
    res = run_bass_kernel_spmd(_get_nc("fused"), maps, core_ids=list(range(8)))
    y = np.zeros((4, 4096, 1024), np.float32)
    for c in range(8):
        y[c // 2, (c % 2) * 2048:(c % 2 + 1) * 2048] = res.results[c]["y_out"]
    return y


def kernel_unfused(x, norm_mix, norm_ffn, even_w_in, even_w_out, odd_w_in, odd_conv, odd_w_out,
           peer_w_q, peer_sub_keys, peer_u, peer_v, final_norm):
    from concourse.bass_utils import run_bass_kernel_spmd
    f32 = lambda a: np.ascontiguousarray(np.asarray(a, dtype=np.float32))
    h = f32(x).copy()
    cst = host_consts()
    econ = [even_consts(0), even_consts(1)]
    cores = list(range(8))
    y = None
    for layer in range(4):
        i = layer // 2
        maps = []
        for c in cores:
            b, half = c // 2, c % 2
            own = f32(h[b, half * 2048:(half + 1) * 2048])
            if layer % 2 == 0:
                maps.append(dict(h_own=own, h_oth=f32(h[b, (1 - half) * 2048:(2 - half) * 2048]), g=f32(norm_mix[layer]),
                                 w_in=f32(even_w_in[i]), w_out=f32(even_w_out[i]), **cst, **econ[half]))
            else:
                prev = h[b, 1920:2048] if half == 1 else h[b, 0:128]
                maps.append(dict(h_own=own, h_prev=f32(prev), flag=np.full((128, 1), float(half), np.float32), g=f32(norm_mix[layer]),
                                 w_in=f32(odd_w_in[i]), w_out=f32(odd_w_out[i]), convw=f32(np.asarray(odd_conv[i]).reshape(24, 128)), **cst))
        res = run_bass_kernel_spmd(_get_nc("even" if layer % 2 == 0 else "odd"), maps, core_ids=cores)
        for c in cores:
            h[c // 2, (c % 2) * 2048:(c % 2 + 1) * 2048] = res.results[c]["h_out"]
        last = layer == 3
        maps = []
        for c in cores:
            b, half = c // 2, c % 2
            m = dict(h_in=f32(h[b, half * 2048:(half + 1) * 2048]), g=f32(norm_ffn[layer]), wq=f32(peer_w_q[layer]),
                     sk=f32(np.asarray(peer_sub_keys[layer]).reshape(16, 128, 128)), u=f32(peer_u[layer]), v=f32(peer_v[layer]), **cst)
            if last:
                m["fnorm"] = f32(final_norm)
            maps.append(m)
        res = run_bass_kernel_spmd(_get_nc("peer_final" if last else "peer"), maps, core_ids=cores)
        for c in cores:
            h[c // 2, (c % 2) * 2048:(c % 2 + 1) * 2048] = res.results[c]["h_out"]
        if last:
            y = np.zeros_like(h)
            for c in cores:
                y[c // 2, (c % 2) * 2048:(c % 2 + 1) * 2048] = res.results[c]["y_out"]
    return y.astype(np.float32)
```

```python
from contextlib import ExitStack

import numpy as np
import concourse.bass as bass
import concourse.mybir as mybir

F32 = mybir.dt.float32
BF16 = mybir.dt.bfloat16
I32 = mybir.dt.int32
U32 = mybir.dt.uint32
AF = mybir.ActivationFunctionType
ALU = mybir.AluOpType
AX = mybir.AxisListType

ENGS = ("pe", "act", "dve", "pool", "sp")
STRICT = False


class Prog:
    def __init__(self, nc, stack, n_dma_sems=64, n_shared=12):
        self.n_shared = n_shared
        self.nc = nc
        self.stack = stack
        self.eng = {"pe": nc.tensor, "act": nc.scalar, "dve": nc.vector, "pool": nc.gpsimd, "sp": nc.sync}
        self.sem = {e: stack.enter_context(nc.semaphore("s_" + e)) for e in ENGS}
        self.cnt = {e: 0 for e in ENGS}
        self.lists = {e: [] for e in ENGS}
        self.waited = {e: {} for e in ENGS}
        self.last_w = {}
        self.readers = {}
        self.dma_sems = [stack.enter_context(nc.semaphore("s_dma%d" % i)) for i in range(n_dma_sems)]
        self.dma_cnt = [0] * n_dma_sems
        self.dma_rr = 0
        self.dma_rr_pool = 0
        self.uid = 0
        self.lazy = set()
        self._rec = None
        self.semname = {}
        for e in ENGS:
            self.semname[id(self.sem[e])] = e

    def sb(self, name, shape, dt, st=None):
        self.uid += 1
        return (st or self.stack).enter_context(self.nc.sbuf_tensor("sb_%s_%d" % (name, self.uid), list(shape), dt))

    def ps(self, name, shape, dt=F32, st=None):
        self.uid += 1
        return (st or self.stack).enter_context(self.nc.psum_tensor("ps_%s_%d" % (name, self.uid), list(shape), dt))

    def cc(self, fn, semi):
        waits = self._deps("pool", [], [])
        semi = self.n_shared + semi
        self.dma_cnt[semi] += 1
        self.lists["pool"].append((waits, fn, self.dma_sems[semi], None))

    def _deps(self, e, reads, writes):
        toks = []
        assert all(not r.startswith("ps:") for r in reads)
        for r in reads:
            t = self.last_w.get(r)
            if t is not None:
                toks.append((t, "raw"))
        for w in writes:
            t = self.last_w.get(w)
            if t is not None:
                toks.append((t, "waw"))
            for t in self.readers.get(w, ()):
                toks.append((t, "war"))
        waits = {}
        for (semkey, sem, val), kind in toks:
            if isinstance(semkey, int) and semkey < self.n_shared:
                val = max(val, self.dma_cnt[semkey])
            if semkey == e and kind != "raw" and not STRICT:
                continue
            if self.waited[e].get(semkey, 0) >= val:
                continue
            if waits.get(semkey, (None, 0))[1] < val:
                waits[semkey] = (sem, val)
        for semkey, (sem, val) in waits.items():
            self.waited[e][semkey] = val
        return list(waits.values())

    def _commit(self, tok, reads, writes):
        for w in writes:
            self.last_w[w] = tok
            self.readers[w] = []
        for r in reads:
            self.readers.setdefault(r, []).append(tok)

    def start_record(self):
        self._rec = []

    def stop_record(self):
        r, self._rec = self._rec, None
        return r

    def gap(self):
        if self._rec is not None:
            self._rec.append(("gap",))

    def op(self, e, fn, reads=(), writes=()):
        if self._rec is not None:
            self._rec.append(("op", e, fn, list(reads), list(writes)))
            return None
        writes = list(writes) + [r for r in reads if r.startswith("ps:")]
        reads = [r for r in reads if not r.startswith("ps:")]
        waits = self._deps(e, reads, writes)
        self.cnt[e] += 1
        tok = (e, self.sem[e], self.cnt[e])
        self.lists[e].append((waits, fn, self.sem[e], 1))
        self._commit(tok, reads, writes)
        return tok

    def dma(self, e, fn, reads=(), writes=(), semi=None):
        waits = self._deps(e, reads, writes)
        if semi is None:
            if e == "pool":
                semi = 8 + self.dma_rr_pool
                self.dma_rr_pool = (self.dma_rr_pool + 1) % (self.n_shared - 8)
            else:
                semi = self.dma_rr
                self.dma_rr = (self.dma_rr + 1) % 8
        else:
            semi = self.n_shared + semi
        self.dma_cnt[semi] += 16
        tok = (semi, self.dma_sems[semi], self.dma_cnt[semi])
        self.lists[e].append((waits, fn, self.dma_sems[semi], 16))
        self._commit(tok, reads, writes)
        return tok

    def join(self, engs=ENGS):
        for e in engs:
            waits = []
            for ee in ENGS:
                if ee != e and self.cnt[ee] > 0 and self.waited[e].get(ee, 0) < self.cnt[ee]:
                    waits.append((self.sem[ee], self.cnt[ee]))
                    self.waited[e][ee] = self.cnt[ee]
            for i, s in enumerate(self.dma_sems):
                if i in self.lazy:
                    continue
                if self.dma_cnt[i] > 0 and self.waited[e].get(i, 0) < self.dma_cnt[i]:
                    waits.append((s, self.dma_cnt[i]))
                    self.waited[e][i] = self.dma_cnt[i]
            if waits:
                self.lists[e].append((waits, None, None, 0))

    def wait_all(self, e):
        waits = []
        for ee in ENGS:
            if self.cnt[ee] > 0 and self.waited[e].get(ee, 0) < self.cnt[ee]:
                waits.append((self.sem[ee], self.cnt[ee]))
        for i, s in enumerate(self.dma_sems):
            if self.dma_cnt[i] > 0:
                waits.append((s, self.dma_cnt[i]))
        self.lists[e].append((waits, None, None, 0))

    def emit(self):
        nc = self.nc
        with nc.Block() as block:
            def run(e):
                def body(engine):
                    for waits, fn, sem, inc in self.lists[e]:
                        for s, v in waits:
                            engine.wait_ge(s, v)
                        if fn is not None:
                            ins = fn(engine)
                            if inc is None:
                                ins.then_inc(sem)
                            else:
                                ins.then_inc(sem, inc)
                return body
            block.sync(run("sp"))
            block.scalar(run("act"))
            block.vector(run("dve"))
            block.gpsimd(run("pool"))
            block.tensor(run("pe"))


EPS = 1e-6
DBG_STAGE = 99
DBG_SUB = 99
NT = 16
TOK = NT * 128


def cust(ap, dims):
    return bass.AP(ap.tensor, ap.offset, [list(ap.ap[0])] + [list(d) for d in dims])


def bcast_rows(dram_ap, n):
    return bass.AP(dram_ap.tensor, dram_ap.offset, [[0, 128], [1, n]])


def emit_rstd(P, st, pref, h, hkey, ntiles, junk):
    ss = P.sb(pref + "ss", [128, ntiles], F32, st)
    sq = P.sb(pref + "sq", [128, ntiles], F32, st)
    rstd = P.sb(pref + "rstd", [128, ntiles], F32, st)
    for i in range(ntiles):
        P.op("act", lambda e, i=i: e.activation(out=junk[:], in_=h[:, i, :], func=AF.Square, accum_out=ss[:, i:i + 1]),
             reads=[hkey(i)], writes=[pref + "junk", pref + "ss%d" % i])
    P.op("act", lambda e: e.activation(out=sq[:], in_=ss[:], func=AF.Sqrt, scale=1.0 / 1024, bias=EPS),
         reads=[pref + "ss%d" % i for i in range(ntiles)], writes=[pref + "sq"])
    P.op("dve", lambda e: e.reciprocal(out=rstd[:], in_=sq[:]), reads=[pref + "sq"], writes=[pref + "rstd"])
    return rstd


def emit_peer(P, h, hkey, D, C):
    idf, idb, iota16 = C["idf"], C["idb"], C["iota16"]
    with ExitStack() as stC:
        xn = P.sb("pe_xn", [128, NT, 1024], BF16, stC)
        junk = P.sb("pe_junk", [128, 1024], BF16, stC)
        idxT = P.sb("pe_idxT", [128, 256], I32, stC)
        gateT = P.sb("pe_gateT", [128, 256], F32, stC)
        wqb = P.sb("pe_wqb", [128, 8, 2048], BF16, stC)
        KT = P.sb("pe_KT", [128, 16, 128], BF16, stC)
        pA0 = P.ps("pe_pA0", [128, 512], F32, stC)
        pA1 = P.ps("pe_pA1", [128, 512], F32, stC)
        pT = pA0[:].bitcast(BF16).rearrange("p (c t) -> p c t", c=8)
        for c in range(8):
            P.dma("pool", lambda e, c=c: e.dma_start(out=wqb[:, c, :], in_=D["wq"][c * 128:(c + 1) * 128, :]), writes=["pe_wqb"])
        scs = P.sb("pe_scs", [128, 2048], F32, stC)
        skf = scs[:].rearrange("p (a d) -> p a d", a=16)
        P.dma("sp", lambda e: e.dma_start(out=skf, in_=D["sk"].rearrange("a n d -> n a d")), writes=["pe_scs%d" % q for q in range(4)])
        for a in range(16):
            P.op("pe", lambda e, a=a: e.transpose(out=pA1[:, (a % 4) * 128:(a % 4 + 1) * 128], in_=skf[:, a, :], identity=idf[:]),
                 reads=["pe_scs%d" % q for q in range(4)] + ["idf"], writes=["ps:A1"])
            if a % 4 == 3:
                g4 = a // 4
                P.op("act", lambda e, g4=g4: e.activation(out=KT[:, g4 * 4:(g4 + 1) * 4, :], in_=pA1[:], func=AF.Copy),
                     reads=["ps:A1"], writes=["pe_KT"])
        with ExitStack() as stN:
            gB = P.sb("pe_gB", [128, 1024], F32, stN)
            P.dma("sp", lambda e: e.dma_start(out=gB[:], in_=bcast_rows(D["g"], 1024)), writes=["pe_gB"])
            rstd = emit_rstd(P, stN, "pe_", h, hkey, NT, junk)
            for i in range(NT):
                P.op("dve", lambda e, i=i: e.scalar_tensor_tensor(out=xn[:, i, :], in0=h[:, i, :], scalar=rstd[:, i:i + 1], in1=gB[:],
                                                                 op0=ALU.mult, op1=ALU.mult),
                     reads=[hkey(i), "pe_rstd", "pe_gB"], writes=["pe_xn%d" % i])
            P.join()
        xnT = P.sb("pe_xnT", [128, 8, 128], BF16, stC)
        qT = P.sb("pe_qT", [128, 16, 128], BF16, stC)
        scr = P.sb("pe_scr", [128, 2048], F32, stC)
        eq = scr[:].bitcast(BF16)[:, 0:2048]
        prod = scr[:].bitcast(BF16)[:, 2048:4096]
        m = P.sb("pe_m", [128, 16, 16], F32, stC)
        ix = P.sb("pe_ix", [128, 16, 16], U32, stC)
        ixf = P.sb("pe_ixf", [128, 16, 16], F32, stC)
        cand = scs[:].rearrange("p (h c) -> p h c", h=8)
        ts = P.sb("pe_ts", [128, 8, 16], F32, stC)
        pos = P.sb("pe_pos", [128, 8, 16], U32, stC)
        au = P.sb("pe_au", [128, 128], U32, stC)
        bu = P.sb("pe_bu", [128, 128], U32, stC)
        af = P.sb("pe_af", [128, 128], F32, stC)
        bf = P.sb("pe_bf", [128, 128], F32, stC)
        e1 = P.sb("pe_e1", [128, 128], F32, stC)
        e2 = P.sb("pe_e2", [128, 128], F32, stC)
        eidx = P.sb("pe_eidx", [128, 128], F32, stC)
        negm = P.sb("pe_negm", [128, 8], F32, stC)
        ex = P.sb("pe_ex", [128, 8, 16], F32, stC)
        Z = P.sb("pe_Z", [128, 8], F32, stC)
        rZ = P.sb("pe_rZ", [128, 8], F32, stC)
        gate = P.sb("pe_gate", [128, 128], F32, stC)
        SCRK = ["pe_scr%d" % k for k in range(16)]
        SCSK = ["pe_scs%d" % q for q in range(4)]

        def phase_a(i):
            sl = (i % 2) * 128
            banks = [(pA1, "ps:A1"), (pA0, "ps:A0")]
            for c in range(8):
                P.op("pe", lambda e, c=c: e.transpose(out=pT[:, c, :], in_=xn[:, i, c * 128:(c + 1) * 128], identity=idb[:]),
                     reads=["pe_xn%d" % i, "idb"], writes=["ps:A0"])
            P.gap()
            P.op("act", lambda e: e.activation(out=xnT[:], in_=pT, func=AF.Copy), reads=["ps:A0"], writes=["pe_xnT"])
            P.gap()

            def qproj(g):
                bk, bn = banks[g % 2]
                for k in range(4):
                    hp = 4 * g + k
                    for c in range(8):
                        P.op("pe", lambda e, hp=hp, c=c, k=k: e.matmul(bk[:, k * 128:(k + 1) * 128], lhsT=wqb[:, c, hp * 128:(hp + 1) * 128], rhs=xnT[:, c, :],
                                                                       start=(c == 0), stop=(c == 7)),
                             reads=["pe_wqb", "pe_xnT"], writes=[bn])

            def qcopy(g):
                bk, bn = banks[g % 2]
                P.op("act", lambda e: e.activation(out=qT[:, 4 * g:4 * g + 4, :], in_=bk[:], func=AF.Copy), reads=[bn], writes=["pe_qT%d" % g])

            def score(q4):
                bk, bn = banks[q4 % 2]
                for k in range(4):
                    hp = q4 * 4 + k
                    P.op("pe", lambda e, k=k, hp=hp: e.matmul(bk[:, k * 128:(k + 1) * 128], lhsT=qT[:, hp, :], rhs=KT[:, hp, :], start=True, stop=True),
                         reads=["pe_qT%d" % q4, "pe_KT"], writes=[bn])

            def scopy(q4):
                bk, bn = banks[q4 % 2]
                P.op("act", lambda e: e.activation(out=scs[:, q4 * 512:(q4 + 1) * 512], in_=bk[:], func=AF.Copy), reads=[bn], writes=["pe_scs%d" % q4])

            qproj(0); qproj(1); P.gap(); qcopy(0); qcopy(1); P.gap()
            qproj(2); qproj(3); P.gap(); qcopy(2); qcopy(3); P.gap()
            score(0); score(1); P.gap(); scopy(0); scopy(1); P.gap()
            score(2); score(3); P.gap(); scopy(2); scopy(3); P.gap()
            svs = [scs[:, hp * 128:(hp + 1) * 128] for hp in range(16)]
            sks = ["pe_scs%d" % (hp // 4) for hp in range(16)]
            for hp in range(16):
                P.op("dve", lambda e, hp=hp: e.max(out=m[:, hp, 0:8], in_=svs[hp]), reads=[sks[hp]], writes=["pe_m%da" % hp])
            for hp in range(16):
                P.op("dve", lambda e, hp=hp: e.max_index(out=ix[:, hp, 0:8], in_max=m[:, hp, 0:8], in_values=svs[hp]),
                     reads=[sks[hp], "pe_m%da" % hp], writes=["pe_ix%da" % hp])
            for hp in range(16):
                P.op("dve", lambda e, hp=hp: e.match_replace(out=scr[:, hp * 128:(hp + 1) * 128], in_to_replace=m[:, hp, 0:8], in_values=svs[hp], imm_value=-1e30),
                     reads=[sks[hp], "pe_m%da" % hp], writes=[SCRK[hp]])
            for hp in range(16):
                P.op("dve", lambda e, hp=hp: e.max(out=m[:, hp, 8:16], in_=scr[:, hp * 128:(hp + 1) * 128]), reads=[SCRK[hp]], writes=["pe_m%db" % hp])
            for hp in range(16):
                P.op("dve", lambda e, hp=hp: e.max_index(out=ix[:, hp, 8:16], in_max=m[:, hp, 8:16], in_values=svs[hp]),
                     reads=[sks[hp], "pe_m%db" % hp], writes=["pe_ix%db" % hp])
            mkeys = ["pe_m%d%s" % (hp, x) for hp in range(16) for x in "ab"]
            ixkeys = ["pe_ix%d%s" % (hp, x) for hp in range(16) for x in "ab"]
            P.op("dve", lambda e: e.tensor_tensor(
                out=cand[:].rearrange("p h (a b) -> p h a b", a=16),
                in0=cust(m[:, 0, 0:1], [[32, 8], [1, 16], [0, 16]]),
                in1=cust(m[:, 1, 0:1], [[32, 8], [0, 16], [1, 16]]), op=ALU.add),
                reads=mkeys + ixkeys, writes=["pe_cand"] + SCSK)
            P.op("dve", lambda e: e.tensor_copy(out=ixf[:], in_=ix[:]), reads=ixkeys, writes=["pe_ixf"])
            cvs = [cand[:, hh, :] for hh in range(8)]
            for hh in range(8):
                P.op("dve", lambda e, hh=hh: e.max(out=ts[:, hh, 0:8], in_=cvs[hh]), reads=["pe_cand"] + SCSK, writes=["pe_ts%da" % hh])
            for hh in range(8):
                P.op("dve", lambda e, hh=hh: e.max_index(out=pos[:, hh, 0:8], in_max=ts[:, hh, 0:8], in_values=cvs[hh]),
                     reads=["pe_cand", "pe_ts%da" % hh] + SCSK, writes=["pe_pos%da" % hh])
            for hh in range(8):
                P.op("dve", lambda e, hh=hh: e.match_replace(out=scr[:, hh * 256:(hh + 1) * 256], in_to_replace=ts[:, hh, 0:8], in_values=cvs[hh], imm_value=-1e30),
                     reads=["pe_cand", "pe_ts%da" % hh] + SCSK, writes=[SCRK[2 * hh], SCRK[2 * hh + 1]])
            for hh in range(8):
                P.op("dve", lambda e, hh=hh: e.max(out=ts[:, hh, 8:16], in_=scr[:, hh * 256:(hh + 1) * 256]),
                     reads=[SCRK[2 * hh], SCRK[2 * hh + 1]], writes=["pe_ts%db" % hh])
            for hh in range(8):
                P.op("dve", lambda e, hh=hh: e.max_index(out=pos[:, hh, 8:16], in_max=ts[:, hh, 8:16], in_values=cvs[hh]),
                     reads=["pe_cand", "pe_ts%db" % hh] + SCSK, writes=["pe_pos%db" % hh])
            tskeys = ["pe_ts%d%s" % (hh, x) for hh in range(8) for x in "ab"]
            poskeys = ["pe_pos%d%s" % (hh, x) for hh in range(8) for x in "ab"]
            posf = pos[:].rearrange("p h k -> p (h k)")
            P.op("dve", lambda e: e.tensor_single_scalar(out=au[:], in_=posf, scalar=4, op=ALU.logical_shift_right), reads=poskeys, writes=["pe_au"])
            P.op("dve", lambda e: e.tensor_single_scalar(out=bu[:], in_=posf, scalar=15, op=ALU.bitwise_and), reads=poskeys, writes=["pe_bu"])
            P.op("dve", lambda e: e.tensor_copy(out=af[:], in_=au[:]), reads=["pe_au"], writes=["pe_af"])
            P.op("dve", lambda e: e.tensor_copy(out=bf[:], in_=bu[:]), reads=["pe_bu"], writes=["pe_bf"])
            for (sel, side, eo, en) in ((af, 0, e1, "pe_e1"), (bf, 1, e2, "pe_e2")):
                sn = "pe_af" if side == 0 else "pe_bf"
                P.op("dve", lambda e, sel=sel: e.tensor_tensor(
                    out=eq.rearrange("p (s a) -> p s a", a=16),
                    in0=cust(sel[:, 0:1], [[1, 128], [0, 16]]),
                    in1=cust(iota16[:, 0:1], [[0, 128], [1, 16]]), op=ALU.is_equal),
                    reads=[sn, "iota16"] + SCRK, writes=["pe_eq"])
                P.op("dve", lambda e, side=side: e.tensor_tensor(
                    out=prod.rearrange("p (h k a) -> p h k a", h=8, k=16),
                    in0=eq.rearrange("p (h k a) -> p h k a", h=8, k=16),
                    in1=cust(ixf[:, side, 0:1], [[32, 8], [0, 16], [1, 16]]), op=ALU.mult),
                    reads=["pe_eq", "pe_ixf"], writes=["pe_prod"])
                P.op("dve", lambda e, eo=eo: e.tensor_reduce(out=eo[:], in_=prod.rearrange("p (s a) -> p s a", a=16), axis=AX.X, op=ALU.add),
                     reads=["pe_prod"], writes=[en])
            P.op("dve", lambda e: e.scalar_tensor_tensor(out=eidx[:], in0=e1[:], scalar=C["c128"][:, 0:1], in1=e2[:], op0=ALU.mult, op1=ALU.add),
                 reads=["pe_e1", "pe_e2", "c128"], writes=["pe_eidx"] + SCRK)
            P.op("dve", lambda e: e.tensor_scalar(out=negm[:], in0=cust(ts[:, 0, 0:1], [[16, 8]]), scalar1=-1.0, scalar2=None, op0=ALU.mult),
                 reads=tskeys, writes=["pe_negm"])
            P.gap()
            for hh in range(8):
                P.op("act", lambda e, hh=hh: e.activation(out=ex[:, hh, :], in_=ts[:, hh, :], func=AF.Exp, bias=negm[:, hh:hh + 1],
                                                          accum_out=Z[:, hh:hh + 1]),
                     reads=tskeys + ["pe_negm"], writes=["pe_ex%d" % hh, "pe_Z%d" % hh])
            P.gap()
            P.op("dve", lambda e: e.reciprocal(out=rZ[:], in_=Z[:]), reads=["pe_Z%d" % hh for hh in range(8)], writes=["pe_rZ"])
            P.op("dve", lambda e: e.tensor_tensor(out=gate[:].rearrange("p (h k) -> p h k", h=8), in0=ex[:],
                                                  in1=cust(rZ[:, 0:1], [[1, 8], [0, 16]]), op=ALU.mult),
                 reads=["pe_ex%d" % hh for hh in range(8)] + ["pe_rZ"], writes=["pe_gate"])
            P.gap()
            P.op("pe", lambda e: e.transpose(out=pA0[:, 0:128], in_=eidx[:], identity=idf[:]), reads=["pe_eidx", "idf"], writes=["ps:A0"])
            P.op("pe", lambda e: e.transpose(out=pA0[:, 128:256], in_=gate[:], identity=idf[:]), reads=["pe_gate", "idf"], writes=["ps:A0"])
            P.gap()
            P.op("dve", lambda e: e.tensor_copy(out=idxT[:, sl:sl + 128], in_=pA0[:, 0:128]), reads=["ps:A0"], writes=["pe_idxT%d" % (i % 2)])
            P.op("act", lambda e: e.activation(out=gateT[:, sl:sl + 128], in_=pA0[:, 128:256], func=AF.Copy),
                 reads=["ps:A0"], writes=["pe_gateT%d" % (i % 2)])

        NR = 8
        UV = [P.sb("pe_UV%d" % r, [128, 2048], BF16, stC) for r in range(NR)]
        NZ = 4
        Zb = [P.sb("pe_Zb%d" % r, [128, 256], BF16, stC) for r in range(NZ)]
        hidc = P.sb("pe_hidc", [128, 8], F32, stC)
        gelc = P.sb("pe_gelc", [128, 8], F32, stC)
        pxb = [P.ps("pe_pxb%d" % r, [128, 1024], F32, stC) for r in range(2)]
        pout = P.ps("pe_pout", [128, 1024], F32, stC)
        for r in range(NZ):
            P.op("pool", lambda e, r=r: e.memset(Zb[r][:], 0.0), writes=["pe_Zb%d" % r])

        def u_stage(i, t):
            tt = i * 128 + t
            col = (i % 2) * 128 + t
            r, k8 = tt % NR, tt % 8
            P.dma("pool", lambda e: e.indirect_dma_start(out=UV[r][:], out_offset=None, in_=D["uv"],
                                                         in_offset=bass.IndirectOffsetOnAxis(ap=idxT[:, col:col + 1], axis=0)),
                  reads=["pe_idxT%d" % (i % 2), D["uvkey"]], writes=["pe_UV%d" % r], semi=r)
            pb = pxb[tt % 2]
            lhs = cust(idb[:, t:t + 1], [[0, 128]])
            for hf in range(2):
                P.op("pe", lambda e, hf=hf: e.matmul(pb[:, hf * 512:(hf + 1) * 512], lhsT=lhs, rhs=xn[:, i, hf * 512:(hf + 1) * 512], start=True, stop=True),
                     reads=["idb", "pe_xn%d" % i], writes=["ps:pxb%d_%d" % (tt % 2, hf)])
            P.op("dve", lambda e: e.scalar_tensor_tensor(out=junk[:], in0=UV[r][:, 0:1024], scalar=C["c1"][:, 0:1], in1=pb[:], op0=ALU.mult, op1=ALU.mult,
                                                         accum_out=hidc[:, k8:k8 + 1]),
                 reads=["pe_UV%d" % r, "ps:pxb%d_0" % (tt % 2), "ps:pxb%d_1" % (tt % 2), "c1"], writes=["pe_junk3", "pe_hidc%d" % k8])
            z = tt % NZ
            P.op("act", lambda e: e.activation(out=gelc[:, k8:k8 + 1], in_=hidc[:, k8:k8 + 1], func=AF.Gelu), reads=["pe_hidc%d" % k8], writes=["pe_gelc%d" % k8])
            P.op("act", lambda e: e.activation(out=Zb[z][:, 127:128], in_=gelc[:, k8:k8 + 1], func=AF.Copy, scale=gateT[:, col:col + 1]),
                 reads=["pe_gelc%d" % k8, "pe_gateT%d" % (i % 2)], writes=["pe_Zb%d" % z])

        def v_mm(i, t):
            tt = i * 128 + t
            r, z = tt % NR, tt % NZ
            for hf in range(2):
                P.op("pe", lambda e, hf=hf: e.matmul(pout[:, hf * 512:(hf + 1) * 512], lhsT=Zb[z][:, 127 - t:255 - t],
                                                     rhs=UV[r][:, 1024 + hf * 512:1024 + (hf + 1) * 512], start=(t == 0), stop=(t == 127)),
                     reads=["pe_Zb%d" % z, "pe_UV%d" % r], writes=["ps:pout_%d" % hf])
            if t == 127:
                P.op("dve", lambda e: e.tensor_tensor(out=h[:, i, :], in0=h[:, i, :], in1=pout[:], op=ALU.add),
                     reads=[hkey(i), "ps:pout_0", "ps:pout_1"], writes=[hkey(i)])

        def record_a(i):
            P.start_record()
            phase_a(i)
            return P.stop_record()

        for it in record_a(0):
            if it[0] == "op":
                P.op(*it[1:])
        LAG = 2
        pend = []
        for i in range(NT):
            ops = record_a(i + 1) if i + 1 < NT else []
            skip = 0
            for t in range(128):
                u_stage(i, t)
                pend.append((i, t))
                if len(pend) > LAG:
                    v_mm(*pend.pop(0))
                if skip > 0:
                    skip -= 1
                    continue
                nd = no = 0
                while ops:
                    it = ops[0]
                    if it[0] == "gap":
                        ops.pop(0); skip = 2
                        break
                    if it[1] == "dve":
                        if nd >= 2:
                            break
                        nd += 1
                    else:
                        if no >= 16:
                            break
                        no += 1
                    ops.pop(0)
                    P.op(*it[1:])
            for it in ops:
                if it[0] == "op":
                    P.op(*it[1:])
        while pend:
            v_mm(*pend.pop(0))
        P.join()


def uv_convert_steps(P, u_dram, v_dram, uv_dram, uvkey, semi):
    steps = []
    for k in range(16):
        rs = slice(k * 1024, (k + 1) * 1024)
        steps.append(lambda rs=rs: P.dma("pool", lambda e: e.dma_start(out=uv_dram[rs, 0:1024], in_=u_dram[rs, :]), writes=[uvkey], semi=semi))
        steps.append(lambda rs=rs: P.dma("pool", lambda e: e.dma_start(out=uv_dram[rs, 1024:2048], in_=v_dram[rs, :]), writes=[uvkey], semi=semi))
    return steps


def emit_uv_convert(P, u_dram, v_dram, uv_dram, uvkey, semi):
    for f in uv_convert_steps(P, u_dram, v_dram, uv_dram, uvkey, semi):
        f()


def load_consts(P, nc, cd):
    idf = P.sb("idf", [128, 128], F32)
    idb = P.sb("idb", [128, 128], BF16)
    iota16 = P.sb("iota16", [128, 16], F32)
    P.dma("sp", lambda e: e.dma_start(out=idf[:], in_=cd["ident"]), writes=["idf"])
    P.dma("sp", lambda e: e.dma_start(out=iota16[:], in_=cd["iota16"]), writes=["iota16"])
    P.op("dve", lambda e: e.tensor_copy(out=idb[:], in_=idf[:]), reads=["idf"], writes=["idb"])
    c1 = P.sb("c1", [128, 1], F32)
    c128 = P.sb("c128", [128, 1], F32)
    P.op("pool", lambda e: e.memset(c1[:], 1.0), writes=["c1"])
    P.op("pool", lambda e: e.memset(c128[:], 128.0), writes=["c128"])
    return {"idf": idf, "idb": idb, "iota16": iota16, "c1": c1, "c128": c128}


def host_consts():
    return {"ident": np.eye(128, dtype=np.float32),
            "iota16": np.tile(np.arange(16, dtype=np.float32)[None, :], (128, 1))}


def emit_peer_block(P, C, D, fin=None):
    with ExitStack() as st:
        h = P.sb("h", [128, NT, 1024], F32, st)
        hkey = lambda i: "h%d" % i
        hv = D["h_in"].rearrange("(i p) d -> p i d", p=128)
        for i in range(NT):
            P.dma("sp", lambda e, i=i: e.dma_start(out=h[:, i, :], in_=hv[:, i, :]), writes=[hkey(i)])
        emit_peer(P, h, hkey, D, C)
        ho = D["h_out"].rearrange("(i p) d -> p i d", p=128)
        if fin is None:
            for i in range(NT):
                P.dma("sp", lambda e, i=i: e.dma_start(out=ho[:, i, :], in_=h[:, i, :]), reads=[hkey(i)])
        else:
            fnorm, y_out = fin
            gF = P.sb("fn_gF", [128, 1024], F32, st)
            jF = P.sb("fn_jF", [128, 1024], BF16, st)
            yt = [P.sb("fn_yt%d" % k, [128, 1024], F32, st) for k in range(2)]
            P.dma("sp", lambda e: e.dma_start(out=gF[:], in_=bcast_rows(fnorm, 1024)), writes=["fn_gF"])
            rstd2 = emit_rstd(P, st, "fn_", h, hkey, NT, jF)
            yo = y_out.rearrange("(i p) d -> p i d", p=128)
            for i in range(NT):
                k = i % 2
                P.op("dve", lambda e, i=i, k=k: e.scalar_tensor_tensor(out=yt[k][:], in0=h[:, i, :], scalar=rstd2[:, i:i + 1], in1=gF[:],
                                                                      op0=ALU.mult, op1=ALU.mult),
                     reads=[hkey(i), "fn_rstd", "fn_gF"], writes=["fn_yt%d" % k])
                P.dma("sp", lambda e, i=i, k=k: e.dma_start(out=yo[:, i, :], in_=yt[k][:]), reads=["fn_yt%d" % k])
        P.join()


def build_peer_nc(final=False):
    nc = bass.Bass("TRN2", target_bir_lowering=False)
    dr = lambda n, s, k="ExternalInput", dt=F32: nc.dram_tensor(n, list(s), dt, kind=k).ap()
    D = {"h_in": dr("h_in", [TOK, 1024]), "g": dr("g", [1024]), "wq": dr("wq", [1024, 2048]), "sk": dr("sk", [16, 128, 128]),
         "u": dr("u", [16384, 1024]), "v": dr("v", [16384, 1024])}
    cd = {"ident": dr("ident", [128, 128]), "iota16": dr("iota16", [128, 16])}
    D["h_out"] = dr("h_out", [TOK, 1024], "ExternalOutput")
    fin = (dr("fnorm", [1024]), dr("y_out", [TOK, 1024], "ExternalOutput")) if final else None
    D["uv"] = nc.dram_tensor("uv_i", [16384, 2048], BF16).ap()
    D["uvkey"] = "uvtab"
    with ExitStack() as st:
        P = Prog(nc, st)
        C = load_consts(P, nc, cd)
        emit_uv_convert(P, D["u"], D["v"], D["uv"], "uvtab", 50)
        emit_peer_block(P, C, D, fin)
        P.wait_all("sp")
        P.emit()
    return nc


def emit_xnT(P, st, pref, src, g_dram, xnT, col0, ntiles, C, pT):
    idb = C["idb"]
    gB = P.sb(pref + "gB", [128, 1024], F32, st)
    P.dma("sp", lambda e: e.dma_start(out=gB[:], in_=bcast_rows(g_dram, 1024)), writes=[pref + "gB"])
    NH = 4
    ht = [P.sb(pref + "ht%d" % k, [128, 1024], F32, st) for k in range(NH)]
    xt = [P.sb(pref + "xt%d" % k, [128, 1024], BF16, st) for k in range(2)]
    jk = P.sb(pref + "jk", [128, 1024], BF16, st)
    ss = P.sb(pref + "ss", [128, ntiles], F32, st)
    sq = P.sb(pref + "sq", [128, ntiles], F32, st)
    rs = P.sb(pref + "rs", [128, ntiles], F32, st)
    if callable(src):
        tile_ap = src
    else:
        sv = src.rearrange("(i p) d -> p i d", p=128)
        tile_ap = lambda i: sv[:, i, :]

    def load(i):
        P.dma("sp", lambda e: e.dma_start(out=ht[i % NH][:], in_=tile_ap(i)), writes=[pref + "ht%d" % (i % NH)])

    def stats(i):
        k = i % NH
        P.op("dve", lambda e: e.scalar_tensor_tensor(out=jk[:], in0=ht[k][:], scalar=C["c1"][:, 0:1], in1=ht[k][:], op0=ALU.mult, op1=ALU.mult,
                                                     accum_out=ss[:, i:i + 1]),
             reads=[pref + "ht%d" % k, "c1"], writes=[pref + "jk", pref + "ss%d" % i])
        P.op("act", lambda e: e.activation(out=sq[:, i:i + 1], in_=ss[:, i:i + 1], func=AF.Sqrt, scale=1.0 / 1024, bias=EPS),
             reads=[pref + "ss%d" % i], writes=[pref + "sq%d" % i])

    def evac(i):
        P.op("dve", lambda e: e.tensor_copy(out=xnT[:, :, col0 + i * 128:col0 + (i + 1) * 128], in_=pT[:]),
             reads=["ps:pT"], writes=[pref + "xnT"])

    for i in range(min(NH - 1, ntiles)):
        load(i)
    stats(0)
    for i in range(ntiles):
        k, kx = i % NH, i % 2
        if i + 1 < ntiles:
            stats(i + 1)
        P.op("dve", lambda e, i=i: e.reciprocal(out=rs[:, i:i + 1], in_=sq[:, i:i + 1]), reads=[pref + "sq%d" % i], writes=[pref + "rs%d" % i])
        P.op("dve", lambda e, i=i, k=k, kx=kx: e.scalar_tensor_tensor(out=xt[kx][:], in0=ht[k][:], scalar=rs[:, i:i + 1], in1=gB[:], op0=ALU.mult, op1=ALU.mult),
             reads=[pref + "ht%d" % k, pref + "rs%d" % i, pref + "gB"], writes=[pref + "xt%d" % kx])
        if i > 0:
            evac(i - 1)
        for c in range(8):
            P.op("pe", lambda e, c=c, kx=kx: e.transpose(out=pT[:, c, :], in_=xt[kx][:, c * 128:(c + 1) * 128], identity=idb[:]),
                 reads=[pref + "xt%d" % kx, "idb"], writes=["ps:pT"])
        if i + NH - 1 < ntiles:
            load(i + NH - 1)
    evac(ntiles - 1)


def emit_outproj(P, st, pref, catT, catkey, wout_dram, h_src, h_dst, po):
    wo = P.sb(pref + "wo", [128, 8, 1024], BF16, st)
    for c in range(8):
        P.dma("pool", lambda e, c=c: e.dma_start(out=wo[:, c, :], in_=wout_dram[c * 128:(c + 1) * 128, :]), writes=[pref + "wo"])
    NB = 4
    hb = [P.sb(pref + "hb%d" % k, [128, 1024], F32, st) for k in range(NB)]
    sv = h_src.rearrange("(i p) d -> p i d", p=128)
    dv = h_dst.rearrange("(i p) d -> p i d", p=128)

    def load(i):
        P.dma("sp", lambda e: e.dma_start(out=hb[i % NB][:], in_=sv[:, i, :]), writes=[pref + "hb%d" % (i % NB)])

    for i in range(NB - 1):
        load(i)
    for i in range(NT):
        k = i % NB
        for hf in range(2):
            for f in range(8):
                P.op("pe", lambda e, i=i, hf=hf, f=f: e.matmul(po[:, hf * 512:(hf + 1) * 512], lhsT=catT[:, f, i * 128:(i + 1) * 128],
                                                               rhs=wo[:, f, hf * 512:(hf + 1) * 512], start=(f == 0), stop=(f == 7)),
                     reads=[catkey, pref + "wo"], writes=["ps:po%d" % hf])
        P.op("dve", lambda e, k=k: e.tensor_tensor(out=hb[k][:], in0=hb[k][:], in1=po[:], op=ALU.add),
             reads=[pref + "hb%d" % k, "ps:po0", "ps:po1"], writes=[pref + "hb%d" % k])
        if i + NB - 1 < NT:
            load(i + NB - 1)
        P.dma("act", lambda e, i=i, k=k: e.dma_start(out=dv[:, i, :], in_=hb[k][:]), reads=[pref + "hb%d" % k])


def emit_even(P, C, D, bg=None):
    h_own, h_oth, g, w_in, w_out = D["h_own"], D["h_oth"], D["g"], D["w_in"], D["w_out"]
    cos2, sin2, gmask, cm, dpat, cp, h_out = D["cos2"], D["sin2"], D["gmask"], D["cm"], D["dpat"], D["cp"], D["h_out"]
    SC = 128.0 ** -0.5
    with ExitStack() as st:
        idf, idb = C["idf"], C["idb"]
        xnT = P.sb("xnT", [128, 8, 4096], BF16, st)
        catT = P.sb("catT", [128, 8, TOK], BF16, st)
        cosb = P.sb("cosb", [128, 4096], BF16, st); sinb = P.sb("sinb", [128, 4096], BF16, st)
        cmb = P.sb("cmb", [128, 2048], BF16, st)
        gm3 = P.sb("gm3", [128, 768], F32, st)
        onesb = P.sb("onesb", [128, 128], BF16, st)
        P.dma("pool", lambda e: e.dma_start(out=cosb[:], in_=cos2), writes=["cosb"])
        P.dma("pool", lambda e: e.dma_start(out=sinb[:], in_=sin2), writes=["sinb"])
        P.dma("pool", lambda e: e.dma_start(out=cmb[:], in_=cm), writes=["cmb"])
        P.dma("sp", lambda e: e.dma_start(out=gm3[:], in_=bcast_rows(gmask, 768)), writes=["gm3"])
        P.op("pool", lambda e: e.memset(onesb[:], 1.0), writes=["onesb"])
        pT = P.ps("pT", [128, 8, 128], BF16, st)
        with ExitStack() as s1:
            emit_xnT(P, s1, "no_", h_oth, g, xnT, 0, NT, C, pT)
            P.join()
        with ExitStack() as s1:
            emit_xnT(P, s1, "nw_", h_own, g, xnT, TOK, NT, C, pT)
            P.join()
        with ExitStack() as s2:
            pp = [P.ps("pp%d" % k, [128, 512], F32, s2) for k in range(2)]
            pst = [P.ps("pst%d" % k, [128, 512], F32, s2) for k in range(2)]
            pout = P.ps("pout", [128, 512], F32, s2)
            psum = P.ps("psum", [128, 512], F32, s2)
            pmisc = P.ps("pmisc", [128, 512], F32, s2)
            wh = P.sb("wh", [128, 8, 4, 128], BF16, s2)
            qT = P.sb("qT", [128, TOK], BF16, s2)
            kT = P.sb("kT", [128, 4096], BF16, s2)
            V = P.sb("V", [128, 32, 128], BF16, s2)
            sg = P.sb("sg", [128, TOK], F32, s2)
            tA = P.sb("tA", [128, 512], F32, s2)
            tB = P.sb("tB", [128, 512], F32, s2)
            PTb = [P.sb("PTb%d" % k, [128, 512], BF16, s2) for k in range(2)]
            dp = P.sb("dp", [128, 2560], F32, s2)
            cpb = P.sb("cpb", [128, 128], F32, s2)
            km = P.sb("km", [128, 16], F32, s2); kmb = P.sb("kmb", [128, 16], BF16, s2)
            gmv = P.sb("gmv", [128, 256], F32, s2); sel = P.sb("sel", [128, 256], F32, s2)
            m8 = P.sb("m8", [128, 16, 8], F32, s2)
            biasT = P.sb("biasT", [128, TOK], BF16, s2)
            P.op("pool", lambda e: e.memset(biasT[:], 0.0), writes=["biasT"])
            rsb = P.sb("rsb", [128, 512], F32, s2)
            sqb = P.sb("sqb", [128, 512], BF16, s2)
            ppi = [0]

            def proj_feat(which, dst, tok0, ntok, mode, dkey):
                for t0 in range(0, ntok, 512):
                    k = ppi[0] % 2; ppi[0] += 1
                    for c in range(8):
                        P.op("pe", lambda e, c=c, k=k, t0=t0: e.matmul(pp[k][:], lhsT=wh[:, c, which, :], rhs=xnT[:, c, tok0 + t0:tok0 + t0 + 512],
                                                                       start=(c == 0), stop=(c == 7)),
                             reads=["wh", "no_xnT", "nw_xnT"], writes=["ps:pp%d" % k])
                    if mode == "silu":
                        P.op("act", lambda e, k=k, t0=t0: e.activation(out=dst[:, t0:t0 + 512], in_=pp[k][:], func=AF.Silu), reads=["ps:pp%d" % k], writes=[dkey])
                        continue
                    l0 = tok0 + t0
                    P.op("dve", lambda e, k=k, l0=l0: e.tensor_tensor(out=tA[:], in0=pp[k][:], in1=cosb[:, l0:l0 + 512], op=ALU.mult),
                         reads=["ps:pp%d" % k, "cosb"], writes=["tA"])
                    P.op("dve", lambda e, k=k, l0=l0: e.tensor_tensor(out=tB[0:64, :], in0=pp[k][64:128, :], in1=sinb[64:128, l0:l0 + 512], op=ALU.mult),
                         reads=["ps:pp%d" % k, "sinb"], writes=["tB"])
                    P.op("dve", lambda e, k=k, l0=l0: e.tensor_tensor(out=tB[64:128, :], in0=pp[k][0:64, :], in1=sinb[0:64, l0:l0 + 512], op=ALU.mult),
                         reads=["ps:pp%d" % k, "sinb"], writes=["tB"])
                    P.op("dve", lambda e, t0=t0: e.tensor_tensor(out=dst[:, t0:t0 + 512], in0=tA[:], in1=tB[:], op=ALU.add), reads=["tA", "tB"], writes=[dkey])

            for head in range(8):
                moba = head < 4
                hh = head % 4
                base = 0 if moba else 1536
                cols = [base + hh * 128, base + 512 + hh * 128, base + 1024 + hh * 128] + ([] if moba else [3072 + hh * 128])
                for wi, c0 in enumerate(cols):
                    P.dma("pool", lambda e, wi=wi, c0=c0: e.dma_start(out=wh[:, :, wi, :], in_=w_in[:, c0:c0 + 128].rearrange("(c p) n -> p c n", p=128)),
                          writes=["wh"])
                if bg is not None:
                    bg()
                proj_feat(0, qT, TOK, TOK, "rope", "qT")
                proj_feat(1, kT, 0, 4096, "rope", "kT")
                if not moba:
                    proj_feat(3, sg, TOK, TOK, "silu", "sg")
                    P.dma("sp", lambda e, hh=hh: e.dma_start(out=dp[:], in_=dpat[hh]), writes=["dp"])
                    P.dma("sp", lambda e, hh=hh: e.dma_start(out=cpb[:], in_=bcast_rows(cp[hh], 128)), writes=["cpb"])
                for i4 in range(8):
                    k = ppi[0] % 2; ppi[0] += 1
                    for q4 in range(4):
                        i = i4 * 4 + q4
                        for c in range(8):
                            P.op("pe", lambda e, c=c, k=k, i=i, q4=q4: e.matmul(pp[k][:, q4 * 128:(q4 + 1) * 128], lhsT=xnT[:, c, i * 128:(i + 1) * 128],
                                                                               rhs=wh[:, c, 2, :], start=(c == 0), stop=(c == 7)),
                                 reads=["wh", "no_xnT", "nw_xnT"], writes=["ps:pp%d" % k])
                    P.op("act", lambda e, k=k, i4=i4: e.activation(out=V[:, i4 * 4:(i4 + 1) * 4, :], in_=pp[k][:], func=AF.Copy), reads=["ps:pp%d" % k], writes=["V"])
                if moba:
                    P.op("dve", lambda e: e.tensor_reduce(out=km[:], in_=kT[:].rearrange("p (n k) -> p n k", k=256), axis=AX.X, op=ALU.add), reads=["kT"], writes=["km"])
                    P.op("dve", lambda e: e.tensor_copy(out=kmb[:], in_=km[:]), reads=["km"], writes=["kmb"])
                    for j in range(16):
                        P.op("pe", lambda e, j=j: e.matmul(pmisc[:, j * 16:(j + 1) * 16], lhsT=qT[:, j * 128:(j + 1) * 128], rhs=kmb[:], start=True, stop=True),
                             reads=["qT", "kmb"], writes=["ps:pmisc"])
                    P.op("dve", lambda e: e.tensor_tensor(out=gmv[:], in0=pmisc[:, 0:256], in1=gm3[:, 0:256], op=ALU.add), reads=["ps:pmisc", "gm3"], writes=["gmv"])
                    for j in range(16):
                        P.op("dve", lambda e, j=j: e.max(out=m8[:, j, :], in_=gmv[:, j * 16:(j + 1) * 16]), reads=["gmv"], writes=["m8_%d" % j])
                        P.op("dve", lambda e, j=j: e.tensor_scalar(out=sel[:, j * 16:(j + 1) * 16], in0=gmv[:, j * 16:(j + 1) * 16], scalar1=m8[:, j, 2:3], scalar2=None,
                                                                   op0=ALU.is_ge), reads=["gmv", "m8_%d" % j], writes=["sel"])
                    P.op("dve", lambda e: e.tensor_tensor(out=sel[:], in0=sel[:], in1=gm3[:, 256:512], op=ALU.mult), reads=["sel", "gm3"], writes=["sel"])
                    P.op("dve", lambda e: e.tensor_tensor(out=sel[:], in0=sel[:], in1=gm3[:, 512:768], op=ALU.add), reads=["sel", "gm3"], writes=["sel"])
                    P.op("dve", lambda e: e.tensor_scalar(out=sel[:], in0=sel[:], scalar1=-1.0, scalar2=30000.0, op0=ALU.add, op1=ALU.mult), reads=["sel"], writes=["sel"])
                    for j4 in range(4):
                        for q4 in range(4):
                            j = j4 * 4 + q4
                            P.op("pe", lambda e, j=j, q4=q4: e.transpose(out=pmisc[0:16, q4 * 128:(q4 + 1) * 128], in_=sel[:, j * 16:(j + 1) * 16], identity=idf[:]),
                                 reads=["sel", "idf"], writes=["ps:pmisc"])
                        P.op("act", lambda e, j4=j4: e.activation(out=biasT[0:16, j4 * 512:(j4 + 1) * 512], in_=pmisc[0:16, :], func=AF.Copy), reads=["ps:pmisc"], writes=["biasT"])
                units = []
                for c in range(4):
                    ktiles = list(range(16)) + [16 + k for k in range(4 * c + 4)]
                    for idx, kt in enumerate(ktiles):
                        units.append((c, kt, idx == 0, idx == len(ktiles) - 1))

                def s1(u):
                    c, kt, first, last = units[u]
                    k = u % 2
                    qs = slice(c * 512, (c + 1) * 512)
                    diag = kt >= 16 + 4 * c
                    r4 = kt - 16 - 4 * c
                    P.op("pe", lambda e: e.matmul(pst[k][:], lhsT=kT[:, kt * 128:(kt + 1) * 128], rhs=qT[:, qs], start=True, stop=(not moba)),
                         reads=["kT", "qT"], writes=["ps:pst%d" % k])
                    if moba:
                        n = kt // 2
                        P.op("pe", lambda e: e.matmul(pst[k][:], lhsT=cust(idb[:, n:n + 1], [[0, 128]]), rhs=biasT[:, qs], start=False, stop=(not diag)),
                             reads=["idb", "biasT"], writes=["ps:pst%d" % k])
                        if diag:
                            P.op("pe", lambda e: e.matmul(pst[k][:], lhsT=idb[:], rhs=cmb[:, r4 * 512:(r4 + 1) * 512], start=False, stop=True),
                                 reads=["idb", "cmb"], writes=["ps:pst%d" % k])

                def s2(u):
                    c, kt, first, last = units[u]
                    k = u % 2
                    diag = kt >= 16 + 4 * c
                    r4 = kt - 16 - 4 * c
                    if moba:
                        P.op("act", lambda e: e.activation(out=PTb[k][:], in_=pst[k][:], func=AF.Exp, scale=SC), reads=["ps:pst%d" % k], writes=["PTb%d" % k])
                    else:
                        var = (r4 + 1) if diag else 0
                        ci = c * 32 + kt
                        P.op("dve", lambda e: e.scalar_tensor_tensor(out=PTb[k][:], in0=pst[k][:], scalar=cpb[:, ci:ci + 1],
                                                                     in1=dp[:, var * 512:(var + 1) * 512], op0=ALU.mult, op1=ALU.mult),
                             reads=["ps:pst%d" % k, "cpb", "dp"], writes=["PTb%d" % k])

                def s3(u):
                    c, kt, first, last = units[u]
                    k = u % 2
                    qs = slice(c * 512, (c + 1) * 512)
                    P.op("pe", lambda e: e.matmul(pout[:], lhsT=V[:, kt, :], rhs=PTb[k][:], start=first, stop=last),
                         reads=["V", "PTb%d" % k], writes=["ps:pout"])
                    if moba:
                        P.op("pe", lambda e: e.matmul(psum[:], lhsT=onesb[:], rhs=PTb[k][:], start=first, stop=last),
                             reads=["onesb", "PTb%d" % k], writes=["ps:psum"])
                    if not last:
                        return
                    dst = catT[:, head, qs]
                    if moba:
                        P.op("dve", lambda e: e.reciprocal(out=rsb[:], in_=psum[:]), reads=["ps:psum"], writes=["rsb"])
                        P.op("dve", lambda e: e.tensor_tensor(out=dst, in0=pout[:], in1=rsb[:], op=ALU.mult), reads=["ps:pout", "rsb"], writes=["catT"])
                    else:
                        P.op("act", lambda e: e.activation(out=sqb[:], in_=pout[:], func=AF.Square), reads=["ps:pout"], writes=["sqb"])
                        P.op("pe", lambda e: e.matmul(psum[:], lhsT=onesb[:], rhs=sqb[:], start=True, stop=True), reads=["onesb", "sqb"], writes=["ps:psum"])
                        P.op("act", lambda e: e.activation(out=rsb[:], in_=psum[:], func=AF.Sqrt, scale=1.0 / 128, bias=EPS), reads=["ps:psum"], writes=["rsb"])
                        P.op("dve", lambda e: e.reciprocal(out=tA[:], in_=rsb[:]), reads=["rsb"], writes=["tA"])
                        P.op("dve", lambda e: e.tensor_tensor(out=tB[:], in0=pout[:], in1=tA[:], op=ALU.mult), reads=["ps:pout", "tA"], writes=["tB"])
                        P.op("dve", lambda e: e.tensor_tensor(out=dst, in0=tB[:], in1=sg[:, qs], op=ALU.mult), reads=["tB", "sg"], writes=["catT"])

                s1(0)
                for u in range(len(units)):
                    if u + 1 < len(units):
                        s1(u + 1)
                    s2(u)
                    s3(u)
            P.join()
        with ExitStack() as s3:
            po = P.ps("po", [128, 1024], F32, s3)
            emit_outproj(P, s3, "op_", catT, "catT", w_out, h_own, h_out, po)
            P.join()


def build_even_nc():
    nc = bass.Bass("TRN2", target_bir_lowering=False)
    dr = lambda n, s, k="ExternalInput", dt=F32: nc.dram_tensor(n, list(s), dt, kind=k).ap()
    D = {"h_own": dr("h_own", [TOK, 1024]), "h_oth": dr("h_oth", [TOK, 1024]), "g": dr("g", [1024]), "w_in": dr("w_in", [1024, 3584]),
         "w_out": dr("w_out", [1024, 1024]), "cos2": dr("cos2", [128, 4096]), "sin2": dr("sin2", [128, 4096]), "gmask": dr("gmask", [768]),
         "cm": dr("cm", [128, 2048]), "dpat": dr("dpat", [4, 128, 2560]), "cp": dr("cp", [4, 128])}
    cd = {"ident": dr("ident", [128, 128]), "iota16": dr("iota16", [128, 16])}
    D["h_out"] = dr("h_out", [TOK, 1024], "ExternalOutput")
    with ExitStack() as st:
        P = Prog(nc, st)
        C = load_consts(P, nc, cd)
        emit_even(P, C, D)
        P.wait_all("sp")
        P.emit()
    return nc


def even_consts(half):
    inv = (10000.0 ** (-np.arange(64, dtype=np.float32) / 64)).astype(np.float32)
    pos = np.concatenate([np.arange(2048) + (1 - half) * 2048, np.arange(2048) + half * 2048]).astype(np.float32)
    ang = (pos[None, :] * np.tile(inv, 2)[:, None]).astype(np.float32)
    cos2 = np.cos(ang).astype(np.float32)
    sin2 = np.sin(ang).astype(np.float32)
    sin2[64:] = -sin2[64:]
    past = np.zeros((16, 16), np.float32); own = np.zeros((16, 16), np.float32)
    for j in range(16):
        qb = j // 2
        for n in range(16):
            if n < 8:
                past[j, n] = 1.0 if half == 1 else 0.0
            else:
                past[j, n] = 1.0 if n - 8 < qb else 0.0
                own[j, n] = 1.0 if n - 8 == qb else 0.0
    gmask = np.concatenate([np.where(past > 0, 0.0, -1e30).ravel(), past.ravel(), own.ravel()]).astype(np.float32)
    p = np.arange(128)[:, None]; q = np.arange(512)[None, :]
    cm = np.zeros((128, 4, 512), np.float32)
    for r4 in range(4):
        same = (q // 256) == (r4 // 2)
        cm[:, r4, :] = np.where(same & (r4 * 128 + p > q), -30000.0, 0.0)
    SC = 128.0 ** -0.5
    dpat = np.zeros((4, 128, 5, 512), np.float64); cp = np.zeros((4, 128), np.float64)
    for hh in range(4):
        gam = 1.0 - 2.0 ** (-5.0 - hh)
        dpat[hh, :, 0, :] = gam ** (q - p).astype(np.float64)
        for r4 in range(4):
            e = (q - r4 * 128 - p).astype(np.float64)
            dpat[hh, :, r4 + 1, :] = np.where(e >= 0, gam ** np.maximum(e, 0), 0.0)
        for c in range(4):
            for kt in range(32):
                if kt < 16:
                    v = gam ** float(2048 + c * 512 - kt * 128) * SC if half == 1 else 0.0
                else:
                    ko = kt - 16
                    v = gam ** float(c * 512 - ko * 128) * SC if ko * 128 < c * 512 else (SC if ko < 4 * c + 4 else 0.0)
                cp[hh, c * 32 + kt] = v
    return {"cos2": cos2, "sin2": sin2, "gmask": gmask, "cm": cm.reshape(128, 2048),
            "dpat": dpat.reshape(4, 128, 2560).astype(np.float32), "cp": cp.astype(np.float32)}


def emit_odd(P, C, D, bg=None):
    h_own, h_prev, g, w_in, w_out, convw, flag, h_out = (D[k] for k in ("h_own", "h_prev", "g", "w_in", "w_out", "convw", "flag", "h_out"))
    with ExitStack() as st:
        idf = C["idf"]
        xnT = P.sb("xnT", [128, 8, 128 + TOK], BF16, st)
        zT = P.sb("zT", [128, 8, TOK], BF16, st)
        cw = P.sb("cw", [128, 24], F32, st)
        cws = P.sb("cws", [24, 128], F32, st)
        fl = P.sb("fl", [128, 1], F32, st)
        P.dma("sp", lambda e: e.dma_start(out=cws[:], in_=convw), writes=["cws"])
        P.dma("sp", lambda e: e.dma_start(out=fl[:], in_=flag), writes=["fl"])
        pT = P.ps("pT", [128, 8, 128], BF16, st)
        with ExitStack() as s1:
            emit_xnT(P, s1, "no_", h_prev, g, xnT, 0, 1, C, pT)
            P.join()
        with ExitStack() as s1:
            emit_xnT(P, s1, "nw_", h_own, g, xnT, 128, NT, C, pT)
            P.join()
        with ExitStack() as s2:
            pb = [P.ps("pb%d" % k, [128, 512], F32, s2) for k in range(2)]
            pc = [P.ps("pc%d" % k, [128, 512], F32, s2) for k in range(2)]
            ph = [P.ps("ph%d" % k, [128, 512], F32, s2) for k in range(2)]
            P.op("pe", lambda e: e.transpose(out=pb[0][:, 0:24], in_=cws[:], identity=idf[0:24, 0:24]), reads=["cws", "idf"], writes=["ps:pb0"])
            P.op("dve", lambda e: e.tensor_copy(out=cw[:], in_=pb[0][:, 0:24]), reads=["ps:pb0"], writes=["cw"])
            whs = [P.sb("wh%d" % k, [128, 8, 3, 128], BF16, s2) for k in range(2)]
            uT = P.sb("uT", [128, 2 + TOK], F32, s2)
            cgs = P.sb("cgs", [128, 512], F32, s2)
            yb = P.sb("yb", [128, 512], F32, s2)
            ui = 0
            def load_w(f):
                for gi in range(3):
                    c0 = gi * 1024 + f * 128
                    P.dma("pool", lambda e, gi=gi, c0=c0: e.dma_start(out=whs[f % 2][:, :, gi, :], in_=w_in[:, c0:c0 + 128].rearrange("(c p) n -> p c n", p=128)),
                          writes=["wh%d" % (f % 2)])

            load_w(0)
            for f in range(8):
                wh, whk = whs[f % 2], "wh%d" % (f % 2)
                if f + 1 < 8:
                    load_w(f + 1)
                if bg is not None:
                    bg()
                k = ui % 2; ui += 1
                for c in range(8):
                    P.op("pe", lambda e, c=c, k=k, wh=wh: e.matmul(pc[k][:, 0:128], lhsT=wh[:, c, 1, :], rhs=xnT[:, c, 0:128], start=(c == 0), stop=(c == 7)),
                         reads=[whk, "no_xnT"], writes=["ps:pc%d" % k])
                for c in range(8):
                    P.op("pe", lambda e, c=c, k=k, wh=wh: e.matmul(ph[k][:, 0:128], lhsT=wh[:, c, 2, :], rhs=xnT[:, c, 0:128], start=(c == 0), stop=(c == 7)),
                         reads=[whk, "no_xnT"], writes=["ps:ph%d" % k])
                P.op("act", lambda e, k=k: e.activation(out=cgs[:, 0:128], in_=pc[k][:, 0:128], func=AF.Copy), reads=["ps:pc%d" % k], writes=["cgs"])
                P.op("dve", lambda e, k=k: e.tensor_tensor(out=yb[:, 0:128], in0=ph[k][:, 0:128], in1=cgs[:, 0:128], op=ALU.mult), reads=["ps:ph%d" % k, "cgs"], writes=["yb"])
                P.op("dve", lambda e: e.tensor_scalar(out=uT[:, 0:2], in0=yb[:, 126:128], scalar1=fl[:, 0:1], scalar2=None, op0=ALU.mult), reads=["yb", "fl"], writes=["uT"])
                for t0 in range(0, TOK, 512):
                    k = ui % 2; ui += 1
                    for (gi, pz, nm) in ((0, pb, "pb"), (1, pc, "pc"), (2, ph, "ph")):
                        for c in range(8):
                            P.op("pe", lambda e, c=c, k=k, gi=gi, pz=pz, t0=t0, wh=wh: e.matmul(pz[k][:], lhsT=wh[:, c, gi, :], rhs=xnT[:, c, 128 + t0:128 + t0 + 512],
                                                                                        start=(c == 0), stop=(c == 7)),
                                 reads=[whk, "nw_xnT"], writes=["ps:%s%d" % (nm, k)])
                    P.op("act", lambda e, k=k: e.activation(out=cgs[:], in_=pc[k][:], func=AF.Copy), reads=["ps:pc%d" % k], writes=["cgs"])
                    P.op("dve", lambda e, k=k, t0=t0: e.tensor_tensor(out=uT[:, 2 + t0:2 + t0 + 512], in0=ph[k][:], in1=cgs[:], op=ALU.mult),
                         reads=["ps:ph%d" % k, "cgs"], writes=["uT"])
                    P.op("dve", lambda e, f=f, t0=t0: e.tensor_scalar(out=yb[:], in0=uT[:, t0:t0 + 512], scalar1=cw[:, f:f + 1], scalar2=None, op0=ALU.mult),
                         reads=["uT", "cw"], writes=["yb"])
                    for kk in (1, 2):
                        P.op("dve", lambda e, f=f, t0=t0, kk=kk: e.scalar_tensor_tensor(out=yb[:], in0=uT[:, t0 + kk:t0 + kk + 512], scalar=cw[:, kk * 8 + f:kk * 8 + f + 1],
                                                                                        in1=yb[:], op0=ALU.mult, op1=ALU.add),
                             reads=["uT", "cw", "yb"], writes=["yb"])
                    P.op("dve", lambda e, f=f, k=k, t0=t0: e.tensor_tensor(out=zT[:, f, t0:t0 + 512], in0=yb[:], in1=pb[k][:], op=ALU.mult),
                         reads=["yb", "ps:pb%d" % k], writes=["zT"])
            P.join()
        with ExitStack() as s3:
            po = P.ps("po", [128, 1024], F32, s3)
            emit_outproj(P, s3, "op_", zT, "zT", w_out, h_own, h_out, po)
            P.join()


def build_odd_nc():
    nc = bass.Bass("TRN2", target_bir_lowering=False)
    dr = lambda n, s, k="ExternalInput", dt=F32: nc.dram_tensor(n, list(s), dt, kind=k).ap()
    D = {"h_own": dr("h_own", [TOK, 1024]), "h_prev": dr("h_prev", [128, 1024]), "g": dr("g", [1024]), "w_in": dr("w_in", [1024, 3072]),
         "w_out": dr("w_out", [1024, 1024]), "convw": dr("convw", [24, 128]), "flag": dr("flag", [128, 1])}
    cd = {"ident": dr("ident", [128, 128]), "iota16": dr("iota16", [128, 16])}
    D["h_out"] = dr("h_out", [TOK, 1024], "ExternalOutput")
    with ExitStack() as st:
        P = Prog(nc, st)
        C = load_consts(P, nc, cd)
        emit_odd(P, C, D)
        P.wait_all("sp")
        P.emit()
    return nc


PAIRS = [[0, 1], [2, 3], [4, 5], [6, 7]]


def build_fused_nc(nlayers=4, do_cc=True, dbg=False):
    nc = bass.Bass("TRN2", target_bir_lowering=False)
    dr = lambda n, s, k="ExternalInput", dt=F32: nc.dram_tensor(n, list(s), dt, kind=k).ap()
    x_own = dr("x_own", [TOK, 1024]); x_first = dr("x_first", [TOK, 1024])
    gmix = [dr("gmix%d" % l, [1024]) for l in range(4)]
    gffn = [dr("gffn%d" % l, [1024]) for l in range(4)]
    ewin = [dr("ewin%d" % i, [1024, 3584]) for i in range(2)]
    ewout = [dr("ewout%d" % i, [1024, 1024]) for i in range(2)]
    owin = [dr("owin%d" % i, [1024, 3072]) for i in range(2)]
    oconv = [dr("oconv%d" % i, [24, 128]) for i in range(2)]
    owout = [dr("owout%d" % i, [1024, 1024]) for i in range(2)]
    wq = [dr("wq%d" % l, [1024, 2048]) for l in range(nlayers)]
    sk = [dr("sk%d" % l, [16, 128, 128]) for l in range(nlayers)]
    pu = [dr("pu%d" % l, [16384, 1024]) for l in range(nlayers)]
    pv = [dr("pv%d" % l, [16384, 1024]) for l in range(nlayers)]
    fnorm = dr("fnorm", [1024])
    ec = {"cos2": dr("cos2", [128, 4096]), "sin2": dr("sin2", [128, 4096]), "gmask": dr("gmask", [768]),
          "cm": dr("cm", [128, 2048]), "dpat": dr("dpat", [4, 128, 2560]), "cp": dr("cp", [4, 128])}
    flag = dr("flag", [128, 1])
    cd = {"ident": dr("ident", [128, 128]), "iota16": dr("iota16", [128, 16])}
    y_out = dr("y_out", [TOK, 1024], "ExternalOutput")
    hA = nc.dram_tensor("hA_i", [TOK, 1024], F32)
    hB = nc.dram_tensor("hB_i", [TOK, 1024], F32)
    hG = nc.dram_tensor("hG_i", [8, 512, 1024], F32)
    uvt = [nc.dram_tensor("uv%d_i" % k, [16384, 2048], BF16).ap() for k in range(2)]
    with ExitStack() as st:
        P = Prog(nc, st)
        C = load_consts(P, nc, cd)
        P.lazy = {P.n_shared + 50, P.n_shared + 51}
        conv = lambda l: uv_convert_steps(P, pu[l], pv[l], uvt[l % 2], "uvtab%d" % (l % 2), 50 + (l % 2))
        for layer in range(nlayers):
            i = layer // 2
            todo = []
            if layer % 2 == 0:
                todo = conv(layer) + (conv(layer + 1) if layer + 1 < nlayers else [])
            per = -(-len(todo) // 8)

            def bg(todo=todo, per=per):
                for _ in range(per):
                    if todo:
                        todo.pop(0)()
            h_src = x_own if layer == 0 else hB.ap()
            if layer % 2 == 0:
                D = dict(ec)
                D.update(h_own=h_src, h_oth=(x_first if layer == 0 else (lambda t: hG.ap()[t // 2, (t % 2) * 128:(t % 2 + 1) * 128, :])), g=gmix[layer], w_in=ewin[i], w_out=ewout[i], h_out=hA.ap())
                emit_even(P, C, D, bg)
            else:
                D = dict(h_own=h_src, h_prev=hG.ap()[7, 128:256, :], g=gmix[layer], w_in=owin[i], w_out=owout[i], convw=oconv[i], flag=flag, h_out=hA.ap())
                emit_odd(P, C, D, bg)
            while todo:
                todo.pop(0)()
            Dp = dict(h_in=hA.ap(), h_out=hB.ap(), g=gffn[layer], wq=wq[layer], sk=sk[layer], uv=uvt[layer % 2], uvkey="uvtab%d" % (layer % 2))
            emit_peer_block(P, C, Dp, fin=((fnorm, y_out) if layer == nlayers - 1 else None))
            if layer < 3 and do_cc and (layer < nlayers - 1 or dbg):
                for k in (range(8) if layer == 1 else [7]):
                    P.cc(lambda e, k=k: e.collective_compute("AllGather", ALU.bypass, replica_groups=PAIRS,
                                                             ins=[hB.ap()[k * 256:(k + 1) * 256, :].opt()], outs=[hG.ap()[k].opt()]), semi=48)
                P.join()
        P.wait_all("sp")
        P.emit()
    return nc

_NC_CACHE = {}


def _get_nc(kind):
    if kind not in _NC_CACHE:
        _NC_CACHE[kind] = {"even": build_even_nc, "odd": build_odd_nc, "peer": build_peer_nc,
                           "peer_final": lambda: build_peer_nc(final=True), "fused": build_fused_nc}[kind]()
    return _NC_CACHE[kind]


def kernel(x, norm_mix, norm_ffn, even_w_in, even_w_out, odd_w_in, odd_conv, odd_w_out,
           peer_w_q, peer_sub_keys, peer_u, peer_v, final_norm):
    from concourse.bass_utils import run_bass_kernel_spmd
    f32 = lambda a: np.ascontiguousarray(np.asarray(a, dtype=np.float32))
    x = f32(x)
    cst = host_consts()
    econ = [even_consts(0), even_consts(1)]
    shared = {"fnorm": f32(final_norm)}
    for l in range(4):
        shared["gmix%d" % l] = f32(norm_mix[l]); shared["gffn%d" % l] = f32(norm_ffn[l])
        shared["wq%d" % l] = f32(peer_w_q[l]); shared["sk%d" % l] = f32(np.asarray(peer_sub_keys[l]).reshape(16, 128, 128))
        shared["pu%d" % l] = f32(peer_u[l]); shared["pv%d" % l] = f32(peer_v[l])
    for i in range(2):
        shared["ewin%d" % i] = f32(even_w_in[i]); shared["ewout%d" % i] = f32(even_w_out[i])
        shared["owin%d" % i] = f32(odd_w_in[i]); shared["owout%d" % i] = f32(odd_w_out[i])
        shared["oconv%d" % i] = f32(np.asarray(odd_conv[i]).reshape(24, 128))
    maps = []
    for c in range(8):
        b, half = c // 2, c % 2
        m = dict(shared)
        m.update(cst); m.update(econ[half])
        m["x_own"] = f32(x[b, half * 2048:(half + 1) * 2048]); m["x_first"] = f32(x[b, 0:2048])
        m["flag"] = np.full((128, 1), float(half), np.float32)
        maps.append(m)
    res = run_bass_kernel_spmd(_get_nc("fused"), maps, core_ids=list(range(8)))
    y = np.zeros((4, 4096, 1024), np.float32)
    for c in range(8):
        y[c // 2, (c % 2) * 2048:(c % 2 + 1) * 2048] = res.results[c]["y_out"]
    return y


def kernel_unfused(x, norm_mix, norm_ffn, even_w_in, even_w_out, odd_w_in, odd_conv, odd_w_out,
           peer_w_q, peer_sub_keys, peer_u, peer_v, final_norm):
    from concourse.bass_utils import run_bass_kernel_spmd
    f32 = lambda a: np.ascontiguousarray(np.asarray(a, dtype=np.float32))
    h = f32(x).copy()
    cst = host_consts()
    econ = [even_consts(0), even_consts(1)]
    cores = list(range(8))
    y = None
    for layer in range(4):
        i = layer // 2
        maps = []
        for c in cores:
            b, half = c // 2, c % 2
            own = f32(h[b, half * 2048:(half + 1) * 2048])
            if layer % 2 == 0:
                maps.append(dict(h_own=own, h_oth=f32(h[b, (1 - half) * 2048:(2 - half) * 2048]), g=f32(norm_mix[layer]),
                                 w_in=f32(even_w_in[i]), w_out=f32(even_w_out[i]), **cst, **econ[half]))
            else:
                prev = h[b, 1920:2048] if half == 1 else h[b, 0:128]
                maps.append(dict(h_own=own, h_prev=f32(prev), flag=np.full((128, 1), float(half), np.float32), g=f32(norm_mix[layer]),
                                 w_in=f32(odd_w_in[i]), w_out=f32(odd_w_out[i]), convw=f32(np.asarray(odd_conv[i]).reshape(24, 128)), **cst))
        res = run_bass_kernel_spmd(_get_nc("even" if layer % 2 == 0 else "odd"), maps, core_ids=cores)
        for c in cores:
            h[c // 2, (c % 2) * 2048:(c % 2 + 1) * 2048] = res.results[c]["h_out"]
        last = layer == 3
        maps = []
        for c in cores:
            b, half = c // 2, c % 2
            m = dict(h_in=f32(h[b, half * 2048:(half + 1) * 2048]), g=f32(norm_ffn[layer]), wq=f32(peer_w_q[layer]),
                     sk=f32(np.asarray(peer_sub_keys[layer]).reshape(16, 128, 128)), u=f32(peer_u[layer]), v=f32(peer_v[layer]), **cst)
            if last:
                m["fnorm"] = f32(final_norm)
            maps.append(m)
        res = run_bass_kernel_spmd(_get_nc("peer_final" if last else "peer"), maps, core_ids=cores)
        for c in cores:
            h[c // 2, (c % 2) * 2048:(c % 2 + 1) * 2048] = res.results[c]["h_out"]
        if last:
            y = np.zeros_like(h)
            for c in cores:
                y[c // 2, (c % 2) * 2048:(c % 2 + 1) * 2048] = res.results[c]["y_out"]
    return y.astype(np.float32)
```

```python
from contextlib import ExitStack

import numpy as np
import concourse.bass as bass
import concourse.mybir as mybir

F32 = mybir.dt.float32
BF16 = mybir.dt.bfloat16
I32 = mybir.dt.int32
U32 = mybir.dt.uint32
AF = mybir.ActivationFunctionType
ALU = mybir.AluOpType
AX = mybir.AxisListType

ENGS = ("pe", "act", "dve", "pool", "sp")
STRICT = False


class Prog:
    def __init__(self, nc, stack, n_dma_sems=64, n_shared=12):
        self.n_shared = n_shared
        self.nc = nc
        self.stack = stack
        self.eng = {"pe": nc.tensor, "act": nc.scalar, "dve": nc.vector, "pool": nc.gpsimd, "sp": nc.sync}
        self.sem = {e: stack.enter_context(nc.semaphore("s_" + e)) for e in ENGS}
        self.cnt = {e: 0 for e in ENGS}
        self.lists = {e: [] for e in ENGS}
        self.waited = {e: {} for e in ENGS}
        self.last_w = {}
        self.readers = {}
        self.dma_sems = [stack.enter_context(nc.semaphore("s_dma%d" % i)) for i in range(n_dma_sems)]
        self.dma_cnt = [0] * n_dma_sems
        self.dma_rr = 0
        self.dma_rr_pool = 0
        self.uid = 0
        self.lazy = set()
        self._rec = None
        self.semname = {}
        for e in ENGS:
            self.semname[id(self.sem[e])] = e

    def sb(self, name, shape, dt, st=None):
        self.uid += 1
        return (st or self.stack).enter_context(self.nc.sbuf_tensor("sb_%s_%d" % (name, self.uid), list(shape), dt))

    def ps(self, name, shape, dt=F32, st=None):
        self.uid += 1
        return (st or self.stack).enter_context(self.nc.psum_tensor("ps_%s_%d" % (name, self.uid), list(shape), dt))

    def cc(self, fn, semi):
        waits = self._deps("pool", [], [])
        semi = self.n_shared + semi
        self.dma_cnt[semi] += 1
        self.lists["pool"].append((waits, fn, self.dma_sems[semi], None))

    def _deps(self, e, reads, writes):
        toks = []
        assert all(not r.startswith("ps:") for r in reads)
        for r in reads:
            t = self.last_w.get(r)
            if t is not None:
                toks.append((t, "raw"))
        for w in writes:
            t = self.last_w.get(w)
            if t is not None:
                toks.append((t, "waw"))
            for t in self.readers.get(w, ()):
                toks.append((t, "war"))
        waits = {}
        for (semkey, sem, val), kind in toks:
            if isinstance(semkey, int) and semkey < self.n_shared:
                val = max(val, self.dma_cnt[semkey])
            if semkey == e and kind != "raw" and not STRICT:
                continue
            if self.waited[e].get(semkey, 0) >= val:
                continue
            if waits.get(semkey, (None, 0))[1] < val:
                waits[semkey] = (sem, val)
        for semkey, (sem, val) in waits.items():
            self.waited[e][semkey] = val
        return list(waits.values())

    def _commit(self, tok, reads, writes):
        for w in writes:
            self.last_w[w] = tok
            self.readers[w] = []
        for r in reads:
            self.readers.setdefault(r, []).append(tok)

    def start_record(self):
        self._rec = []

    def stop_record(self):
        r, self._rec = self._rec, None
        return r

    def gap(self):
        if self._rec is not None:
            self._rec.append(("gap",))

    def op(self, e, fn, reads=(), writes=()):
        if self._rec is not None:
            self._rec.append(("op", e, fn, list(reads), list(writes)))
            return None
        writes = list(writes) + [r for r in reads if r.startswith("ps:")]
        reads = [r for r in reads if not r.startswith("ps:")]
        waits = self._deps(e, reads, writes)
        self.cnt[e] += 1
        tok = (e, self.sem[e], self.cnt[e])
        self.lists[e].append((waits, fn, self.sem[e], 1))
        self._commit(tok, reads, writes)
        return tok

    def dma(self, e, fn, reads=(), writes=(), semi=None):
        waits = self._deps(e, reads, writes)
        if semi is None:
            if e == "pool":
                semi = 8 + self.dma_rr_pool
                self.dma_rr_pool = (self.dma_rr_pool + 1) % (self.n_shared - 8)
            else:
                semi = self.dma_rr
                self.dma_rr = (self.dma_rr + 1) % 8
        else:
            semi = self.n_shared + semi
        self.dma_cnt[semi] += 16
        tok = (semi, self.dma_sems[semi], self.dma_cnt[semi])
        self.lists[e].append((waits, fn, self.dma_sems[semi], 16))
        self._commit(tok, reads, writes)
        return tok

    def join(self, engs=ENGS):
        for e in engs:
            waits = []
            for ee in ENGS:
                if ee != e and self.cnt[ee] > 0 and self.waited[e].get(ee, 0) < self.cnt[ee]:
                    waits.append((self.sem[ee], self.cnt[ee]))
                    self.waited[e][ee] = self.cnt[ee]
            for i, s in enumerate(self.dma_sems):
                if i in self.lazy:
                    continue
                if self.dma_cnt[i] > 0 and self.waited[e].get(i, 0) < self.dma_cnt[i]:
                    waits.append((s, self.dma_cnt[i]))
                    self.waited[e][i] = self.dma_cnt[i]
            if waits:
                self.lists[e].append((waits, None, None, 0))

    def wait_all(self, e):
        waits = []
        for ee in ENGS:
            if self.cnt[ee] > 0 and self.waited[e].get(ee, 0) < self.cnt[ee]:
                waits.append((self.sem[ee], self.cnt[ee]))
        for i, s in enumerate(self.dma_sems):
            if self.dma_cnt[i] > 0:
                waits.append((s, self.dma_cnt[i]))
        self.lists[e].append((waits, None, None, 0))

    def emit(self):
        nc = self.nc
        with nc.Block() as block:
            def run(e):
                def body(engine):
                    for waits, fn, sem, inc in self.lists[e]:
                        for s, v in waits:
                            engine.wait_ge(s, v)
                        if fn is not None:
                            ins = fn(engine)
                            if inc is None:
                                ins.then_inc(sem)
                            else:
                                ins.then_inc(sem, inc)
                return body
            block.sync(run("sp"))
            block.scalar(run("act"))
            block.vector(run("dve"))
            block.gpsimd(run("pool"))
            block.tensor(run("pe"))


EPS = 1e-6
DBG_STAGE = 99
DBG_SUB = 99
NT = 16
TOK = NT * 128


def cust(ap, dims):
    return bass.AP(ap.tensor, ap.offset, [list(ap.ap[0])] + [list(d) for d in dims])


def bcast_rows(dram_ap, n):
    return bass.AP(dram_ap.tensor, dram_ap.offset, [[0, 128], [1, n]])


def emit_rstd(P, st, pref, h, hkey, ntiles, junk):
    ss = P.sb(pref + "ss", [128, ntiles], F32, st)
    sq = P.sb(pref + "sq", [128, ntiles], F32, st)
    rstd = P.sb(pref + "rstd", [128, ntiles], F32, st)
    for i in range(ntiles):
        P.op("act", lambda e, i=i: e.activation(out=junk[:], in_=h[:, i, :], func=AF.Square, accum_out=ss[:, i:i + 1]),
             reads=[hkey(i)], writes=[pref + "junk", pref + "ss%d" % i])
    P.op("act", lambda e: e.activation(out=sq[:], in_=ss[:], func=AF.Sqrt, scale=1.0 / 1024, bias=EPS),
         reads=[pref + "ss%d" % i for i in range(ntiles)], writes=[pref + "sq"])
    P.op("dve", lambda e: e.reciprocal(out=rstd[:], in_=sq[:]), reads=[pref + "sq"], writes=[pref + "rstd"])
    return rstd


def emit_peer(P, h, hkey, D, C):
    idf, idb, iota16 = C["idf"], C["idb"], C["iota16"]
    with ExitStack() as stC:
        xn = P.sb("pe_xn", [128, NT, 1024], BF16, stC)
        junk = P.sb("pe_junk", [128, 1024], BF16, stC)
        idxT = P.sb("pe_idxT", [128, 256], I32, stC)
        gateT = P.sb("pe_gateT", [128, 256], F32, stC)
        wqb = P.sb("pe_wqb", [128, 8, 2048], BF16, stC)
        KT = P.sb("pe_KT", [128, 16, 128], BF16, stC)
        pA0 = P.ps("pe_pA0", [128, 512], F32, stC)
        pA1 = P.ps("pe_pA1", [128, 512], F32, stC)
        pT = pA0[:].bitcast(BF16).rearrange("p (c t) -> p c t", c=8)
        for c in range(8):
            P.dma("pool", lambda e, c=c: e.dma_start(out=wqb[:, c, :], in_=D["wq"][c * 128:(c + 1) * 128, :]), writes=["pe_wqb"])
        scs = P.sb("pe_scs", [128, 2048], F32, stC)
        skf = scs[:].rearrange("p (a d) -> p a d", a=16)
        P.dma("sp", lambda e: e.dma_start(out=skf, in_=D["sk"].rearrange("a n d -> n a d")), writes=["pe_scs%d" % q for q in range(4)])
        for a in range(16):
            P.op("pe", lambda e, a=a: e.transpose(out=pA1[:, (a % 4) * 128:(a % 4 + 1) * 128], in_=skf[:, a, :], identity=idf[:]),
                 reads=["pe_scs%d" % q for q in range(4)] + ["idf"], writes=["ps:A1"])
            if a % 4 == 3:
                g4 = a // 4
                P.op("act", lambda e, g4=g4: e.activation(out=KT[:, g4 * 4:(g4 + 1) * 4, :], in_=pA1[:], func=AF.Copy),
                     reads=["ps:A1"], writes=["pe_KT"])
        with ExitStack() as stN:
            gB = P.sb("pe_gB", [128, 1024], F32, stN)
            P.dma("sp", lambda e: e.dma_start(out=gB[:], in_=bcast_rows(D["g"], 1024)), writes=["pe_gB"])
            rstd = emit_rstd(P, stN, "pe_", h, hkey, NT, junk)
            for i in range(NT):
                P.op("dve", lambda e, i=i: e.scalar_tensor_tensor(out=xn[:, i, :], in0=h[:, i, :], scalar=rstd[:, i:i + 1], in1=gB[:],
                                                                 op0=ALU.mult, op1=ALU.mult),
                     reads=[hkey(i), "pe_rstd", "pe_gB"], writes=["pe_xn%d" % i])
            P.join()
        xnT = P.sb("pe_xnT", [128, 8, 128], BF16, stC)
        qT = P.sb("pe_qT", [128, 16, 128], BF16, stC)
        scr = P.sb("pe_scr", [128, 2048], F32, stC)
        eq = scr[:].bitcast(BF16)[:, 0:2048]
        prod = scr[:].bitcast(BF16)[:, 2048:4096]
        m = P.sb("pe_m", [128, 16, 16], F32, stC)
        ix = P.sb("pe_ix", [128, 16, 16], U32, stC)
        ixf = P.sb("pe_ixf", [128, 16, 16], F32, stC)
        cand = scs[:].rearrange("p (h c) -> p h c", h=8)
        ts = P.sb("pe_ts", [128, 8, 16], F32, stC)
        pos = P.sb("pe_pos", [128, 8, 16], U32, stC)
        au = P.sb("pe_au", [128, 128], U32, stC)
        bu = P.sb("pe_bu", [128, 128], U32, stC)
        af = P.sb("pe_af", [128, 128], F32, stC)
        bf = P.sb("pe_bf", [128, 128], F32, stC)
        e1 = P.sb("pe_e1", [128, 128], F32, stC)
        e2 = P.sb("pe_e2", [128, 128], F32, stC)
        eidx = P.sb("pe_eidx", [128, 128], F32, stC)
        negm = P.sb("pe_negm", [128, 8], F32, stC)
        ex = P.sb("pe_ex", [128, 8, 16], F32, stC)
        Z = P.sb("pe_Z", [128, 8], F32, stC)
        rZ = P.sb("pe_rZ", [128, 8], F32, stC)
        gate = P.sb("pe_gate", [128, 128], F32, stC)
        SCRK = ["pe_scr%d" % k for k in range(16)]
        SCSK = ["pe_scs%d" % q for q in range(4)]

        def phase_a(i):
            sl = (i % 2) * 128
            banks = [(pA1, "ps:A1"), (pA0, "ps:A0")]
            for c in range(8):
                P.op("pe", lambda e, c=c: e.transpose(out=pT[:, c, :], in_=xn[:, i, c * 128:(c + 1) * 128], identity=idb[:]),
                     reads=["pe_xn%d" % i, "idb"], writes=["ps:A0"])
            P.gap()
            P.op("act", lambda e: e.activation(out=xnT[:], in_=pT, func=AF.Copy), reads=["ps:A0"], writes=["pe_xnT"])
            P.gap()

            def qproj(g):
                bk, bn = banks[g % 2]
                for k in range(4):
                    hp = 4 * g + k
                    for c in range(8):
                        P.op("pe", lambda e, hp=hp, c=c, k=k: e.matmul(bk[:, k * 128:(k + 1) * 128], lhsT=wqb[:, c, hp * 128:(hp + 1) * 128], rhs=xnT[:, c, :],
                                                                       start=(c == 0), stop=(c == 7)),
                             reads=["pe_wqb", "pe_xnT"], writes=[bn])

            def qcopy(g):
                bk, bn = banks[g % 2]
                P.op("act", lambda e: e.activation(out=qT[:, 4 * g:4 * g + 4, :], in_=bk[:], func=AF.Copy), reads=[bn], writes=["pe_qT%d" % g])

            def score(q4):
                bk, bn = banks[q4 % 2]
                for k in range(4):
                    hp = q4 * 4 + k
                    P.op("pe", lambda e, k=k, hp=hp: e.matmul(bk[:, k * 128:(k + 1) * 128], lhsT=qT[:, hp, :], rhs=KT[:, hp, :], start=True, stop=True),
                         reads=["pe_qT%d" % q4, "pe_KT"], writes=[bn])

            def scopy(q4):
                bk, bn = banks[q4 % 2]
                P.op("act", lambda e: e.activation(out=scs[:, q4 * 512:(q4 + 1) * 512], in_=bk[:], func=AF.Copy), reads=[bn], writes=["pe_scs%d" % q4])

            qproj(0); qproj(1); P.gap(); qcopy(0); qcopy(1); P.gap()
            qproj(2); qproj(3); P.gap(); qcopy(2); qcopy(3); P.gap()
            score(0); score(1); P.gap(); scopy(0); scopy(1); P.gap()
            score(2); score(3); P.gap(); scopy(2); scopy(3); P.gap()
            svs = [scs[:, hp * 128:(hp + 1) * 128] for hp in range(16)]
            sks = ["pe_scs%d" % (hp // 4) for hp in range(16)]
            for hp in range(16):
                P.op("dve", lambda e, hp=hp: e.max(out=m[:, hp, 0:8], in_=svs[hp]), reads=[sks[hp]], writes=["pe_m%da" % hp])
            for hp in range(16):
                P.op("dve", lambda e, hp=hp: e.max_index(out=ix[:, hp, 0:8], in_max=m[:, hp, 0:8], in_values=svs[hp]),
                     reads=[sks[hp], "pe_m%da" % hp], writes=["pe_ix%da" % hp])
            for hp in range(16):
                P.op("dve", lambda e, hp=hp: e.match_replace(out=scr[:, hp * 128:(hp + 1) * 128], in_to_replace=m[:, hp, 0:8], in_values=svs[hp], imm_value=-1e30),
                     reads=[sks[hp], "pe_m%da" % hp], writes=[SCRK[hp]])
            for hp in range(16):
                P.op("dve", lambda e, hp=hp: e.max(out=m[:, hp, 8:16], in_=scr[:, hp * 128:(hp + 1) * 128]), reads=[SCRK[hp]], writes=["pe_m%db" % hp])
            for hp in range(16):
                P.op("dve", lambda e, hp=hp: e.max_index(out=ix[:, hp, 8:16], in_max=m[:, hp, 8:16], in_values=svs[hp]),
                     reads=[sks[hp], "pe_m%db" % hp], writes=["pe_ix%db" % hp])
            mkeys = ["pe_m%d%s" % (hp, x) for hp in range(16) for x in "ab"]
            ixkeys = ["pe_ix%d%s" % (hp, x) for hp in range(16) for x in "ab"]
            P.op("dve", lambda e: e.tensor_tensor(
                out=cand[:].rearrange("p h (a b) -> p h a b", a=16),
                in0=cust(m[:, 0, 0:1], [[32, 8], [1, 16], [0, 16]]),
                in1=cust(m[:, 1, 0:1], [[32, 8], [0, 16], [1, 16]]), op=ALU.add),
                reads=mkeys + ixkeys, writes=["pe_cand"] + SCSK)
            P.op("dve", lambda e: e.tensor_copy(out=ixf[:], in_=ix[:]), reads=ixkeys, writes=["pe_ixf"])
            cvs = [cand[:, hh, :] for hh in range(8)]
            for hh in range(8):
                P.op("dve", lambda e, hh=hh: e.max(out=ts[:, hh, 0:8], in_=cvs[hh]), reads=["pe_cand"] + SCSK, writes=["pe_ts%da" % hh])
            for hh in range(8):
                P.op("dve", lambda e, hh=hh: e.max_index(out=pos[:, hh, 0:8], in_max=ts[:, hh, 0:8], in_values=cvs[hh]),
                     reads=["pe_cand", "pe_ts%da" % hh] + SCSK, writes=["pe_pos%da" % hh])
            for hh in range(8):
                P.op("dve", lambda e, hh=hh: e.match_replace(out=scr[:, hh * 256:(hh + 1) * 256], in_to_replace=ts[:, hh, 0:8], in_values=cvs[hh], imm_value=-1e30),
                     reads=["pe_cand", "pe_ts%da" % hh] + SCSK, writes=[SCRK[2 * hh], SCRK[2 * hh + 1]])
            for hh in range(8):
                P.op("dve", lambda e, hh=hh: e.max(out=ts[:, hh, 8:16], in_=scr[:, hh * 256:(hh + 1) * 256]),
                     reads=[SCRK[2 * hh], SCRK[2 * hh + 1]], writes=["pe_ts%db" % hh])
            for hh in range(8):
                P.op("dve", lambda e, hh=hh: e.max_index(out=pos[:, hh, 8:16], in_max=ts[:, hh, 8:16], in_values=cvs[hh]),
                     reads=["pe_cand", "pe_ts%db" % hh] + SCSK, writes=["pe_pos%db" % hh])
            tskeys = ["pe_ts%d%s" % (hh, x) for hh in range(8) for x in "ab"]
            poskeys = ["pe_pos%d%s" % (hh, x) for hh in range(8) for x in "ab"]
            posf = pos[:].rearrange("p h k -> p (h k)")
            P.op("dve", lambda e: e.tensor_single_scalar(out=au[:], in_=posf, scalar=4, op=ALU.logical_shift_right), reads=poskeys, writes=["pe_au"])
            P.op("dve", lambda e: e.tensor_single_scalar(out=bu[:], in_=posf, scalar=15, op=ALU.bitwise_and), reads=poskeys, writes=["pe_bu"])
            P.op("dve", lambda e: e.tensor_copy(out=af[:], in_=au[:]), reads=["pe_au"], writes=["pe_af"])
            P.op("dve", lambda e: e.tensor_copy(out=bf[:], in_=bu[:]), reads=["pe_bu"], writes=["pe_bf"])
            for (sel, side, eo, en) in ((af, 0, e1, "pe_e1"), (bf, 1, e2, "pe_e2")):
                sn = "pe_af" if side == 0 else "pe_bf"
                P.op("dve", lambda e, sel=sel: e.tensor_tensor(
                    out=eq.rearrange("p (s a) -> p s a", a=16),
                    in0=cust(sel[:, 0:1], [[1, 128], [0, 16]]),
                    in1=cust(iota16[:, 0:1], [[0, 128], [1, 16]]), op=ALU.is_equal),
                    reads=[sn, "iota16"] + SCRK, writes=["pe_eq"])
                P.op("dve", lambda e, side=side: e.tensor_tensor(
                    out=prod.rearrange("p (h k a) -> p h k a", h=8, k=16),
                    in0=eq.rearrange("p (h k a) -> p h k a", h=8, k=16),
                    in1=cust(ixf[:, side, 0:1], [[32, 8], [0, 16], [1, 16]]), op=ALU.mult),
                    reads=["pe_eq", "pe_ixf"], writes=["pe_prod"])
                P.op("dve", lambda e, eo=eo: e.tensor_reduce(out=eo[:], in_=prod.rearrange("p (s a) -> p s a", a=16), axis=AX.X, op=ALU.add),
                     reads=["pe_prod"], writes=[en])
            P.op("dve", lambda e: e.scalar_tensor_tensor(out=eidx[:], in0=e1[:], scalar=C["c128"][:, 0:1], in1=e2[:], op0=ALU.mult, op1=ALU.add),
                 reads=["pe_e1", "pe_e2", "c128"], writes=["pe_eidx"] + SCRK)
            P.op("dve", lambda e: e.tensor_scalar(out=negm[:], in0=cust(ts[:, 0, 0:1], [[16, 8]]), scalar1=-1.0, scalar2=None, op0=ALU.mult),
                 reads=tskeys, writes=["pe_negm"])
            P.gap()
            for hh in range(8):
                P.op("act", lambda e, hh=hh: e.activation(out=ex[:, hh, :], in_=ts[:, hh, :], func=AF.Exp, bias=negm[:, hh:hh + 1],
                                                          accum_out=Z[:, hh:hh + 1]),
                     reads=tskeys + ["pe_negm"], writes=["pe_ex%d" % hh, "pe_Z%d" % hh])
            P.gap()
            P.op("dve", lambda e: e.reciprocal(out=rZ[:], in_=Z[:]), reads=["pe_Z%d" % hh for hh in range(8)], writes=["pe_rZ"])
            P.op("dve", lambda e: e.tensor_tensor(out=gate[:].rearrange("p (h k) -> p h k", h=8), in0=ex[:],
                                                  in1=cust(rZ[:, 0:1], [[1, 8], [0, 16]]), op=ALU.mult),
                 reads=["pe_ex%d" % hh for hh in range(8)] + ["pe_rZ"], writes=["pe_gate"])
            P.gap()
            P.op("pe", lambda e: e.transpose(out=pA0[:, 0:128], in_=eidx[:], identity=idf[:]), reads=["pe_eidx", "idf"], writes=["ps:A0"])
            P.op("pe", lambda e: e.transpose(out=pA0[:, 128:256], in_=gate[:], identity=idf[:]), reads=["pe_gate", "idf"], writes=["ps:A0"])
            P.gap()
            P.op("dve", lambda e: e.tensor_copy(out=idxT[:, sl:sl + 128], in_=pA0[:, 0:128]), reads=["ps:A0"], writes=["pe_idxT%d" % (i % 2)])
            P.op("act", lambda e: e.activation(out=gateT[:, sl:sl + 128], in_=pA0[:, 128:256], func=AF.Copy),
                 reads=["ps:A0"], writes=["pe_gateT%d" % (i % 2)])

        NR = 8
        UV = [P.sb("pe_UV%d" % r, [128, 2048], BF16, stC) for r in range(NR)]
        NZ = 4
        Zb = [P.sb("pe_Zb%d" % r, [128, 256], BF16, stC) for r in range(NZ)]
        hidc = P.sb("pe_hidc", [128, 8], F32, stC)
        gelc = P.sb("pe_gelc", [128, 8], F32, stC)
        pxb = [P.ps("pe_pxb%d" % r, [128, 1024], F32, stC) for r in range(2)]
        pout = P.ps("pe_pout", [128, 1024], F32, stC)
        for r in range(NZ):
            P.op("pool", lambda e, r=r: e.memset(Zb[r][:], 0.0), writes=["pe_Zb%d" % r])

        def u_stage(i, t):
            tt = i * 128 + t
            col = (i % 2) * 128 + t
            r, k8 = tt % NR, tt % 8
            P.dma("pool", lambda e: e.indirect_dma_start(out=UV[r][:], out_offset=None, in_=D["uv"],
                                                         in_offset=bass.IndirectOffsetOnAxis(ap=idxT[:, col:col + 1], axis=0)),
                  reads=["pe_idxT%d" % (i % 2), D["uvkey"]], writes=["pe_UV%d" % r], semi=r)
            pb = pxb[tt % 2]
            lhs = cust(idb[:, t:t + 1], [[0, 128]])
            for hf in range(2):
                P.op("pe", lambda e, hf=hf: e.matmul(pb[:, hf * 512:(hf + 1) * 512], lhsT=lhs, rhs=xn[:, i, hf * 512:(hf + 1) * 512], start=True, stop=True),
                     reads=["idb", "pe_xn%d" % i], writes=["ps:pxb%d_%d" % (tt % 2, hf)])
            P.op("dve", lambda e: e.scalar_tensor_tensor(out=junk[:], in0=UV[r][:, 0:1024], scalar=C["c1"][:, 0:1], in1=pb[:], op0=ALU.mult, op1=ALU.mult,
                                                         accum_out=hidc[:, k8:k8 + 1]),
                 reads=["pe_UV%d" % r, "ps:pxb%d_0" % (tt % 2), "ps:pxb%d_1" % (tt % 2), "c1"], writes=["pe_junk3", "pe_hidc%d" % k8])
            z = tt % NZ
            P.op("act", lambda e: e.activation(out=gelc[:, k8:k8 + 1], in_=hidc[:, k8:k8 + 1], func=AF.Gelu), reads=["pe_hidc%d" % k8], writes=["pe_gelc%d" % k8])
            P.op("act", lambda e: e.activation(out=Zb[z][:, 127:128], in_=gelc[:, k8:k8 + 1], func=AF.Copy, scale=gateT[:, col:col + 1]),
                 reads=["pe_gelc%d" % k8, "pe_gateT%d" % (i % 2)], writes=["pe_Zb%d" % z])

        def v_mm(i, t):
            tt = i * 128 + t
            r, z = tt % NR, tt % NZ
            for hf in range(2):
                P.op("pe", lambda e, hf=hf: e.matmul(pout[:, hf * 512:(hf + 1) * 512], lhsT=Zb[z][:, 127 - t:255 - t],
                                                     rhs=UV[r][:, 1024 + hf * 512:1024 + (hf + 1) * 512], start=(t == 0), stop=(t == 127)),
                     reads=["pe_Zb%d" % z, "pe_UV%d" % r], writes=["ps:pout_%d" % hf])
            if t == 127:
                P.op("dve", lambda e: e.tensor_tensor(out=h[:, i, :], in0=h[:, i, :], in1=pout[:], op=ALU.add),
                     reads=[hkey(i), "ps:pout_0", "ps:pout_1"], writes=[hkey(i)])

        def record_a(i):
            P.start_record()
            phase_a(i)
            return P.stop_record()

        for it in record_a(0):
            if it[0] == "op":
                P.op(*it[1:])
        LAG = 2
        pend = []
        for i in range(NT):
            ops = record_a(i + 1) if i + 1 < NT else []
            skip = 0
            for t in range(128):
                u_stage(i, t)
                pend.append((i, t))
                if len(pend) > LAG:
                    v_mm(*pend.pop(0))
                if skip > 0:
                    skip -= 1
                    continue
                nd = no = 0
                while ops:
                    it = ops[0]
                    if it[0] == "gap":
                        ops.pop(0); skip = 2
                        break
                    if it[1] == "dve":
                        if nd >= 2:
                            break
                        nd += 1
                    else:
                        if no >= 16:
                            break
                        no += 1
                    ops.pop(0)
                    P.op(*it[1:])
            for it in ops:
                if it[0] == "op":
                    P.op(*it[1:])
        while pend:
            v_mm(*pend.pop(0))
        P.join()


def uv_convert_steps(P, u_dram, v_dram, uv_dram, uvkey, semi):
    steps = []
    for k in range(16):
        rs = slice(k * 1024, (k + 1) * 1024)
        steps.append(lambda rs=rs: P.dma("pool", lambda e: e.dma_start(out=uv_dram[rs, 0:1024], in_=u_dram[rs, :]), writes=[uvkey], semi=semi))
        steps.append(lambda rs=rs: P.dma("pool", lambda e: e.dma_start(out=uv_dram[rs, 1024:2048], in_=v_dram[rs, :]), writes=[uvkey], semi=semi))
    return steps


def emit_uv_convert(P, u_dram, v_dram, uv_dram, uvkey, semi):
    for f in uv_convert_steps(P, u_dram, v_dram, uv_dram, uvkey, semi):
        f()


def load_consts(P, nc, cd):
    idf = P.sb("idf", [128, 128], F32)
    idb = P.sb("idb", [128, 128], BF16)
    iota16 = P.sb("iota16", [128, 16], F32)
    P.dma("sp", lambda e: e.dma_start(out=idf[:], in_=cd["ident"]), writes=["idf"])
    P.dma("sp", lambda e: e.dma_start(out=iota16[:], in_=cd["iota16"]), writes=["iota16"])
    P.op("dve", lambda e: e.tensor_copy(out=idb[:], in_=idf[:]), reads=["idf"], writes=["idb"])
    c1 = P.sb("c1", [128, 1], F32)
    c128 = P.sb("c128", [128, 1], F32)
    P.op("pool", lambda e: e.memset(c1[:], 1.0), writes=["c1"])
    P.op("pool", lambda e: e.memset(c128[:], 128.0), writes=["c128"])
    return {"idf": idf, "idb": idb, "iota16": iota16, "c1": c1, "c128": c128}


def host_consts():
    return {"ident": np.eye(128, dtype=np.float32),
            "iota16": np.tile(np.arange(16, dtype=np.float32)[None, :], (128, 1))}


def emit_peer_block(P, C, D, fin=None):
    with ExitStack() as st:
        h = P.sb("h", [128, NT, 1024], F32, st)
        hkey = lambda i: "h%d" % i
        hv = D["h_in"].rearrange("(i p) d -> p i d", p=128)
        for i in range(NT):
            P.dma("sp", lambda e, i=i: e.dma_start(out=h[:, i, :], in_=hv[:, i, :]), writes=[hkey(i)])
        emit_peer(P, h, hkey, D, C)
        ho = D["h_out"].rearrange("(i p) d -> p i d", p=128)
        if fin is None:
            for i in range(NT):
                P.dma("sp", lambda e, i=i: e.dma_start(out=ho[:, i, :], in_=h[:, i, :]), reads=[hkey(i)])
        else:
            fnorm, y_out = fin
            gF = P.sb("fn_gF", [128, 1024], F32, st)
            jF = P.sb("fn_jF", [128, 1024], BF16, st)
            yt = [P.sb("fn_yt%d" % k, [128, 1024], F32, st) for k in range(2)]
            P.dma("sp", lambda e: e.dma_start(out=gF[:], in_=bcast_rows(fnorm, 1024)), writes=["fn_gF"])
            rstd2 = emit_rstd(P, st, "fn_", h, hkey, NT, jF)
            yo = y_out.rearrange("(i p) d -> p i d", p=128)
            for i in range(NT):
                k = i % 2
                P.op("dve", lambda e, i=i, k=k: e.scalar_tensor_tensor(out=yt[k][:], in0=h[:, i, :], scalar=rstd2[:, i:i + 1], in1=gF[:],
                                                                      op0=ALU.mult, op1=ALU.mult),
                     reads=[hkey(i), "fn_rstd", "fn_gF"], writes=["fn_yt%d" % k])
                P.dma("sp", lambda e, i=i, k=k: e.dma_start(out=yo[:, i, :], in_=yt[k][:]), reads=["fn_yt%d" % k])
        P.join()


def build_peer_nc(final=False):
    nc = bass.Bass("TRN2", target_bir_lowering=False)
    dr = lambda n, s, k="ExternalInput", dt=F32: nc.dram_tensor(n, list(s), dt, kind=k).ap()
    D = {"h_in": dr("h_in", [TOK, 1024]), "g": dr("g", [1024]), "wq": dr("wq", [1024, 2048]), "sk": dr("sk", [16, 128, 128]),
         "u": dr("u", [16384, 1024]), "v": dr("v", [16384, 1024])}
    cd = {"ident": dr("ident", [128, 128]), "iota16": dr("iota16", [128, 16])}
    D["h_out"] = dr("h_out", [TOK, 1024], "ExternalOutput")
    fin = (dr("fnorm", [1024]), dr("y_out", [TOK, 1024], "ExternalOutput")) if final else None
    D["uv"] = nc.dram_tensor("uv_i", [16384, 2048], BF16).ap()
    D["uvkey"] = "uvtab"
    with ExitStack() as st:
        P = Prog(nc, st)
        C = load_consts(P, nc, cd)
        emit_uv_convert(P, D["u"], D["v"], D["uv"], "uvtab", 50)
        emit_peer_block(P, C, D, fin)
        P.wait_all("sp")
        P.emit()
    return nc


def emit_xnT(P, st, pref, src, g_dram, xnT, col0, ntiles, C, pT):
    idb = C["idb"]
    gB = P.sb(pref + "gB", [128, 1024], F32, st)
    P.dma("sp", lambda e: e.dma_start(out=gB[:], in_=bcast_rows(g_dram, 1024)), writes=[pref + "gB"])
    NH = 4
    ht = [P.sb(pref + "ht%d" % k, [128, 1024], F32, st) for k in range(NH)]
    xt = [P.sb(pref + "xt%d" % k, [128, 1024], BF16, st) for k in range(2)]
    jk = P.sb(pref + "jk", [128, 1024], BF16, st)
    ss = P.sb(pref + "ss", [128, ntiles], F32, st)
    sq = P.sb(pref + "sq", [128, ntiles], F32, st)
    rs = P.sb(pref + "rs", [128, ntiles], F32, st)
    if callable(src):
        tile_ap = src
    else:
        sv = src.rearrange("(i p) d -> p i d", p=128)
        tile_ap = lambda i: sv[:, i, :]

    def load(i):
        P.dma("sp", lambda e: e.dma_start(out=ht[i % NH][:], in_=tile_ap(i)), writes=[pref + "ht%d" % (i % NH)])

    def stats(i):
        k = i % NH
        P.op("dve", lambda e: e.scalar_tensor_tensor(out=jk[:], in0=ht[k][:], scalar=C["c1"][:, 0:1], in1=ht[k][:], op0=ALU.mult, op1=ALU.mult,
                                                     accum_out=ss[:, i:i + 1]),
             reads=[pref + "ht%d" % k, "c1"], writes=[pref + "jk", pref + "ss%d" % i])
        P.op("act", lambda e: e.activation(out=sq[:, i:i + 1], in_=ss[:, i:i + 1], func=AF.Sqrt, scale=1.0 / 1024, bias=EPS),
             reads=[pref + "ss%d" % i], writes=[pref + "sq%d" % i])

    def evac(i):
        P.op("dve", lambda e: e.tensor_copy(out=xnT[:, :, col0 + i * 128:col0 + (i + 1) * 128], in_=pT[:]),
             reads=["ps:pT"], writes=[pref + "xnT"])

    for i in range(min(NH - 1, ntiles)):
        load(i)
    stats(0)
    for i in range(ntiles):
        k, kx = i % NH, i % 2
        if i + 1 < ntiles:
            stats(i + 1)
        P.op("dve", lambda e, i=i: e.reciprocal(out=rs[:, i:i + 1], in_=sq[:, i:i + 1]), reads=[pref + "sq%d" % i], writes=[pref + "rs%d" % i])
        P.op("dve", lambda e, i=i, k=k, kx=kx: e.scalar_tensor_tensor(out=xt[kx][:], in0=ht[k][:], scalar=rs[:, i:i + 1], in1=gB[:], op0=ALU.mult, op1=ALU.mult),
             reads=[pref + "ht%d" % k, pref + "rs%d" % i, pref + "gB"], writes=[pref + "xt%d" % kx])
        if i > 0:
            evac(i - 1)
        for c in range(8):
            P.op("pe", lambda e, c=c, kx=kx: e.transpose(out=pT[:, c, :], in_=xt[kx][:, c * 128:(c + 1) * 128], identity=idb[:]),
                 reads=[pref + "xt%d" % kx, "idb"], writes=["ps:pT"])
        if i + NH - 1 < ntiles:
            load(i + NH - 1)
    evac(ntiles - 1)


def emit_outproj(P, st, pref, catT, catkey, wout_dram, h_src, h_dst, po):
    wo = P.sb(pref + "wo", [128, 8, 1024], BF16, st)
    for c in range(8):
        P.dma("pool", lambda e, c=c: e.dma_start(out=wo[:, c, :], in_=wout_dram[c * 128:(c + 1) * 128, :]), writes=[pref + "wo"])
    NB = 4
    hb = [P.sb(pref + "hb%d" % k, [128, 1024], F32, st) for k in range(NB)]
    sv = h_src.rearrange("(i p) d -> p i d", p=128)
    dv = h_dst.rearrange("(i p) d -> p i d", p=128)

    def load(i):
        P.dma("sp", lambda e: e.dma_start(out=hb[i % NB][:], in_=sv[:, i, :]), writes=[pref + "hb%d" % (i % NB)])

    for i in range(NB - 1):
        load(i)
    for i in range(NT):
        k = i % NB
        for hf in range(2):
            for f in range(8):
                P.op("pe", lambda e, i=i, hf=hf, f=f: e.matmul(po[:, hf * 512:(hf + 1) * 512], lhsT=catT[:, f, i * 128:(i + 1) * 128],
                                                               rhs=wo[:, f, hf * 512:(hf + 1) * 512], start=(f == 0), stop=(f == 7)),
                     reads=[catkey, pref + "wo"], writes=["ps:po%d" % hf])
        P.op("dve", lambda e, k=k: e.tensor_tensor(out=hb[k][:], in0=hb[k][:], in1=po[:], op=ALU.add),
             reads=[pref + "hb%d" % k, "ps:po0", "ps:po1"], writes=[pref + "hb%d" % k])
        if i + NB - 1 < NT:
            load(i + NB - 1)
        P.dma("sp", lambda e, i=i, k=k: e.dma_start(out=dv[:, i, :], in_=hb[k][:]), reads=[pref + "hb%d" % k])


def emit_even(P, C, D, bg=None):
    h_own, h_oth, g, w_in, w_out = D["h_own"], D["h_oth"], D["g"], D["w_in"], D["w_out"]
    cos2, sin2, gmask, cm, dpat, cp, h_out = D["cos2"], D["sin2"], D["gmask"], D["cm"], D["dpat"], D["cp"], D["h_out"]
    SC = 128.0 ** -0.5
    with ExitStack() as st:
        idf, idb = C["idf"], C["idb"]
        xnT = P.sb("xnT", [128, 8, 4096], BF16, st)
        catT = P.sb("catT", [128, 8, TOK], BF16, st)
        cosb = P.sb("cosb", [128, 4096], BF16, st); sinb = P.sb("sinb", [128, 4096], BF16, st)
        cmb = P.sb("cmb", [128, 2048], BF16, st)
        gm3 = P.sb("gm3", [128, 768], F32, st)
        onesb = P.sb("onesb", [128, 128], BF16, st)
        P.dma("pool", lambda e: e.dma_start(out=cosb[:], in_=cos2), writes=["cosb"])
        P.dma("pool", lambda e: e.dma_start(out=sinb[:], in_=sin2), writes=["sinb"])
        P.dma("pool", lambda e: e.dma_start(out=cmb[:], in_=cm), writes=["cmb"])
        P.dma("sp", lambda e: e.dma_start(out=gm3[:], in_=bcast_rows(gmask, 768)), writes=["gm3"])
        P.op("pool", lambda e: e.memset(onesb[:], 1.0), writes=["onesb"])
        pT = P.ps("pT", [128, 8, 128], BF16, st)
        with ExitStack() as s1:
            emit_xnT(P, s1, "no_", h_oth, g, xnT, 0, NT, C, pT)
            emit_xnT(P, s1, "nw_", h_own, g, xnT, TOK, NT, C, pT)
            P.join()
        with ExitStack() as s2:
            pp = [P.ps("pp%d" % k, [128, 512], F32, s2) for k in range(2)]
            pst = [P.ps("pst%d" % k, [128, 512], F32, s2) for k in range(2)]
            pout = P.ps("pout", [128, 512], F32, s2)
            psum = P.ps("psum", [128, 512], F32, s2)
            pmisc = P.ps("pmisc", [128, 512], F32, s2)
            wh = P.sb("wh", [128, 8, 4, 128], BF16, s2)
            qT = P.sb("qT", [128, TOK], BF16, s2)
            kT = P.sb("kT", [128, 4096], BF16, s2)
            V = P.sb("V", [128, 32, 128], BF16, s2)
            sg = P.sb("sg", [128, TOK], F32, s2)
            tA = P.sb("tA", [128, 512], F32, s2)
            tB = P.sb("tB", [128, 512], F32, s2)
            PTb = [P.sb("PTb%d" % k, [128, 512], BF16, s2) for k in range(2)]
            dp = P.sb("dp", [128, 2560], F32, s2)
            cpb = P.sb("cpb", [128, 128], F32, s2)
            km = P.sb("km", [128, 16], F32, s2); kmb = P.sb("kmb", [128, 16], BF16, s2)
            gmv = P.sb("gmv", [128, 256], F32, s2); sel = P.sb("sel", [128, 256], F32, s2)
            m8 = P.sb("m8", [128, 16, 8], F32, s2)
            biasT = P.sb("biasT", [128, TOK], BF16, s2)
            P.op("pool", lambda e: e.memset(biasT[:], 0.0), writes=["biasT"])
            rsb = P.sb("rsb", [128, 512], F32, s2)
            sqb = P.sb("sqb", [128, 512], BF16, s2)
            ppi = [0]

            def proj_feat(which, dst, tok0, ntok, mode, dkey):
                for t0 in range(0, ntok, 512):
                    k = ppi[0] % 2; ppi[0] += 1
                    for c in range(8):
                        P.op("pe", lambda e, c=c, k=k, t0=t0: e.matmul(pp[k][:], lhsT=wh[:, c, which, :], rhs=xnT[:, c, tok0 + t0:tok0 + t0 + 512],
                                                                       start=(c == 0), stop=(c == 7)),
                             reads=["wh", "no_xnT", "nw_xnT"], writes=["ps:pp%d" % k])
                    if mode == "silu":
                        P.op("act", lambda e, k=k, t0=t0: e.activation(out=dst[:, t0:t0 + 512], in_=pp[k][:], func=AF.Silu), reads=["ps:pp%d" % k], writes=[dkey])
                        continue
                    l0 = tok0 + t0
                    P.op("dve", lambda e, k=k, l0=l0: e.tensor_tensor(out=tA[:], in0=pp[k][:], in1=cosb[:, l0:l0 + 512], op=ALU.mult),
                         reads=["ps:pp%d" % k, "cosb"], writes=["tA"])
                    P.op("dve", lambda e, k=k, l0=l0: e.tensor_tensor(out=tB[0:64, :], in0=pp[k][64:128, :], in1=sinb[64:128, l0:l0 + 512], op=ALU.mult),
                         reads=["ps:pp%d" % k, "sinb"], writes=["tB"])
                    P.op("dve", lambda e, k=k, l0=l0: e.tensor_tensor(out=tB[64:128, :], in0=pp[k][0:64, :], in1=sinb[0:64, l0:l0 + 512], op=ALU.mult),
                         reads=["ps:pp%d" % k, "sinb"], writes=["tB"])
                    P.op("dve", lambda e, t0=t0: e.tensor_tensor(out=dst[:, t0:t0 + 512], in0=tA[:], in1=tB[:], op=ALU.add), reads=["tA", "tB"], writes=[dkey])

            for head in range(8):
                moba = head < 4
                hh = head % 4
                base = 0 if moba else 1536
                cols = [base + hh * 128, base + 512 + hh * 128, base + 1024 + hh * 128] + ([] if moba else [3072 + hh * 128])
                for wi, c0 in enumerate(cols):
                    P.dma("pool", lambda e, wi=wi, c0=c0: e.dma_start(out=wh[:, :, wi, :], in_=w_in[:, c0:c0 + 128].rearrange("(c p) n -> p c n", p=128)),
                          writes=["wh"])
                if bg is not None:
                    bg()
                proj_feat(0, qT, TOK, TOK, "rope", "qT")
                proj_feat(1, kT, 0, 4096, "rope", "kT")
                if not moba:
                    proj_feat(3, sg, TOK, TOK, "silu", "sg")
                    P.dma("sp", lambda e, hh=hh: e.dma_start(out=dp[:], in_=dpat[hh]), writes=["dp"])
                    P.dma("sp", lambda e, hh=hh: e.dma_start(out=cpb[:], in_=bcast_rows(cp[hh], 128)), writes=["cpb"])
                for i4 in range(8):
                    k = ppi[0] % 2; ppi[0] += 1
                    for q4 in range(4):
                        i = i4 * 4 + q4
                        for c in range(8):
                            P.op("pe", lambda e, c=c, k=k, i=i, q4=q4: e.matmul(pp[k][:, q4 * 128:(q4 + 1) * 128], lhsT=xnT[:, c, i * 128:(i + 1) * 128],
                                                                               rhs=wh[:, c, 2, :], start=(c == 0), stop=(c == 7)),
                                 reads=["wh", "no_xnT", "nw_xnT"], writes=["ps:pp%d" % k])
                    P.op("act", lambda e, k=k, i4=i4: e.activation(out=V[:, i4 * 4:(i4 + 1) * 4, :], in_=pp[k][:], func=AF.Copy), reads=["ps:pp%d" % k], writes=["V"])
                if moba:
                    P.op("dve", lambda e: e.tensor_reduce(out=km[:], in_=kT[:].rearrange("p (n k) -> p n k", k=256), axis=AX.X, op=ALU.add), reads=["kT"], writes=["km"])
                    P.op("dve", lambda e: e.tensor_copy(out=kmb[:], in_=km[:]), reads=["km"], writes=["kmb"])
                    for j in range(16):
                        P.op("pe", lambda e, j=j: e.matmul(pmisc[:, j * 16:(j + 1) * 16], lhsT=qT[:, j * 128:(j + 1) * 128], rhs=kmb[:], start=True, stop=True),
                             reads=["qT", "kmb"], writes=["ps:pmisc"])
                    P.op("dve", lambda e: e.tensor_tensor(out=gmv[:], in0=pmisc[:, 0:256], in1=gm3[:, 0:256], op=ALU.add), reads=["ps:pmisc", "gm3"], writes=["gmv"])
                    for j in range(16):
                        P.op("dve", lambda e, j=j: e.max(out=m8[:, j, :], in_=gmv[:, j * 16:(j + 1) * 16]), reads=["gmv"], writes=["m8_%d" % j])
                        P.op("dve", lambda e, j=j: e.tensor_scalar(out=sel[:, j * 16:(j + 1) * 16], in0=gmv[:, j * 16:(j + 1) * 16], scalar1=m8[:, j, 2:3], scalar2=None,
                                                                   op0=ALU.is_ge), reads=["gmv", "m8_%d" % j], writes=["sel"])
                    P.op("dve", lambda e: e.tensor_tensor(out=sel[:], in0=sel[:], in1=gm3[:, 256:512], op=ALU.mult), reads=["sel", "gm3"], writes=["sel"])
                    P.op("dve", lambda e: e.tensor_tensor(out=sel[:], in0=sel[:], in1=gm3[:, 512:768], op=ALU.add), reads=["sel", "gm3"], writes=["sel"])
                    P.op("dve", lambda e: e.tensor_scalar(out=sel[:], in0=sel[:], scalar1=-1.0, scalar2=30000.0, op0=ALU.add, op1=ALU.mult), reads=["sel"], writes=["sel"])
                    for j4 in range(4):
                        for q4 in range(4):
                            j = j4 * 4 + q4
                            P.op("pe", lambda e, j=j, q4=q4: e.transpose(out=pmisc[0:16, q4 * 128:(q4 + 1) * 128], in_=sel[:, j * 16:(j + 1) * 16], identity=idf[:]),
                                 reads=["sel", "idf"], writes=["ps:pmisc"])
                        P.op("act", lambda e, j4=j4: e.activation(out=biasT[0:16, j4 * 512:(j4 + 1) * 512], in_=pmisc[0:16, :], func=AF.Copy), reads=["ps:pmisc"], writes=["biasT"])
                units = []
                for c in range(4):
                    ktiles = list(range(16)) + [16 + k for k in range(4 * c + 4)]
                    for idx, kt in enumerate(ktiles):
                        units.append((c, kt, idx == 0, idx == len(ktiles) - 1))

                def s1(u):
                    c, kt, first, last = units[u]
                    k = u % 2
                    qs = slice(c * 512, (c + 1) * 512)
                    diag = kt >= 16 + 4 * c
                    r4 = kt - 16 - 4 * c
                    P.op("pe", lambda e: e.matmul(pst[k][:], lhsT=kT[:, kt * 128:(kt + 1) * 128], rhs=qT[:, qs], start=True, stop=(not moba)),
                         reads=["kT", "qT"], writes=["ps:pst%d" % k])
                    if moba:
                        n = kt // 2
                        P.op("pe", lambda e: e.matmul(pst[k][:], lhsT=cust(idb[:, n:n + 1], [[0, 128]]), rhs=biasT[:, qs], start=False, stop=(not diag)),
                             reads=["idb", "biasT"], writes=["ps:pst%d" % k])
                        if diag:
                            P.op("pe", lambda e: e.matmul(pst[k][:], lhsT=idb[:], rhs=cmb[:, r4 * 512:(r4 + 1) * 512], start=False, stop=True),
                                 reads=["idb", "cmb"], writes=["ps:pst%d" % k])

                def s2(u):
                    c, kt, first, last = units[u]
                    k = u % 2
                    diag = kt >= 16 + 4 * c
                    r4 = kt - 16 - 4 * c
                    if moba:
                        P.op("act", lambda e: e.activation(out=PTb[k][:], in_=pst[k][:], func=AF.Exp, scale=SC), reads=["ps:pst%d" % k], writes=["PTb%d" % k])
                    else:
                        var = (r4 + 1) if diag else 0
                        ci = c * 32 + kt
                        P.op("dve", lambda e: e.scalar_tensor_tensor(out=PTb[k][:], in0=pst[k][:], scalar=cpb[:, ci:ci + 1],
                                                                     in1=dp[:, var * 512:(var + 1) * 512], op0=ALU.mult, op1=ALU.mult),
                             reads=["ps:pst%d" % k, "cpb", "dp"], writes=["PTb%d" % k])

                def s3(u):
                    c, kt, first, last = units[u]
                    k = u % 2
                    qs = slice(c * 512, (c + 1) * 512)
                    P.op("pe", lambda e: e.matmul(pout[:], lhsT=V[:, kt, :], rhs=PTb[k][:], start=first, stop=last),
                         reads=["V", "PTb%d" % k], writes=["ps:pout"])
                    if moba:
                        P.op("pe", lambda e: e.matmul(psum[:], lhsT=onesb[:], rhs=PTb[k][:], start=first, stop=last),
                             reads=["onesb", "PTb%d" % k], writes=["ps:psum"])
                    if not last:
                        return
                    dst = catT[:, head, qs]
                    if moba:
                        P.op("dve", lambda e: e.reciprocal(out=rsb[:], in_=psum[:]), reads=["ps:psum"], writes=["rsb"])
                        P.op("dve", lambda e: e.tensor_tensor(out=dst, in0=pout[:], in1=rsb[:], op=ALU.mult), reads=["ps:pout", "rsb"], writes=["catT"])
                    else:
                        P.op("act", lambda e: e.activation(out=sqb[:], in_=pout[:], func=AF.Square), reads=["ps:pout"], writes=["sqb"])
                        P.op("pe", lambda e: e.matmul(psum[:], lhsT=onesb[:], rhs=sqb[:], start=True, stop=True), reads=["onesb", "sqb"], writes=["ps:psum"])
                        P.op("act", lambda e: e.activation(out=rsb[:], in_=psum[:], func=AF.Sqrt, scale=1.0 / 128, bias=EPS), reads=["ps:psum"], writes=["rsb"])
                        P.op("dve", lambda e: e.reciprocal(out=tA[:], in_=rsb[:]), reads=["rsb"], writes=["tA"])
                        P.op("dve", lambda e: e.tensor_tensor(out=tB[:], in0=pout[:], in1=tA[:], op=ALU.mult), reads=["ps:pout", "tA"], writes=["tB"])
                        P.op("dve", lambda e: e.tensor_tensor(out=dst, in0=tB[:], in1=sg[:, qs], op=ALU.mult), reads=["tB", "sg"], writes=["catT"])

                s1(0)
                for u in range(len(units)):
                    if u + 1 < len(units):
                        s1(u + 1)
                    s2(u)
                    s3(u)
            P.join()
        with ExitStack() as s3:
            po = P.ps("po", [128, 1024], F32, s3)
            emit_outproj(P, s3, "op_", catT, "catT", w_out, h_own, h_out, po)
            P.join()


def build_even_nc():
    nc = bass.Bass("TRN2", target_bir_lowering=False)
    dr = lambda n, s, k="ExternalInput", dt=F32: nc.dram_tensor(n, list(s), dt, kind=k).ap()
    D = {"h_own": dr("h_own", [TOK, 1024]), "h_oth": dr("h_oth", [TOK, 1024]), "g": dr("g", [1024]), "w_in": dr("w_in", [1024, 3584]),
         "w_out": dr("w_out", [1024, 1024]), "cos2": dr("cos2", [128, 4096]), "sin2": dr("sin2", [128, 4096]), "gmask": dr("gmask", [768]),
         "cm": dr("cm", [128, 2048]), "dpat": dr("dpat", [4, 128, 2560]), "cp": dr("cp", [4, 128])}
    cd = {"ident": dr("ident", [128, 128]), "iota16": dr("iota16", [128, 16])}
    D["h_out"] = dr("h_out", [TOK, 1024], "ExternalOutput")
    with ExitStack() as st:
        P = Prog(nc, st)
        C = load_consts(P, nc, cd)
        emit_even(P, C, D)
        P.wait_all("sp")
        P.emit()
    return nc


def even_consts(half):
    inv = (10000.0 ** (-np.arange(64, dtype=np.float32) / 64)).astype(np.float32)
    pos = np.concatenate([np.arange(2048) + (1 - half) * 2048, np.arange(2048) + half * 2048]).astype(np.float32)
    ang = (pos[None, :] * np.tile(inv, 2)[:, None]).astype(np.float32)
    cos2 = np.cos(ang).astype(np.float32)
    sin2 = np.sin(ang).astype(np.float32)
    sin2[64:] = -sin2[64:]
    past = np.zeros((16, 16), np.float32); own = np.zeros((16, 16), np.float32)
    for j in range(16):
        qb = j // 2
        for n in range(16):
            if n < 8:
                past[j, n] = 1.0 if half == 1 else 0.0
            else:
                past[j, n] = 1.0 if n - 8 < qb else 0.0
                own[j, n] = 1.0 if n - 8 == qb else 0.0
    gmask = np.concatenate([np.where(past > 0, 0.0, -1e30).ravel(), past.ravel(), own.ravel()]).astype(np.float32)
    p = np.arange(128)[:, None]; q = np.arange(512)[None, :]
    cm = np.zeros((128, 4, 512), np.float32)
    for r4 in range(4):
        same = (q // 256) == (r4 // 2)
        cm[:, r4, :] = np.where(same & (r4 * 128 + p > q), -30000.0, 0.0)
    SC = 128.0 ** -0.5
    dpat = np.zeros((4, 128, 5, 512), np.float64); cp = np.zeros((4, 128), np.float64)
    for hh in range(4):
        gam = 1.0 - 2.0 ** (-5.0 - hh)
        dpat[hh, :, 0, :] = gam ** (q - p).astype(np.float64)
        for r4 in range(4):
            e = (q - r4 * 128 - p).astype(np.float64)
            dpat[hh, :, r4 + 1, :] = np.where(e >= 0, gam ** np.maximum(e, 0), 0.0)
        for c in range(4):
            for kt in range(32):
                if kt < 16:
                    v = gam ** float(2048 + c * 512 - kt * 128) * SC if half == 1 else 0.0
                else:
                    ko = kt - 16
                    v = gam ** float(c * 512 - ko * 128) * SC if ko * 128 < c * 512 else (SC if ko < 4 * c + 4 else 0.0)
                cp[hh, c * 32 + kt] = v
    return {"cos2": cos2, "sin2": sin2, "gmask": gmask, "cm": cm.reshape(128, 2048),
            "dpat": dpat.reshape(4, 128, 2560).astype(np.float32), "cp": cp.astype(np.float32)}


def emit_odd(P, C, D, bg=None):
    h_own, h_prev, g, w_in, w_out, convw, flag, h_out = (D[k] for k in ("h_own", "h_prev", "g", "w_in", "w_out", "convw", "flag", "h_out"))
    with ExitStack() as st:
        idf = C["idf"]
        xnT = P.sb("xnT", [128, 8, 128 + TOK], BF16, st)
        zT = P.sb("zT", [128, 8, TOK], BF16, st)
        cw = P.sb("cw", [128, 24], F32, st)
        cws = P.sb("cws", [24, 128], F32, st)
        fl = P.sb("fl", [128, 1], F32, st)
        P.dma("sp", lambda e: e.dma_start(out=cws[:], in_=convw), writes=["cws"])
        P.dma("sp", lambda e: e.dma_start(out=fl[:], in_=flag), writes=["fl"])
        pT = P.ps("pT", [128, 8, 128], BF16, st)
        with ExitStack() as s1:
            emit_xnT(P, s1, "no_", h_prev, g, xnT, 0, 1, C, pT)
            emit_xnT(P, s1, "nw_", h_own, g, xnT, 128, NT, C, pT)
            P.join()
        with ExitStack() as s2:
            pb = [P.ps("pb%d" % k, [128, 512], F32, s2) for k in range(2)]
            pc = [P.ps("pc%d" % k, [128, 512], F32, s2) for k in range(2)]
            ph = [P.ps("ph%d" % k, [128, 512], F32, s2) for k in range(2)]
            P.op("pe", lambda e: e.transpose(out=pb[0][:, 0:24], in_=cws[:], identity=idf[0:24, 0:24]), reads=["cws", "idf"], writes=["ps:pb0"])
            P.op("dve", lambda e: e.tensor_copy(out=cw[:], in_=pb[0][:, 0:24]), reads=["ps:pb0"], writes=["cw"])
            whs = [P.sb("wh%d" % k, [128, 8, 3, 128], BF16, s2) for k in range(2)]
            uT = P.sb("uT", [128, 2 + TOK], F32, s2)
            cgs = P.sb("cgs", [128, 512], F32, s2)
            yb = P.sb("yb", [128, 512], F32, s2)
            ui = 0
            def load_w(f):
                for gi in range(3):
                    c0 = gi * 1024 + f * 128
                    P.dma("pool", lambda e, gi=gi, c0=c0: e.dma_start(out=whs[f % 2][:, :, gi, :], in_=w_in[:, c0:c0 + 128].rearrange("(c p) n -> p c n", p=128)),
                          writes=["wh%d" % (f % 2)])

            load_w(0)
            for f in range(8):
                wh, whk = whs[f % 2], "wh%d" % (f % 2)
                if f + 1 < 8:
                    load_w(f + 1)
                if bg is not None:
                    bg()
                k = ui % 2; ui += 1
                for c in range(8):
                    P.op("pe", lambda e, c=c, k=k, wh=wh: e.matmul(pc[k][:, 0:128], lhsT=wh[:, c, 1, :], rhs=xnT[:, c, 0:128], start=(c == 0), stop=(c == 7)),
                         reads=[whk, "no_xnT"], writes=["ps:pc%d" % k])
                for c in range(8):
                    P.op("pe", lambda e, c=c, k=k, wh=wh: e.matmul(ph[k][:, 0:128], lhsT=wh[:, c, 2, :], rhs=xnT[:, c, 0:128], start=(c == 0), stop=(c == 7)),
                         reads=[whk, "no_xnT"], writes=["ps:ph%d" % k])
                P.op("act", lambda e, k=k: e.activation(out=cgs[:, 0:128], in_=pc[k][:, 0:128], func=AF.Copy), reads=["ps:pc%d" % k], writes=["cgs"])
                P.op("dve", lambda e, k=k: e.tensor_tensor(out=yb[:, 0:128], in0=ph[k][:, 0:128], in1=cgs[:, 0:128], op=ALU.mult), reads=["ps:ph%d" % k, "cgs"], writes=["yb"])
                P.op("dve", lambda e: e.tensor_scalar(out=uT[:, 0:2], in0=yb[:, 126:128], scalar1=fl[:, 0:1], scalar2=None, op0=ALU.mult), reads=["yb", "fl"], writes=["uT"])
                for t0 in range(0, TOK, 512):
                    k = ui % 2; ui += 1
                    for (gi, pz, nm) in ((0, pb, "pb"), (1, pc, "pc"), (2, ph, "ph")):
                        for c in range(8):
                            P.op("pe", lambda e, c=c, k=k, gi=gi, pz=pz, t0=t0, wh=wh: e.matmul(pz[k][:], lhsT=wh[:, c, gi, :], rhs=xnT[:, c, 128 + t0:128 + t0 + 512],
                                                                                        start=(c == 0), stop=(c == 7)),
                                 reads=[whk, "nw_xnT"], writes=["ps:%s%d" % (nm, k)])
                    P.op("act", lambda e, k=k: e.activation(out=cgs[:], in_=pc[k][:], func=AF.Copy), reads=["ps:pc%d" % k], writes=["cgs"])
                    P.op("dve", lambda e, k=k, t0=t0: e.tensor_tensor(out=uT[:, 2 + t0:2 + t0 + 512], in0=ph[k][:], in1=cgs[:], op=ALU.mult),
                         reads=["ps:ph%d" % k, "cgs"], writes=["uT"])
                    P.op("dve", lambda e, f=f, t0=t0: e.tensor_scalar(out=yb[:], in0=uT[:, t0:t0 + 512], scalar1=cw[:, f:f + 1], scalar2=None, op0=ALU.mult),
                         reads=["uT", "cw"], writes=["yb"])
                    for kk in (1, 2):
                        P.op("dve", lambda e, f=f, t0=t0, kk=kk: e.scalar_tensor_tensor(out=yb[:], in0=uT[:, t0 + kk:t0 + kk + 512], scalar=cw[:, kk * 8 + f:kk * 8 + f + 1],
                                                                                        in1=yb[:], op0=ALU.mult, op1=ALU.add),
                             reads=["uT", "cw", "yb"], writes=["yb"])
                    P.op("dve", lambda e, f=f, k=k, t0=t0: e.tensor_tensor(out=zT[:, f, t0:t0 + 512], in0=yb[:], in1=pb[k][:], op=ALU.mult),
                         reads=["yb", "ps:pb%d" % k], writes=["zT"])
            P.join()
        with ExitStack() as s3:
            po = P.ps("po", [128, 1024], F32, s3)
            emit_outproj(P, s3, "op_", zT, "zT", w_out, h_own, h_out, po)
            P.join()


def build_odd_nc():
    nc = bass.Bass("TRN2", target_bir_lowering=False)
    dr = lambda n, s, k="ExternalInput", dt=F32: nc.dram_tensor(n, list(s), dt, kind=k).ap()
    D = {"h_own": dr("h_own", [TOK, 1024]), "h_prev": dr("h_prev", [128, 1024]), "g": dr("g", [1024]), "w_in": dr("w_in", [1024, 3072]),
         "w_out": dr("w_out", [1024, 1024]), "convw": dr("convw", [24, 128]), "flag": dr("flag", [128, 1])}
    cd = {"ident": dr("ident", [128, 128]), "iota16": dr("iota16", [128, 16])}
    D["h_out"] = dr("h_out", [TOK, 1024], "ExternalOutput")
    with ExitStack() as st:
        P = Prog(nc, st)
        C = load_consts(P, nc, cd)
        emit_odd(P, C, D)
        P.wait_all("sp")
        P.emit()
    return nc


PAIRS = [[0, 1], [2, 3], [4, 5], [6, 7]]


def build_fused_nc(nlayers=4, do_cc=True, dbg=False):
    nc = bass.Bass("TRN2", target_bir_lowering=False)
    dr = lambda n, s, k="ExternalInput", dt=F32: nc.dram_tensor(n, list(s), dt, kind=k).ap()
    x_own = dr("x_own", [TOK, 1024]); x_first = dr("x_first", [TOK, 1024])
    gmix = [dr("gmix%d" % l, [1024]) for l in range(4)]
    gffn = [dr("gffn%d" % l, [1024]) for l in range(4)]
    ewin = [dr("ewin%d" % i, [1024, 3584]) for i in range(2)]
    ewout = [dr("ewout%d" % i, [1024, 1024]) for i in range(2)]
    owin = [dr("owin%d" % i, [1024, 3072]) for i in range(2)]
    oconv = [dr("oconv%d" % i, [24, 128]) for i in range(2)]
    owout = [dr("owout%d" % i, [1024, 1024]) for i in range(2)]
    wq = [dr("wq%d" % l, [1024, 2048]) for l in range(nlayers)]
    sk = [dr("sk%d" % l, [16, 128, 128]) for l in range(nlayers)]
    pu = [dr("pu%d" % l, [16384, 1024]) for l in range(nlayers)]
    pv = [dr("pv%d" % l, [16384, 1024]) for l in range(nlayers)]
    fnorm = dr("fnorm", [1024])
    ec = {"cos2": dr("cos2", [128, 4096]), "sin2": dr("sin2", [128, 4096]), "gmask": dr("gmask", [768]),
          "cm": dr("cm", [128, 2048]), "dpat": dr("dpat", [4, 128, 2560]), "cp": dr("cp", [4, 128])}
    flag = dr("flag", [128, 1])
    cd = {"ident": dr("ident", [128, 128]), "iota16": dr("iota16", [128, 16])}
    y_out = dr("y_out", [TOK, 1024], "ExternalOutput")
    hA = nc.dram_tensor("hA_i", [TOK, 1024], F32)
    hB = nc.dram_tensor("hB_i", [TOK, 1024], F32)
    hG = nc.dram_tensor("hG_i", [8, 512, 1024], F32)
    uvt = [nc.dram_tensor("uv%d_i" % k, [16384, 2048], BF16).ap() for k in range(2)]
    with ExitStack() as st:
        P = Prog(nc, st)
        C = load_consts(P, nc, cd)
        P.lazy = {P.n_shared + 50, P.n_shared + 51}
        conv = lambda l: uv_convert_steps(P, pu[l], pv[l], uvt[l % 2], "uvtab%d" % (l % 2), 50 + (l % 2))
        for layer in range(nlayers):
            i = layer // 2
            todo = []
            if layer % 2 == 0:
                todo = conv(layer) + (conv(layer + 1) if layer + 1 < nlayers else [])
            per = -(-len(todo) // 8)

            def bg(todo=todo, per=per):
                for _ in range(per):
                    if todo:
                        todo.pop(0)()
            h_src = x_own if layer == 0 else hB.ap()
            if layer % 2 == 0:
                D = dict(ec)
                D.update(h_own=h_src, h_oth=(x_first if layer == 0 else (lambda t: hG.ap()[t // 2, (t % 2) * 128:(t % 2 + 1) * 128, :])), g=gmix[layer], w_in=ewin[i], w_out=ewout[i], h_out=hA.ap())
                emit_even(P, C, D, bg)
            else:
                D = dict(h_own=h_src, h_prev=hG.ap()[7, 128:256, :], g=gmix[layer], w_in=owin[i], w_out=owout[i], convw=oconv[i], flag=flag, h_out=hA.ap())
                emit_odd(P, C, D, bg)
            while todo:
                todo.pop(0)()
            Dp = dict(h_in=hA.ap(), h_out=hB.ap(), g=gffn[layer], wq=wq[layer], sk=sk[layer], uv=uvt[layer % 2], uvkey="uvtab%d" % (layer % 2))
            emit_peer_block(P, C, Dp, fin=((fnorm, y_out) if layer == nlayers - 1 else None))
            if layer < 3 and do_cc and (layer < nlayers - 1 or dbg):
                for k in (range(8) if layer == 1 else [7]):
                    P.cc(lambda e, k=k: e.collective_compute("AllGather", ALU.bypass, replica_groups=PAIRS,
                                                             ins=[hB.ap()[k * 256:(k + 1) * 256, :].opt()], outs=[hG.ap()[k].opt()]), semi=48)
                P.join()
        P.wait_all("sp")
        P.emit()
    return nc

_NC_CACHE = {}


def _get_nc(kind):
    if kind not in _NC_CACHE:
        _NC_CACHE[kind] = {"even": build_even_nc, "odd": build_odd_nc, "peer": build_peer_nc,
                           "peer_final": lambda: build_peer_nc(final=True), "fused": build_fused_nc}[kind]()
    return _NC_CACHE[kind]


def kernel(x, norm_mix, norm_ffn, even_w_in, even_w_out, odd_w_in, odd_conv, odd_w_out,
           peer_w_q, peer_sub_keys, peer_u, peer_v, final_norm):
    from concourse.bass_utils import run_bass_kernel_spmd
    f32 = lambda a: np.ascontiguousarray(np.asarray(a, dtype=np.float32))
    x = f32(x)
    cst = host_consts()
    econ = [even_consts(0), even_consts(1)]
    shared = {"fnorm": f32(final_norm)}
    for l in range(4):
        shared["gmix%d" % l] = f32(norm_mix[l]); shared["gffn%d" % l] = f32(norm_ffn[l])
        shared["wq%d" % l] = f32(peer_w_q[l]); shared["sk%d" % l] = f32(np.asarray(peer_sub_keys[l]).reshape(16, 128, 128))
        shared["pu%d" % l] = f32(peer_u[l]); shared["pv%d" % l] = f32(peer_v[l])
    for i in range(2):
        shared["ewin%d" % i] = f32(even_w_in[i]); shared["ewout%d" % i] = f32(even_w_out[i])
        shared["owin%d" % i] = f32(odd_w_in[i]); shared["owout%d" % i] = f32(odd_w_out[i])
        shared["oconv%d" % i] = f32(np.asarray(odd_conv[i]).reshape(24, 128))
    maps = []
    for c in range(8):
        b, half = c // 2, c % 2
        m = dict(shared)
        m.update(cst); m.update(econ[half])
        m["x_own"] = f32(x[b, half * 2048:(half + 1) * 2048]); m["x_first"] = f32(x[b, 0:2048])
        m["flag"] = np.full((128, 1), float(half), np.float32)
        maps.append(m)
    res = run_bass_kernel_spmd(_get_nc("fused"), maps, core_ids=list(range(8)))
    y = np.zeros((4, 4096, 1024), np.float32)
    for c in range(8):
        y[c // 2, (c % 2) * 2048:(c % 2 + 1) * 2048] = res.results[c]["y_out"]
    return y


def kernel_unfused(x, norm_mix, norm_ffn, even_w_in, even_w_out, odd_w_in, odd_conv, odd_w_out,
           peer_w_q, peer_sub_keys, peer_u, peer_v, final_norm):
    from concourse.bass_utils import run_bass_kernel_spmd
    f32 = lambda a: np.ascontiguousarray(np.asarray(a, dtype=np.float32))
    h = f32(x).copy()
    cst = host_consts()
    econ = [even_consts(0), even_consts(1)]
    cores = list(range(8))
    y = None
    for layer in range(4):
        i = layer // 2
        maps = []
        for c in cores:
            b, half = c // 2, c % 2
            own = f32(h[b, half * 2048:(half + 1) * 2048])
            if layer % 2 == 0:
                maps.append(dict(h_own=own, h_oth=f32(h[b, (1 - half) * 2048:(2 - half) * 2048]), g=f32(norm_mix[layer]),
                                 w_in=f32(even_w_in[i]), w_out=f32(even_w_out[i]), **cst, **econ[half]))
            else:
                prev = h[b, 1920:2048] if half == 1 else h[b, 0:128]
                maps.append(dict(h_own=own, h_prev=f32(prev), flag=np.full((128, 1), float(half), np.float32), g=f32(norm_mix[layer]),
                                 w_in=f32(odd_w_in[i]), w_out=f32(odd_w_out[i]), convw=f32(np.asarray(odd_conv[i]).reshape(24, 128)), **cst))
        res = run_bass_kernel_spmd(_get_nc("even" if layer % 2 == 0 else "odd"), maps, core_ids=cores)
        for c in cores:
            h[c // 2, (c % 2) * 2048:(c % 2 + 1) * 2048] = res.results[c]["h_out"]
        last = layer == 3
        maps = []
        for c in cores:
            b, half = c // 2, c % 2
            m = dict(h_in=f32(h[b, half * 2048:(half + 1) * 2048]), g=f32(norm_ffn[layer]), wq=f32(peer_w_q[layer]),
                     sk=f32(np.asarray(peer_sub_keys[layer]).reshape(16, 128, 128)), u=f32(peer_u[layer]), v=f32(peer_v[layer]), **cst)
            if last:
                m["fnorm"] = f32(final_norm)
            maps.append(m)
        res = run_bass_kernel_spmd(_get_nc("peer_final" if last else "peer"), maps, core_ids=cores)
        for c in cores:
            h[c // 2, (c % 2) * 2048:(c % 2 + 1) * 2048] = res.results[c]["h_out"]
        if last:
            y = np.zeros_like(h)
            for c in cores:
                y[c // 2, (c % 2) * 2048:(c % 2 + 1) * 2048] = res.results[c]["y_out"]
    return y.astype(np.float32)
```
